# Optimizing a Trainium2 kernel written in Bass

```python
import jax, jax.numpy as jnp
from jax import lax
import numpy as np

D_MODEL = 1024
BATCH = 16
SEQ = 256
DEPTH = 4
DEC_BATCH = 4
DEC_SEQ = 1024
PAST_LEN = 256

GRID_W = 64
N_MIXERS = 2
N_GLA = (DEPTH + 1) // 2
N_ATT = DEPTH // 2
GLA_HEADS = 4
GLA_DK = D_MODEL // 2 // GLA_HEADS
GLA_DV = D_MODEL // GLA_HEADS
GLA_RANK = 16
GLA_TAU = 16.0
GLA_CHUNK = 64
GLA_QD = GLA_HEADS * GLA_DK
GLA_VD = GLA_HEADS * GLA_DV
GLA_IN = 2 * GLA_QD + 2 * GLA_VD
HEAD_DIM = 128
ATT_HEADS = D_MODEL // HEAD_DIM
ATT_KV_HEADS = ATT_HEADS // 4
ATT_QD = ATT_HEADS * HEAD_DIM
ATT_KD = ATT_KV_HEADS * HEAD_DIM
ATT_IN = 2 * ATT_QD + 2 * ATT_KD
Q_BLOCK = 128
ROPE_THETA = 10000.0
EPS = 1e-6

kernel_name = "hybrid_gla_gqa_diffusion_step"


def rms_norm(x, g):
    xf = x.astype(jnp.float32)
    y = xf * lax.rsqrt(jnp.mean(xf * xf, axis=-1, keepdims=True) + EPS)
    return (y * g.astype(jnp.float32)).astype(x.dtype)


def rope_2d(x):
    T = x.shape[1]
    rows = T // GRID_W
    row = jnp.repeat(jnp.arange(rows), GRID_W)
    col = jnp.tile(jnp.arange(GRID_W), rows)
    half = HEAD_DIM // 2
    nf = half // 2
    freqs = ROPE_THETA ** (-jnp.arange(nf, dtype=jnp.float32) / nf)

    def rot(seg, pos):
        ang = pos.astype(jnp.float32)[:, None] * freqs[None, :]
        cos = jnp.cos(ang)[None, :, None, :]
        sin = jnp.sin(ang)[None, :, None, :]
        a, b = seg[..., :nf], seg[..., nf:]
        return jnp.concatenate([a * cos - b * sin, b * cos + a * sin], axis=-1)

    xf = x.astype(jnp.float32)
    out = jnp.concatenate([rot(xf[..., :half], row), rot(xf[..., half:], col)], axis=-1)
    return out.astype(x.dtype)


def to_chunks(a):
    B, T, H, d = a.shape
    return a.reshape(B, T // GLA_CHUNK, GLA_CHUNK, H, d).transpose(1, 0, 3, 2, 4)


def gla_chunked(q, k, v, logg, s0):
    B, T, H, _ = q.shape
    qc, kc, vc, gc = (to_chunks(a.astype(jnp.float32)) for a in (q, k, v, logg))
    mask = jnp.tril(jnp.ones((GLA_CHUNK, GLA_CHUNK), dtype=bool))

    def step(S, inp):
        qi, ki, vi, gi = inp
        b = jnp.cumsum(gi, axis=2)
        b_last = b[:, :, -1:, :]
        qe = qi * jnp.exp(b)
        ke = ki * jnp.exp(-b)
        a = jnp.where(mask, jnp.einsum("bhtk,bhsk->bhts", qe, ke), 0.0)
        o = jnp.einsum("bhts,bhsv->bhtv", a, vi) + jnp.einsum("bhtk,bhkv->bhtv", qe, S)
        S = jnp.exp(b_last[:, :, 0, :])[..., None] * S + jnp.einsum(
            "bhsk,bhsv->bhkv", ki * jnp.exp(b_last - b), vi)
        return S, o

    S, o = lax.scan(step, s0.astype(jnp.float32), (qc, kc, vc, gc))
    o = o.transpose(1, 0, 3, 2, 4).reshape(B, T, H, -1)
    return o, S


def gla_mixer(h, w_in, wa1, wa2, ba, onorm, w_out, s0):
    B, T, _ = h.shape
    proj = h @ w_in
    q, k, v, gate = jnp.split(proj, [GLA_QD, 2 * GLA_QD, 2 * GLA_QD + GLA_VD], axis=-1)
    q = q.reshape(B, T, GLA_HEADS, GLA_DK) * (GLA_DK ** -0.5)
    k = k.reshape(B, T, GLA_HEADS, GLA_DK)
    v = v.reshape(B, T, GLA_HEADS, GLA_DV)

    def decay(d):
        z = ((h @ wa1[d]) @ wa2[d] + ba[d]).astype(jnp.float32)
        return (jax.nn.log_sigmoid(z) / GLA_TAU).reshape(B, T, GLA_HEADS, GLA_DK)

    flip = lambda a: a[:, ::-1]
    o_f, s_f = gla_chunked(q, k, v, decay(0), s0[:, 0])
    o_b, s_b = gla_chunked(flip(q), flip(k), flip(v), flip(decay(1)), s0[:, 1])
    o = rms_norm(o_f + flip(o_b), onorm).reshape(B, T, GLA_VD).astype(h.dtype)
    y = (o * jax.nn.silu(gate)) @ w_out
    return y, jnp.stack([s_f, s_b], axis=1).astype(h.dtype)


def attn_project(h, w_in, qn, kn):
    B, T, _ = h.shape
    proj = h @ w_in
    q, k, v, gate = jnp.split(proj, [ATT_QD, ATT_QD + ATT_KD, ATT_QD + 2 * ATT_KD], axis=-1)
    q = rms_norm(q.reshape(B, T, ATT_HEADS, HEAD_DIM), qn)
    k = rms_norm(k.reshape(B, T, ATT_KV_HEADS, HEAD_DIM), kn)
    v = v.reshape(B, T, ATT_KV_HEADS, HEAD_DIM)
    return q, k, v, gate


def attend_blocks(q, keys, vals):
    B, T = q.shape[0], q.shape[1]
    G = ATT_HEADS // ATT_KV_HEADS
    nb = T // Q_BLOCK
    qb = q.reshape(B, nb, Q_BLOCK, ATT_KV_HEADS, G, HEAD_DIM).transpose(1, 0, 2, 3, 4, 5)
    scale = HEAD_DIM ** -0.5

    def one(qblk):
        s = jnp.einsum("bqhgd,bkhd->bhgqk", qblk, keys).astype(jnp.float32) * scale
        p = jax.nn.softmax(s, axis=-1).astype(vals.dtype)
        return jnp.einsum("bhgqk,bkhd->bqhgd", p, vals)

    o = lax.map(one, qb)
    return o.transpose(1, 0, 2, 3, 4, 5).reshape(B, T, ATT_QD)


def split_mod(mod):
    return jnp.split(mod, 3, axis=-1)


def setup_inputs(seed: int = 0) -> dict:
    key = jax.random.key(seed)
    ks = jax.random.split(key, 24)
    f32 = jnp.float32
    nrm = lambda k, shape, s: jax.random.normal(k, shape, f32) * s
    return {
        "x_prompt": nrm(ks[0], (BATCH, SEQ, D_MODEL), 1.0),
        "x_sample": nrm(ks[1], (DEC_BATCH, DEC_SEQ, D_MODEL), 1.0),
        "state_gla": nrm(ks[2], (DEC_BATCH, N_GLA, 2, GLA_HEADS, GLA_DK, GLA_DV), 1.0),
        "cache_k": nrm(ks[3], (DEC_BATCH, N_ATT, PAST_LEN, ATT_KV_HEADS, HEAD_DIM), 1.0),
        "cache_v": nrm(ks[4], (DEC_BATCH, N_ATT, PAST_LEN, ATT_KV_HEADS, HEAD_DIM), 1.0),
        "c": nrm(ks[5], (DEC_BATCH, D_MODEL), 1.0),
        "c_ctx": nrm(ks[6], (D_MODEL,), 1.0),
        "norm_g": 1.0 + nrm(ks[7], (DEPTH, D_MODEL), 0.1),
        "w_ada": nrm(ks[8], (DEPTH, D_MODEL, 3 * D_MODEL), 0.5 * D_MODEL ** -0.5),
        "b_ada": nrm(ks[9], (DEPTH, 3 * D_MODEL), 0.02),
        "gla_w_in": nrm(ks[10], (N_GLA, D_MODEL, GLA_IN), D_MODEL ** -0.5),
        "gla_wa1": nrm(ks[11], (N_GLA, 2, D_MODEL, GLA_RANK), D_MODEL ** -0.5),
        "gla_wa2": nrm(ks[12], (N_GLA, 2, GLA_RANK, GLA_QD), GLA_RANK ** -0.5),
        "gla_ba": nrm(ks[13], (N_GLA, 2, GLA_QD), 0.1),
        "gla_onorm": 1.0 + nrm(ks[14], (N_GLA, GLA_DV), 0.1),
        "gla_w_out": nrm(ks[15], (N_GLA, GLA_VD, D_MODEL), GLA_VD ** -0.5),
        "att_w_in": nrm(ks[16], (N_ATT, D_MODEL, ATT_IN), D_MODEL ** -0.5),
        "att_qnorm": 1.0 + nrm(ks[17], (N_ATT, HEAD_DIM), 0.1),
        "att_knorm": 1.0 + nrm(ks[18], (N_ATT, HEAD_DIM), 0.1),
        "att_w_out": nrm(ks[19], (N_ATT, ATT_QD, D_MODEL), ATT_QD ** -0.5),
    }


def reference(x_prompt, x_sample, state_gla, cache_k, cache_v, c, c_ctx, norm_g, w_ada, b_ada,
              gla_w_in, gla_wa1, gla_wa2, gla_ba, gla_onorm, gla_w_out,
              att_w_in, att_qnorm, att_knorm, att_w_out):
    xp, xs = x_prompt, x_sample
    Bp = xp.shape[0]
    gla_states, ctx_keys, ctx_vals = [], [], []
    for l in range(DEPTH):
        i = l // N_MIXERS
        mod_p = (jax.nn.silu(c_ctx) @ w_ada[l] + b_ada[l])[None, None, :]
        mod_s = (jax.nn.silu(c) @ w_ada[l] + b_ada[l])[:, None, :]
        sh_p, sc_p, gt_p = split_mod(mod_p)
        sh_s, sc_s, gt_s = split_mod(mod_s)
        hp = rms_norm(xp, norm_g[l]) * (1.0 + sc_p) + sh_p
        hs = rms_norm(xs, norm_g[l]) * (1.0 + sc_s) + sh_s
        if l % N_MIXERS == 0:
            s_zero = jnp.zeros((Bp, 2, GLA_HEADS, GLA_DK, GLA_DV), xp.dtype)
            out_p, st = gla_mixer(hp, gla_w_in[i], gla_wa1[i], gla_wa2[i], gla_ba[i],
                                  gla_onorm[i], gla_w_out[i], s_zero)
            out_s, _ = gla_mixer(hs, gla_w_in[i], gla_wa1[i], gla_wa2[i], gla_ba[i],
                                 gla_onorm[i], gla_w_out[i], state_gla[:, i])
            gla_states.append(st)
        else:
            qp, kp, vp, gp = attn_project(hp, att_w_in[i], att_qnorm[i], att_knorm[i])
            out_p = (attend_blocks(qp, kp, vp) * jax.nn.silu(gp)) @ att_w_out[i]
            ctx_keys.append(kp)
            ctx_vals.append(vp)
            qs, ks_, vs, gs = attn_project(hs, att_w_in[i], att_qnorm[i], att_knorm[i])
            qs = rope_2d(qs)
            ks_ = rope_2d(ks_)
            keys = jnp.concatenate([ks_, cache_k[:, i].astype(ks_.dtype)], axis=1)
            vals = jnp.concatenate([vs, cache_v[:, i].astype(vs.dtype)], axis=1)
            out_s = (attend_blocks(qs, keys, vals) * jax.nn.silu(gs)) @ att_w_out[i]
        xp = xp + gt_p * out_p
        xs = xs + gt_s * out_s
    state_gla_new = jnp.stack(gla_states, axis=1)
    cache_k_new = jnp.stack(ctx_keys, axis=1)
    cache_v_new = jnp.stack(ctx_vals, axis=1)
    return (xp, xs, state_gla_new, cache_k_new, cache_v_new)
```

```python
import numpy as np
from contextlib import ExitStack
import concourse.bass as bass
import concourse.mybir as mybir
from concourse.bass_utils import run_bass_kernel_spmd

F32 = mybir.dt.float32
BF16 = mybir.dt.bfloat16
AF = mybir.ActivationFunctionType
ALU = mybir.AluOpType

D = 1024
NTOK = 1024
EPS = 1e-6
NSLOT = 3
PV_BADA = 0
PV_NG = 96
PV_ON = 128
PV_QN = 132
PV_KN = 134
PV_AM = 136
PV_RS = 176
PV_AM2 = 192
NPV = 272
CM_TRIF, CM_TRIB, CM_TRISF, CM_TRISB, CM_MF, CM_MB, CM_PROT, CM_ONES, CM_ID = range(9)
NCM = 9

SAME_ENGINE_SYNC = True
import os as _os
ATT_STAGE = int(_os.environ.get('ATT_STAGE', '9'))
REORDER = int(_os.environ.get('REORDER', '1')) != 0
PRIO_CP = int(_os.environ.get('PRIO_CP', '1')) != 0
SYNC_SAME_WAR = int(_os.environ.get('SYNC_SAME_WAR', '1')) != 0


class Ins:
    __slots__ = ("eng", "fn", "reads", "writes", "dma", "deps", "alldeps", "needs_inc", "inc_val", "idx", "phase",
                 "epoch", "cost", "nbytes", "rdep", "t0", "t1")

    def __init__(self, eng, fn, reads, writes, dma):
        self.eng = eng
        self.fn = fn
        self.reads = reads
        self.writes = writes
        self.dma = dma
        self.deps = []
        self.alldeps = []
        self.needs_inc = False
        self.inc_val = None
        self.rdep = None


class Prog:
    ENGS = ("pe", "act", "dve", "pool", "sp")

    def __init__(self):
        self.ins = []
        self.last_write = {}
        self.readers = {}
        self.epoch = 0
        self.dma_names = []
        self.phase = ""

    def barrier(self):
        self.epoch += 1

    def add(self, eng, fn, reads=(), writes=(), dma=None, cost=0.1, nbytes=0):
        I = Ins(eng, fn, tuple(reads), tuple(writes), dma)
        I.idx = len(self.ins)
        I.phase = self.phase
        I.epoch = self.epoch
        I.cost = cost
        I.nbytes = nbytes
        deps = {}
        for r in I.reads:
            w = self.last_write.get(r)
            if w is not None:
                deps[w.idx] = (w, "raw")
        for w_ in I.writes:
            lw = self.last_write.get(w_)
            if lw is not None and lw.idx not in deps:
                deps[lw.idx] = (lw, "waw")
            for rd in self.readers.get(w_, ()):
                if rd.idx not in deps:
                    deps[rd.idx] = (rd, "war")
        for r in I.reads:
            self.readers.setdefault(r, []).append(I)
        for w_ in I.writes:
            self.last_write[w_] = I
            self.readers[w_] = []
        I.alldeps = list(deps.values())
        if dma is not None and dma not in self.dma_names:
            self.dma_names.append(dma)
        self.ins.append(I)
        return I

    def schedule(self, reorder=True):
        LAT_X, LAT_S = 0.2, 0.15
        WIN = int(_os.environ.get("WIN", "400"))
        t_eng = {e: 0.0 for e in self.ENGS}
        fin = {}
        dma_free = [0.0]
        new_order = []
        nseg = self.epoch + 1
        segs = [[] for _ in range(nseg)]
        for I in self.ins:
            segs[I.epoch].append(I)
        bar = []

        def lat(J, I, kind):
            if J.dma is not None:
                return 0.0
            if J.eng != I.eng:
                return 0.5 if J.eng == "pe" else LAT_X
            if I.eng == "pe" or (kind != "raw" and not SYNC_SAME_WAR):
                return 0.0
            return LAT_S

        for seg in segs:
            bar_time = {}
            for e in self.ENGS:
                bt = 0.0
                for J in bar:
                    if J.dma is not None or J.eng != e:
                        bt = max(bt, fin[J.idx] + (0.0 if J.dma is not None else 0.5))
                bar_time[e] = bt
            for I in seg:
                I.deps = [J for J in bar if (J.dma is not None or J.eng != I.eng)]
            pend = {e: [I for I in seg if I.eng == e] for e in self.ENGS}
            bl = {}
            if PRIO_CP:
                succ = {}
                inseg = set(I.idx for I in seg)
                for I in seg:
                    for J, kind in I.alldeps:
                        if J.idx in inseg:
                            succ.setdefault(J.idx, []).append((I, kind))
                for I in reversed(seg):
                    c_ = I.cost if I.dma is None else 3.0
                    m_ = 0.0
                    for K_, kind in succ.get(I.idx, ()):
                        m_ = max(m_, bl[K_.idx] + lat(I, K_, kind))
                    bl[I.idx] = c_ + m_
            head = {e: 0 for e in self.ENGS}
            done = set()
            remaining = len(seg)
            last_sched = {}
            while remaining:
                best = None
                for e in self.ENGS:
                    lst = pend[e]
                    h = head[e]
                    while h < len(lst) and lst[h].idx in done:
                        h += 1
                    head[e] = h
                    W = WIN if (reorder and e in ("pe", "act", "dve")) else 1
                    cnt = 0
                    k = h
                    while k < len(lst) and cnt < W:
                        I = lst[k]
                        k += 1
                        if I.idx in done:
                            continue
                        cnt += 1
                        if I.rdep is None:
                            r = 0.0
                            ok = True
                            for J, kind in I.alldeps:
                                f = fin.get(J.idx)
                                if f is None:
                                    ok = False
                                    break
                                r = max(r, f + lat(J, I, kind))
                            if not ok:
                                continue
                            I.rdep = r
                        r = max(I.rdep, bar_time[e], t_eng[e])
                        key = (int(r / 0.1), -bl[I.idx], I.idx) if PRIO_CP else (int(r / 0.3), I.idx)
                        if best is None or key < best[0]:
                            best = (key, r, e, I)
                assert best is not None, "scheduler deadlock"
                _, r, e, I = best
                I.t0 = r
                if I.dma is not None:
                    occ = 1.2 if e == "pool" else 0.15
                    if e == "sp":
                        f = r + 10.0 + I.nbytes / 150e3
                    elif I.nbytes > 200000:
                        st = max(r, dma_free[0])
                        f = st + 2.0 + I.nbytes / 200e3
                        dma_free[0] = f - 2.0
                    else:
                        f = r + 10.0
                    t_eng[e] = r + occ
                else:
                    f = r + I.cost
                    t_eng[e] = f
                    last_sched[e] = I
                fin[I.idx] = f
                I.t1 = f
                done.add(I.idx)
                new_order.append(I)
                remaining -= 1
            bar = list(last_sched.values())
            ld = {}
            for I in seg:
                if I.dma is not None and not I.dma.startswith("w"):
                    ld[I.dma] = I
            bar += list(ld.values())
        self.ins = new_order
        self.sim_time = max(fin.values())
        pos = {}
        for n, I in enumerate(self.ins):
            pos[I.idx] = n
        for I in self.ins:
            cand = list(I.deps)
            for J, kind in I.alldeps:
                if J.dma is None and J.eng == I.eng and (I.eng == "pe" or (kind != "raw" and not SYNC_SAME_WAR)):
                    continue
                cand.append(J)
            keep = {}
            for J in cand:
                key = ("d", J.dma) if J.dma is not None else ("e", J.eng)
                if key not in keep or pos[J.idx] > pos[keep[key].idx]:
                    keep[key] = J
            I.deps = list(keep.values())
            for J in I.deps:
                J.needs_inc = True

    def finalize(self):
        cnt = {e: 0 for e in self.ENGS}
        dcnt = {s: 0 for s in self.dma_names}
        for I in self.ins:
            if I.dma is not None:
                dcnt[I.dma] += 16
                I.inc_val = dcnt[I.dma]
            elif I.needs_inc:
                cnt[I.eng] += 1
                I.inc_val = cnt[I.eng]
        self.cnt = cnt
        self.dcnt = dcnt

    def emit(self, eng_name, eng_obj, sems, dsems, final_wait=False):
        known = {}
        for I in self.ins:
            if I.eng != eng_name:
                continue
            for J in I.deps:
                if J.dma is not None:
                    s = dsems[J.dma]
                    key = ("d", J.dma)
                else:
                    s = sems[J.eng]
                    key = ("e", J.eng)
                if known.get(key, 0) < J.inc_val:
                    eng_obj.wait_ge(s, J.inc_val)
                    known[key] = J.inc_val
            r = I.fn(eng_obj)
            if I.dma is not None:
                r.then_inc(dsems[I.dma], 16)
            elif I.needs_inc:
                r.then_inc(sems[I.eng], 1)
        if final_wait:
            for name, s in dsems.items():
                if self.dcnt[name] > 0:
                    eng_obj.wait_ge(s, self.dcnt[name])


def build(L=4, dbg_names=()):
    nc = bass.Bass("TRN2", target_bir_lowering=False)
    P = Prog()

    def din(name, shape):
        return nc.dram_tensor(name, list(shape), F32, kind="ExternalInput").ap()

    def dout(name, shape, dt=F32):
        return nc.dram_tensor(name, list(shape), dt, kind="ExternalOutput").ap()

    xT = din("xT", [1024, 1024])
    cv8 = din("cv8", [128, 8])
    pvec = din("pvec", [128, NPV])
    wa1 = din("wa1", [128, 512])
    wa2 = din("wa2", [2, 16, 1024])
    bad = din("ba", [2, 1, 1024])
    cmat = din("cmat", [128, NCM * 128])
    ones1k = din("ones1k", [1, 1024])
    ropeC = din("ropeC", [128, 1024])
    ropeS = din("ropeS", [128, 1024])
    st_in = din("st_in", [2, 2, 4, 128, 256])
    ckT_in = din("ckT_in", [2, 2, 128, 256])
    cv_in = din("cv_in", [2, 256, 2, 128])
    w_ada = din("w_ada", [4, 1024, 3072])
    gla_w_in = din("gla_w_in", [2, 1024, 3072])
    gla_w_out = din("gla_w_out", [2, 1024, 1024])
    att_w_in = din("att_w_in", [2, 1024, 2560])
    att_w_out = din("att_w_out", [2, 1024, 1024])
    yT = dout("yT", [1024, 1024])
    st_out = dout("st_out", [4, 2, 2, 4, 128, 256])
    ckT_out = dout("ckT_out", [2, 2, 128, 1024])
    cv_out = dout("cv_out", [4, 2, 256, 2, 128])
    dbg_out = {}

    es = ExitStack()

    def sb(name, shape, dt):
        return es.enter_context(nc.sbuf_tensor(name, list(shape), dt))

    X = sb("X", [128, 8 * 1024], F32)
    hT = sb("hT", [128, 8 * 1024], BF16)
    sg = sb("sg", [128, 8 * 1024], BF16)
    Wt = [sb(f"W{s}", [128, 8, 512], BF16) for s in range(NSLOT)]
    pv = sb("pv", [128, NPV], F32)
    cv8t = sb("cv8t", [128, 8], F32)
    scb = sb("scb", [128, 8], BF16)
    modt = sb("modt", [128, 4 * 24], F32)
    Gp = sb("Gp", [128, 4 * 8], F32)
    cm = sb("cm", [128, NCM * 128], BF16)
    wa1b = sb("wa1b", [128, 512], BF16)
    NTMP = int(_os.environ.get("NTMP", "8"))
    tmpall = sb("tmpall", [128, NTMP * 512], F32)
    TMP = [tmpall[:, i * 512:(i + 1) * 512] for i in range(NTMP)]
    NSQ = int(_os.environ.get("NSQ", "4"))
    SQ = [sb(f"sq{i}", [128, 512], BF16) for i in range(NSQ)]
    ARENA_BYTES = 94 * 1024
    arena = sb("arena", [128, ARENA_BYTES // 4], F32)
    psall = es.enter_context(nc.psum_tensor("psall", [128, 8 * 512], F32))
    ps = [psall[:, i * 512:(i + 1) * 512] for i in range(8)]

    class Carver:
        def __init__(self, base, nbytes_total):
            self.off = 0
            self.base = base
            self.total = nbytes_total

        def get(self, nelem, dt, rows=128):
            nbytes = nelem * (4 if dt == F32 else 2)
            nbytes = (nbytes + 31) // 32 * 32
            assert self.off + nbytes <= self.total, (self.off, nbytes)
            if self.base is arena:
                v = arena[0:rows, self.off // 4:(self.off + nbytes) // 4]
                if dt != F32:
                    v = v.bitcast(dt)
            else:
                assert dt == BF16
                v = self.base[0:rows, self.off // 2:(self.off + nbytes) // 2]
            self.off += nbytes
            return v

    ctr = {"bank": 0, "tmp": 0, "sq": 0, "evac": 0}

    POOLS = {"d": [0, 1, 2, 3, 4, 5, 6], "ada": [7],
             "gA": [0, 1], "gB": [2, 3], "gCo": [4, 5], "gCx": [6],
             "aQK": [0, 1, 2, 3, 4], "aO": [5, 6],
             "nP": [0, 1, 2, 3], "nS": [4, 5, 6]}
    import json as _json
    if _os.environ.get("GPOOLS"):
        POOLS.update(_json.loads(_os.environ["GPOOLS"]))
    pctr = {k: 0 for k in POOLS}

    def nb(pool="d"):
        lst = POOLS[pool]
        b = lst[pctr[pool] % len(lst)]
        pctr[pool] += 1
        return b

    TPOOLS = {"d": list(range(NTMP)), "gA": [0, 1, 2], "gC": [3, 4, 5]}
    tctr = {k: 0 for k in TPOOLS}

    def nt(pool="d"):
        lst = TPOOLS[pool]
        t = lst[tctr[pool] % len(lst)]
        tctr[pool] += 1
        return t

    def nsq():
        t = ctr["sq"] % NSQ
        ctr["sq"] += 1
        return t

    def is_ps(ap):
        return str(ap.space).endswith("PSUM")

    def c_act(out, in_):
        return 0.15 + out.free_size() * 0.0008

    def c_dve(out, k=1.0):
        return 0.07 + out.free_size() * 0.0011 * k

    def ACT(out, in_, func, reads, writes, bias=None, scale=None):
        kw = {}
        if bias is not None:
            kw["bias"] = bias
        if scale is not None:
            kw["scale"] = scale
        P.add("act", lambda e: e.activation(out=out, in_=in_, func=func, **kw), reads, writes, cost=c_act(out, in_))

    def TT(out, in0, in1, op, reads, writes):
        P.add("dve", lambda e: e.tensor_tensor(out=out, in0=in0, in1=in1, op=op), reads, writes, cost=c_dve(out))

    def TS(out, in0, s1, op0, reads, writes, s2=None, op1=None):
        if op1 is None:
            P.add("dve", lambda e: e.tensor_scalar(out=out, in0=in0, scalar1=s1, scalar2=None, op0=op0), reads, writes, cost=c_dve(out))
        else:
            P.add("dve", lambda e: e.tensor_scalar(out=out, in0=in0, scalar1=s1, scalar2=s2, op0=op0, op1=op1), reads, writes, cost=c_dve(out))

    def STT(out, in0, scalar, in1, op0, op1, reads, writes):
        P.add("dve", lambda e: e.scalar_tensor_tensor(out=out, in0=in0, scalar=scalar, in1=in1, op0=op0, op1=op1), reads, writes,
              cost=c_dve(out, 1.15))

    def CP(eng, out, in_, reads, writes):
        if eng == "act":
            P.add("act", lambda e: e.activation(out=out, in_=in_, func=AF.Copy), reads, writes, cost=c_act(out, in_))
        else:
            P.add("dve", lambda e: e.tensor_copy(out=out, in_=in_), reads, writes, cost=c_dve(out))

    def RECIP(out, in_, reads, writes):
        P.add("dve", lambda e: e.reciprocal(out=out, in_=in_), reads, writes, cost=0.1 + out.free_size() * 0.0065)

    def MM(out, lhsT, rhs, start, stop, reads, writes):
        P.add("pe", lambda e: e.matmul(out, lhsT, rhs, start=start, stop=stop), reads, writes,
              cost=0.03 + max(out.free_size(), 100) / 2700.0)

    def DMA(queue, out, in_, reads, writes, sem):
        P.add(queue, lambda e: e.dma_start(out=out, in_=in_), reads, writes, dma=sem, nbytes=in_.nbytes())

    def EVAC(out, in_, reads, writes, scale=None):
        ctr["evac"] += 1
        if ctr["evac"] % 2 == 0:
            ACT(out, in_, AF.Copy, reads, writes, scale=scale)
        else:
            if scale is None:
                CP("dve", out, in_, reads, writes)
            else:
                TS(out, in_, scale, ALU.mult, reads, writes)

    def DBG(name, ap, shape, reads, dt=F32):
        if name not in dbg_names:
            return
        o = dout("dbg_" + name, shape, dt)
        dbg_out[name] = o
        DMA("sp", o, ap, reads, [], "dbg_" + name)

    def Xs(k, th):
        return X[:, k * 1024 + th * 512:k * 1024 + (th + 1) * 512]

    def hTs(k, th):
        return hT[:, k * 1024 + th * 512:k * 1024 + (th + 1) * 512]

    def cms(slot, n=1):
        return cm[:, slot * 128:(slot + n) * 128]

    ones = cms(CM_ONES)

    def blk(ap, l, c0):
        return ap[l, :, c0:c0 + 512].rearrange("(k p) n -> p k n", p=128)

    def ada_blocks(l):
        return [blk(w_ada, l, j * 512) for j in range(6)]

    srcs = []
    for l in range(L):
        i = l // 2
        if l == 0:
            srcs += ada_blocks(0)[0:4]
        if l % 2 == 0:
            srcs += [blk(gla_w_in, i, c) for c in (2048, 2560, 0, 512, 1024, 1536)]
        else:
            srcs += [blk(att_w_in, i, c) for c in (1536, 2048, 1024, 0, 512)]
        if l == 0:
            srcs += ada_blocks(0)[4:6]
        if l + 1 < L:
            srcs += ada_blocks(l + 1)
        wo = gla_w_out if l % 2 == 0 else att_w_out
        srcs += [blk(wo, i, 0), blk(wo, i, 512)]

    class WStream:
        def __init__(self):
            self.next_issue = 0
            self.next_use = 0
            self.slot_of = {}

        def issue(self, slot):
            if self.next_issue >= len(srcs):
                return
            src = srcs[self.next_issue]
            DMA("pool", Wt[slot][:], src, [], [("W", slot)], f"w{slot}")
            self.slot_of[self.next_issue] = slot
            self.next_issue += 1

        def acquire(self):
            s = self.slot_of[self.next_use]
            self.next_use += 1
            return s

        def release(self, slot):
            self.issue(slot)

    W = WStream()

    DMA("sp", pv[:], pvec[:, :], [], ["pv"], "pv")
    DMA("sp", cv8t[:], cv8[:, :], [], ["cv8t"], "cv8")
    DMA("pool", cm[:], cmat[:, :], [], ["cm"], "cm")
    DMA("pool", wa1b[:], wa1[:, :], [], ["wa1b"], "wa1")
    for s in range(NSLOT):
        W.issue(s)
    for k in range(8):
        DMA("sp", X[:, k * 1024:(k + 1) * 1024], xT[k * 128:(k + 1) * 128, :], [], [("X", k, 0), ("X", k, 1)], f"x{k}")
    ACT(scb[:], cv8t[:], AF.Silu, ["cv8t"], ["scb"])

    def ada(l, pool="ada", blocks=range(6)):
        P.phase = 'ada'
        for b6 in blocks:
            s = W.acquire()
            mb = nb(pool)
            for m in range(4):
                for k in range(8):
                    MM(ps[mb][:, m:m + 1], Wt[s][:, k, m * 128:(m + 1) * 128], scb[:, k:k + 1], k == 0, k == 7,
                       [("W", s), "scb"], [("ps", mb)])
            W.release(s)
            TT(modt[:, l * 24 + b6 * 4:l * 24 + b6 * 4 + 4], ps[mb][:, 0:4],
               pv[:, PV_BADA + l * 24 + b6 * 4:PV_BADA + l * 24 + b6 * 4 + 4], ALU.add, [("ps", mb), "pv"], [("mod", l, b6)])
        if 3 in blocks:
            STT(Gp[:, l * 8:(l + 1) * 8], modt[:, l * 24 + 8:l * 24 + 16], 1.0, pv[:, PV_NG + l * 8:PV_NG + (l + 1) * 8],
                ALU.add, ALU.mult, [("mod", l, 2), ("mod", l, 3), "pv"], [("Gp", l)])

    def normmod(l):
        P.phase = 'normmod'
        for th in range(2):
            b = nb()
            for k in range(8):
                s_ = nsq()
                if k % 2 == 0:
                    ACT(SQ[s_][:], Xs(k, th), AF.Square, [("X", k, th)], [("sq", s_)])
                else:
                    TT(SQ[s_][:], Xs(k, th), Xs(k, th), ALU.mult, [("X", k, th)], [("sq", s_)])
                MM(ps[b][:, :], ones, SQ[s_][:], k == 0, k == 7, [("sq", s_), "cm"], [("ps", b)])
            ta, tc = nt(), nt()
            ACT(TMP[ta][:], ps[b][:, :], AF.Ln, [("ps", b)], [("tmp", ta)], bias=EPS, scale=1.0 / D)
            ACT(TMP[tc][:], TMP[ta][:], AF.Exp, [("tmp", ta)], [("tmp", tc)], scale=-0.5)
            for k in range(8):
                tb = nt()
                STT(TMP[tb][:], Xs(k, th), Gp[:, l * 8 + k:l * 8 + k + 1], TMP[tc][:], ALU.mult, ALU.mult,
                    [("X", k, th), ("Gp", l), ("tmp", tc)], [("tmp", tb)])
                ACT(hTs(k, th), TMP[tb][:], AF.Identity, [("tmp", tb), ("mod", l, 0), ("mod", l, 1)], [("hT", k, th)],
                    bias=modt[:, l * 24 + k:l * 24 + k + 1], scale=1.0)

    def proj_fm(slot, ms, evac, pool="d", kouter=False):
        if kouter:
            for th in range(2):
                bs_ = {m: nb(pool) for m in ms}
                for k in range(8):
                    for m in ms:
                        MM(ps[bs_[m]][:, :], Wt[slot][:, k, m * 128:(m + 1) * 128], hTs(k, th), k == 0, k == 7,
                           [("W", slot), ("hT", k, th)], [("ps", bs_[m])])
                for m in ms:
                    evac(m, th, bs_[m])
            return
        for th in range(2):
            for m in ms:
                b = nb(pool)
                for k in range(8):
                    MM(ps[b][:, :], Wt[slot][:, k, m * 128:(m + 1) * 128], hTs(k, th), k == 0, k == 7,
                       [("W", slot), ("hT", k, th)], [("ps", b)])
                evac(m, th, b)

    def proj_tm(slot, c0, ncols, evac):
        for t in range(8):
            b = nb()
            for k in range(8):
                MM(ps[b][:, 0:ncols], hT[:, k * 1024 + t * 128:k * 1024 + (t + 1) * 128], Wt[slot][:, k, c0:c0 + ncols],
                   k == 0, k == 7, [("W", slot), ("hT", k, t // 4)], [("ps", b)])
            evac(t, b)

    def out_proj(l):
        P.phase = 'outproj'
        for ob in range(2):
            s = W.acquire()
            for th in range(2):
                for m in range(4):
                    fc = ob * 4 + m
                    b = nb()
                    for k in range(8):
                        MM(ps[b][:, :], Wt[s][:, k, m * 128:(m + 1) * 128],
                           sg[:, k * 1024 + th * 512:k * 1024 + (th + 1) * 512], k == 0, k == 7,
                           [("W", s), ("sg", k, th)], [("ps", b)])
                    STT(Xs(fc, th), ps[b][:, :], modt[:, l * 24 + 16 + fc:l * 24 + 17 + fc], Xs(fc, th), ALU.mult, ALU.add,
                        [("ps", b), ("mod", l, 4), ("mod", l, 5), ("X", fc, th)], [("X", fc, th)])
            W.release(s)

    def gate_phase():
        P.phase = 'gate'
        for gb in range(2):
            s = W.acquire()

            def ev(m, th, b, gb=gb):
                fc = gb * 4 + m
                ACT(sg[:, fc * 1024 + th * 512:fc * 1024 + (th + 1) * 512], ps[b][:, :], AF.Silu, [("ps", b)], [("sg", fc, th)])
            proj_fm(s, range(4), ev, kouter=(gb == 0))
            W.release(s)

    def gla_layer(l, mid):
        i = l // 2
        cv = Carver(arena, ARENA_BYTES)
        qT = cv.get(4 * 1024, BF16)
        kT = cv.get(4 * 1024, BF16)
        ktm = cv.get(8 * 512, BF16)
        vtm = cv.get(8 * 1024, BF16)
        rT = [cv.get(1024, BF16, rows=17), cv.get(1024, BF16, rows=17)]
        wa2b = cv.get(1024, BF16, rows=17)
        Srun = cv.get(2 * 2 * 256, F32)
        Sfin = cv.get(2 * 4 * 256, F32)
        AT = [cv.get(256, BF16), cv.get(256, BF16)]
        hcv = Carver(hT, 16 * 1024)
        sets = []
        for si in range(2):
            c_ = cv if si == 0 else hcv
            sets.append(dict(g2=c_.get(16 * 128, BF16), qe=c_.get(2 * 1024, BF16), ke=c_.get(2 * 1024, BF16),
                             kp=c_.get(2 * 8 * 128, BF16)))
        Sall2 = [cv.get(2 * 8 * 256, BF16), cv.get(2 * 8 * 256, BF16)]
        dcol2 = [cv.get(16, F32), cv.get(16, F32)]
        dcolr2 = [cv.get(16, F32), cv.get(16, F32)]
        hT_all = [("hT", k, th) for k in range(8) for th in range(2)]

        DMA("pool", wa2b[0:16, :], wa2[i], [], ["wa2b"], "wa2")
        DMA("pool", wa2b[16:17, :], bad[i], [], ["wa2b1"], "bab")
        for d in range(2):
            DMA("pool", rT[d][16:17, :], ones1k[:, :], [], [("rTone", d)], f"rone{d}")
        P.phase = 'g_q'
        s = W.acquire()

        def evq(m, th, b):
            EVAC(qT[:, m * 1024 + th * 512:m * 1024 + (th + 1) * 512], ps[b][:, :], [("ps", b)], [("qT", m, th)], scale=128 ** -0.5)
        proj_fm(s, range(4), evq)
        W.release(s)
        P.phase = 'g_k'
        s = W.acquire()

        def evk(m, th, b):
            EVAC(kT[:, m * 1024 + th * 512:m * 1024 + (th + 1) * 512], ps[b][:, :], [("ps", b)], [("kT", m, th)])
        proj_fm(s, range(4), evk)

        W.release(s)
        for m in range(4):
            for th in range(2):
                b = nb()
                for tt in range(4):
                    t = th * 4 + tt
                    ot = ps[b][:, tt * 64:(tt + 1) * 64].bitcast(BF16)
                    src = kT[:, m * 1024 + t * 128:m * 1024 + (t + 1) * 128]
                    P.add("pe", lambda e, ot=ot, src=src: e.transpose(ot, src, cms(CM_ID)), [("kT", m, th), "cm"], [("ps", b)], cost=0.08)
                EVAC(ktm[:, m * 1024 + th * 512:m * 1024 + (th + 1) * 512], ps[b][:, 0:256].bitcast(BF16), [("ps", b)], [("ktm", th)])
        P.phase = 'g_v'
        for vb in range(2):
            s = W.acquire()

            def evv(t, b, vb=vb):
                EVAC(vtm[:, t * 1024 + vb * 512:t * 1024 + (vb + 1) * 512], ps[b][:, :], [("ps", b)], [("vtm", t, vb)])
            proj_tm(s, 0, 512, evv)
            W.release(s)
        P.phase = 'g_r'
        for d in range(2):
            for th in range(2):
                b = nb()
                for k in range(8):
                    o0 = ((i * 2 + d) * 8 + k) * 16
                    MM(ps[b][0:16, :], wa1b[:, o0:o0 + 16], hTs(k, th), k == 0, k == 7, ["wa1b", ("hT", k, th)], [("ps", b)])
                EVAC(rT[d][0:16, th * 512:(th + 1) * 512], ps[b][0:16, :], [("ps", b)], [("rT", d)])
        mid()

        def load_s0(h_):
            tb_ = 6 + h_ % 2
            for d_ in range(2):
                DMA("sp", TMP[tb_][:, d_ * 256:(d_ + 1) * 256], st_in[i, d_, h_, :, :], [], [("tmp", tb_)], f"sin{h_ % 2}{d_}")
        load_s0(0)
        for h in range(4):
            S = sets[h % 2]
            g2, qe, ke, kp = S["g2"], S["qe"], S["ke"], S["kp"]
            Sall = Sall2[h % 2]
            dcol = dcol2[h % 2]
            dcolr = dcolr2[h % 2]
            hs = h % 2
            hx = hT_all if hs == 1 else []
            hr = hx
            P.phase = f'gh_g{h}'
            for tq in range(2):
                for tp in (2 * tq, 2 * tq + 1):
                    b = tp % 2
                    for t in (2 * tp, 2 * tp + 1):
                        for d in range(2):
                            idx = (t % 2) * 2 + d
                            MM(ps[b][:, idx * 128:(idx + 1) * 128], rT[d][:, t * 128:(t + 1) * 128],
                               wa2b[:, d * 512 + h * 128:d * 512 + (h + 1) * 128], True, True,
                               [("rT", d), ("rTone", d), "wa2b", "wa2b1"], [("ps", b)])
                ACT(tmpall[:, 0:1024], psall[:, 0:1024], AF.Exp, [("ps", 0), ("ps", 1)], [("tmp", 0), ("tmp", 1)], scale=-1.0)
                ACT(g2[:, 2 * tq * 512:(2 * tq + 2) * 512], tmpall[:, 0:1024], AF.Ln, [("tmp", 0), ("tmp", 1)],
                    [("g2", hs, 2 * tq), ("g2", hs, 2 * tq + 1)] + hx, bias=1.0, scale=1.0)
            P.phase = f'gh_decfm{h}'
            for d in range(2):
                for th in range(2):
                    b = th
                    for tt in range(4):
                        t = th * 4 + tt
                        MM(ps[b][:, tt * 128:(tt + 1) * 128], g2[:, (t * 2 + d) * 128:(t * 2 + d + 1) * 128],
                           cms(CM_TRIF + d), True, True, [("g2", hs, t // 2), "cm"] + hr, [("ps", b)])
                te, ti = nt("gA"), nt("gA")
                Eb = TMP[te][:, :].bitcast(BF16)
                Eib = TMP[ti][:, :].bitcast(BF16)
                ACT(Eb, psall[:, 0:1024], AF.Exp, [("ps", 0), ("ps", 1)], [("tmp", te)])
                ACT(Eib, psall[:, 0:1024], AF.Exp, [("ps", 0), ("ps", 1)], [("tmp", ti)], scale=-1.0)
                src = psall[:, 127:1024:128] if d == 0 else psall[:, 0:1024:128]
                ACT(dcol[:, d * 8:d * 8 + 8], src, AF.Exp, [("ps", 0), ("ps", 1)], [("dcol", hs, d, 0), ("dcol", hs, d, 1)])
                TS(dcolr[:, d * 8:d * 8 + 8], dcol[:, d * 8:d * 8 + 8], pv[:, PV_RS:PV_RS + 1], ALU.mult,
                   [("dcol", hs, d, 0), ("dcol", hs, d, 1), "pv"], [("dcolr", hs, d, 0), ("dcolr", hs, d, 1)])
                TT(qe[:, d * 1024:(d + 1) * 1024], qT[:, h * 1024:(h + 1) * 1024], Eb, ALU.mult,
                   [("qT", h, 0), ("qT", h, 1), ("tmp", te)], [("qe", hs, d, 0), ("qe", hs, d, 1)] + hx)
                TT(ke[:, d * 1024:(d + 1) * 1024], kT[:, h * 1024:(h + 1) * 1024], Eib, ALU.mult,
                   [("kT", h, 0), ("kT", h, 1), ("tmp", ti)], [("ke", hs, d, 0), ("ke", hs, d, 1)] + hx)
            P.phase = f'gh_dectm{h}'
            for d in range(2):
                for th in range(2):
                    b = th
                    for tt in range(4):
                        t = th * 4 + tt
                        MM(ps[b][:, tt * 128:(tt + 1) * 128], cms(CM_TRISF + d),
                           g2[:, (t * 2 + d) * 128:(t * 2 + d + 1) * 128], True, True, [("g2", hs, t // 2), "cm"] + hr, [("ps", b)])
                tp_ = nt("gA")
                Epb = TMP[tp_][:, :].bitcast(BF16)
                ACT(Epb, psall[:, 0:1024], AF.Exp, [("ps", 0), ("ps", 1)], [("tmp", tp_)])
                TT(kp[:, d * 1024:(d + 1) * 1024], ktm[:, h * 1024:(h + 1) * 1024], Epb, ALU.mult,
                   [("ktm", 0), ("ktm", 1), ("tmp", tp_)], [("kp", hs, d, 0), ("kp", hs, d, 1)] + hx)
            P.phase = f'gh_chain{h}'
            if h + 1 < 4:
                load_s0(h + 1)
            cur_ap = {d: TMP[6 + h % 2][:, d * 256:(d + 1) * 256] for d in range(2)}
            cur_res = {d: ("tmp", 6 + h % 2) for d in range(2)}
            cur_slot = {0: 0, 1: 0}
            after_b = {0: False, 1: False}
            for n in range(8):
                for d in range(2):
                    t = n if d == 0 else 7 - n
                    Sc, Sres = cur_ap[d], cur_res[d]
                    so = Sall[:, (d * 8 + t) * 256:(d * 8 + t + 1) * 256]
                    if after_b[d]:
                        ACT(so, Sc, AF.Identity, [Sres, "pv"], [("Sall", hs, d, t)], scale=pv[:, PV_RS:PV_RS + 1])
                        col = dcolr[:, d * 8 + t:d * 8 + t + 1]
                        cres = ("dcolr", hs, d, t // 4)
                    else:
                        CP("act", so, Sc, [Sres], [("Sall", hs, d, t)])
                        col = dcol[:, d * 8 + t:d * 8 + t + 1]
                        cres = ("dcol", hs, d, t // 4)
                    b = nb("gB")
                    MM(ps[b][:, 0:256], kp[:, d * 1024 + t * 128:d * 1024 + (t + 1) * 128],
                       vtm[:, t * 1024 + h * 256:t * 1024 + (h + 1) * 256], True, True,
                       [("kp", hs, d, t // 4), ("vtm", t, h // 2)] + hr, [("ps", b)])
                    boundary = (t % 2 == 1) if d == 0 else (t % 2 == 0)
                    if boundary:
                        sq_ = t // 2
                        o_ap = Sfin[:, (d * 4 + sq_) * 256:(d * 4 + sq_ + 1) * 256]
                        o_res = ("Sfin", d, sq_)
                    else:
                        slot = 1 - cur_slot[d]
                        cur_slot[d] = slot
                        o_ap = Srun[:, (d * 2 + slot) * 256:(d * 2 + slot + 1) * 256]
                        o_res = ("Srun", d, slot)
                    STT(o_ap, Sc, col, ps[b][:, 0:256], ALU.mult, ALU.add, [Sres, cres, ("ps", b)], [o_res])
                    if boundary:
                        DMA("sp", st_out[sq_, i, d, h, :, :], o_ap, [o_res], [], f"sf{d}{sq_}")
                    cur_ap[d], cur_res[d] = o_ap, o_res
                    after_b[d] = boundary
            P.phase = f'gh_out{h}'
            pp = 0
            for th in range(2):
                bo = [nb("gCo"), nb("gCo")]
                for tt in range(4):
                    t = th * 4 + tt
                    if tt % 2 == 0:
                        ba_ = nb("gCx")
                        for t2 in (t, t + 1):
                            for d in range(2):
                                c0_ = (t2 - t) * 256 + d * 128
                                MM(ps[ba_][:, c0_:c0_ + 128], ke[:, d * 1024 + t2 * 128:d * 1024 + (t2 + 1) * 128],
                                   qe[:, d * 1024 + t2 * 128:d * 1024 + (t2 + 1) * 128], True, True,
                                   [("ke", hs, d, th), ("qe", hs, d, th)] + hr, [("ps", ba_)])
                    at = AT[pp]
                    TT(at, ps[ba_][:, (tt % 2) * 256:(tt % 2) * 256 + 256], cms(CM_MF, 2), ALU.mult, [("ps", ba_), "cm"], [("AT", pp)])
                    for j in range(2):
                        o = ps[bo[j]][:, tt * 128:(tt + 1) * 128]
                        vs = vtm[:, t * 1024 + h * 256 + j * 128:t * 1024 + h * 256 + (j + 1) * 128]
                        MM(o, vs, at[:, 0:128], True, False, [("vtm", t, h // 2), ("AT", pp)], [("ps", bo[j])])
                        MM(o, vs, at[:, 128:256], False, False, [("vtm", t, h // 2), ("AT", pp)], [("ps", bo[j])])
                        for d in range(2):
                            MM(o, Sall[:, (d * 8 + t) * 256 + j * 128:(d * 8 + t) * 256 + (j + 1) * 128],
                               qe[:, d * 1024 + t * 128:d * 1024 + (t + 1) * 128], False, d == 1,
                               [("Sall", hs, d, t), ("qe", hs, d, th)] + hr, [("ps", bo[j])])
                    pp ^= 1
                sqs = [nsq(), nsq()]
                for j in range(2):
                    ACT(SQ[sqs[j]][:], ps[bo[j]][:, :], AF.Square, [("ps", bo[j])], [("sq", sqs[j])])
                bs = nb("gCx")
                for j in range(2):
                    MM(ps[bs][:, :], ones, SQ[sqs[j]][:], j == 0, j == 1, [("sq", sqs[j]), "cm"], [("ps", bs)])
                ta, tc = nt("gC"), nt("gC")
                ACT(TMP[ta][:], ps[bs][:, :], AF.Ln, [("ps", bs)], [("tmp", ta)], bias=EPS, scale=1.0 / 256)
                ACT(TMP[tc][:], TMP[ta][:], AF.Exp, [("tmp", ta)], [("tmp", tc)], scale=-0.5)
                for j in range(2):
                    fc = h * 2 + j
                    tb = nt("gC")
                    STT(TMP[tb][:], ps[bo[j]][:, :], pv[:, PV_ON + i * 2 + j:PV_ON + i * 2 + j + 1], TMP[tc][:], ALU.mult, ALU.mult,
                        [("ps", bo[j]), "pv", ("tmp", tc)], [("tmp", tb)])
                    sgs = sg[:, fc * 1024 + th * 512:fc * 1024 + (th + 1) * 512]
                    TT(sgs, TMP[tb][:], sgs, ALU.mult, [("tmp", tb), ("sg", fc, th)], [("sg", fc, th)])

    def att_layer(l, mid):
        i = l // 2
        SC = 128 ** -0.5
        cv = Carver(arena, ARENA_BYTES)
        qT = cv.get(8 * 1024, BF16)
        kTf = cv.get(2 * 1280, BF16)
        vst = cv.get(8 * 256, F32)
        VA = cv.get(20 * 130, BF16)[:, 0:20 * 130]
        VA3 = VA.rearrange("p (n f) -> p n f", f=130)
        PT = [cv.get(5 * 512, BF16), cv.get(5 * 512, BF16)]
        rC = cv.get(1024, F32)
        rS = cv.get(1024, F32)
        kst = [cv.get(1024, F32), cv.get(1024, F32)]

        DMA("sp", rC, ropeC[:, :], [], ["rC"], "rC")
        DMA("sp", rS, ropeS[:, :], [], ["rS"], "rS")
        for g in range(2):
            DMA("pool", kTf[:, g * 1280 + 1024:(g + 1) * 1280], ckT_in[i, g, :, :], [], [("kTf", g, 2)], f"ck{g}")
        for u in range(2):
            DMA("pool", VA3[:, 16 + 2 * u:18 + 2 * u, 0:128], cv_in[i, u * 128:(u + 1) * 128, :, :], [], [("VA", 8 + u)], f"cvin{u}")
        CP("dve", VA3[:, :, 128], cms(CM_ONES)[:, 0:20], ["cm"], [("VAone",)])

        TMPX = [cv.get(512, F32) for _ in range(8)]
        SQX = [cv.get(512, BF16) for _ in range(8)]
        nt_all = [TMP[i_][:] for i_ in range(NTMP)] + TMPX
        ns_all = [SQ[i_][:] for i_ in range(NSQ)] + SQX
        xc = {"t": 0, "s": 0}

        def ntx():
            k_ = xc["t"] % len(nt_all)
            xc["t"] += 1
            return nt_all[k_], ("tmp", k_)

        def nsx():
            k_ = xc["s"] % len(ns_all)
            xc["s"] += 1
            return ns_all[k_], ("sq", k_)

        def normrope(b, wcol, th, out_bf, out_f32, kst_res, out_res):
            s0, r0 = nsx()
            ACT(s0, ps[b][:, :], AF.Square, [("ps", b)], [r0])
            b2 = nb("nS")
            MM(ps[b2][:, :], ones, s0, True, True, [r0, "cm"], [("ps", b2)])
            ta, ra = ntx()
            ACT(ta, ps[b2][:, :], AF.Ln, [("ps", b2)], [ra], bias=EPS, scale=1.0 / 128)
            ACT(ta, ta, AF.Exp, [ra], [ra], scale=-0.5)
            td, rd = ntx()
            STT(td, ps[b][:, :], wcol, ta, ALU.mult, ALU.mult, [("ps", b), "pv", ra], [rd])
            s1, r1 = nsx()
            CP("act", s1, td, [rd], [r1])
            b3 = nb("nS")
            MM(ps[b3][:, :], cms(CM_PROT), s1, True, True, [r1, "cm"], [("ps", b3)])
            TT(td, td, rC[:, th * 512:(th + 1) * 512], ALU.mult, [rd, "rC"], [rd])
            t1, rt1 = ntx()
            TT(t1, ps[b3][:, :], rS[:, th * 512:(th + 1) * 512], ALU.mult, [("ps", b3), "rS"], [rt1])
            if out_f32 is None:
                TT(out_bf, td, t1, ALU.add, [rd, rt1], [out_res])
            else:
                TT(out_f32, td, t1, ALU.add, [rd, rt1], [kst_res])
                CP("act", out_bf, out_f32, [kst_res], [out_res])

        P.phase = 'a_kv'
        s = W.acquire()

        def evk(m, th, b):
            normrope(b, pv[:, PV_KN + i:PV_KN + i + 1], th, kTf[:, m * 1280 + th * 512:m * 1280 + (th + 1) * 512],
                     kst[m][:, th * 512:(th + 1) * 512], ("kst", m, th), ("kTf", m, th))
        proj_fm(s, range(2), evk, pool="nP")
        for g in range(2):
            DMA("sp", ckT_out[i, g, :, :], kst[g], [("kst", g, 0), ("kst", g, 1)], [], f"kst{g}")

        def evv(t, b):
            ACT(vst[:, t * 256:(t + 1) * 256], ps[b][:, 0:256], AF.Copy, [("ps", b)], [("vst", t)])
            CP("dve", VA3[:, t * 2:(t + 1) * 2, 0:128], vst[:, t * 256:(t + 1) * 256].rearrange("p (g f) -> p g f", g=2),
               [("vst", t)], [("VA", t)])
        proj_tm(s, 256, 256, evv)
        W.release(s)
        for s_ in range(4):
            DMA("sp", cv_out[s_, i, :, :, :].rearrange("(u p) g d -> p u (g d)", p=128),
                vst[:, s_ * 512:(s_ + 1) * 512].rearrange("p (u f) -> p u f", u=2), [("vst", 2 * s_), ("vst", 2 * s_ + 1)], [], "vst")
        P.phase = 'a_q'
        for qb in range(2):
            s = W.acquire()

            def evq(m, th, b, qb=qb):
                hq = qb * 4 + m
                normrope(b, pv[:, PV_QN + i:PV_QN + i + 1], th, qT[:, hq * 1024 + th * 512:hq * 1024 + (th + 1) * 512],
                         None, None, ("qT", hq, th))
            proj_fm(s, range(4), evq, pool="nP")
            W.release(s)
        mid()

        P.phase = 'a_core'
        pp = 0
        for g in range(2):
            for c in range(4):
                for j in range(4):
                    hq = 4 * g + j
                    pt = PT[pp]
                    qs = qT[:, hq * 1024 + c * 256:hq * 1024 + (c + 1) * 256]
                    order_k = [k_ for k_ in range(5) if k_ != c] + [c]
                    slot_of = {k_: s_ for s_, k_ in enumerate(order_k)}
                    for s_ in range(5):
                        kpr = order_k[s_]
                        for e_ in range(2):
                            kt = 2 * kpr + e_
                            MM(ps[s_][:, e_ * 256:(e_ + 1) * 256], kTf[:, g * 1280 + kt * 128:g * 1280 + (kt + 1) * 128], qs,
                               True, True, [("kTf", g, min(kt // 4, 2)), ("qT", hq, c // 2)], [("ps", s_)])
                    bias_o = pv[:, PV_AM + c * 10 + 2 * order_k[0]:PV_AM + c * 10 + 2 * order_k[0] + 1]
                    for s_ in (0, 2):
                        ACT(pt[:, s_ * 512:(s_ + 2) * 512], psall[:, s_ * 512:(s_ + 2) * 512], AF.Exp,
                            [("ps", s_), ("ps", s_ + 1), "pv"], [("PT", pp, s_), ("PT", pp, s_ + 1)], scale=SC, bias=bias_o)
                    ACT(pt[:, 4 * 512:5 * 512], ps[4][:, :], AF.Exp, [("ps", 4), "pv"], [("PT", pp, 4)], scale=SC,
                        bias=pv[:, PV_AM + c * 10 + 2 * c:PV_AM + c * 10 + 2 * c + 1])
                    bo = nb("aO")
                    for hf in range(2):
                        for kt in range(10):
                            n_ = kt * 2 + g
                            po_ = slot_of[kt // 2] * 512 + (kt % 2) * 256
                            MM(ps[bo][:, hf * 256:hf * 256 + 129], pt[:, po_ + hf * 128:po_ + (hf + 1) * 128],
                               VA[:, n_ * 130:n_ * 130 + 129], kt == 0, kt == 9,
                               [("VA", kt), ("VAone",), ("PT", pp, slot_of[kt // 2])], [("ps", bo)])
                    bt = nb("aO")
                    for hf in range(2):
                        tr_, so_ = nt(), nsq()
                        rc = TMP[tr_][:, 0:1]
                        P.add("dve", lambda e, rc=rc, src=ps[bo][:, hf * 256 + 128:hf * 256 + 129]: e.reciprocal(out=rc, in_=src),
                              [("ps", bo)], [("tmp", tr_)], cost=0.1)
                        on = SQ[so_][:, 0:128]
                        TS(on, ps[bo][:, hf * 256:hf * 256 + 128], rc, ALU.mult, [("ps", bo), ("tmp", tr_)], [("sq", so_)])
                        ot = ps[bt][:, hf * 64:(hf + 1) * 64].bitcast(BF16)
                        P.add("pe", lambda e, ot=ot, on=on: e.transpose(ot, on, cms(CM_ID)), [("sq", so_), "cm"], [("ps", bt)], cost=0.08)
                    sgs = sg[:, hq * 1024 + c * 256:hq * 1024 + (c + 1) * 256]
                    TT(sgs, ps[bt][:, 0:128].bitcast(BF16), sgs, ALU.mult, [("ps", bt), ("sg", hq, c // 2)], [("sg", hq, c // 2)])
                    pp ^= 1

    PRE_OLD = int(_os.environ.get("PRE_OLD", "1"))
    ada(0, "ada", range(4))
    if not PRE_OLD:
        normmod(0)
        gate_phase()
    for l in range(L):
        def mid(l=l):
            if l == 0:
                ada(0, "ada", range(4, 6))
            if l + 1 < L:
                ada(l + 1, "ada")
        if PRE_OLD:
            normmod(l)
            gate_phase()
        if l % 2 == 0:
            gla_layer(l, mid)
        else:
            att_layer(l, mid)
        out_proj(l)
        if l + 1 < L and not PRE_OLD:
            normmod(l + 1)
            gate_phase()
        P.barrier()
    P.phase = 'final'
    for ob in range(2):
        for th in range(2):
            for m in range(4):
                k = ob * 4 + m
                DMA("sp", yT[k * 128:(k + 1) * 128, th * 512:(th + 1) * 512], Xs(k, th), [("X", k, th)], [], "y")

    P.schedule(reorder=REORDER)
    P.finalize()
    assert len(P.dma_names) + 5 <= 100, len(P.dma_names)
    sems = {e: es.enter_context(nc.semaphore(f"s_{e}")) for e in Prog.ENGS}
    dsems = {n: es.enter_context(nc.semaphore(f"d_{n}")) for n in P.dma_names}
    with nc.Block() as block:
        @block.tensor
        def _(e):
            P.emit("pe", e, sems, dsems)

        @block.scalar
        def _(e):
            P.emit("act", e, sems, dsems)

        @block.vector
        def _(e):
            P.emit("dve", e, sems, dsems)

        @block.gpsimd
        def _(e):
            P.emit("pool", e, sems, dsems)

        @block.sync
        def _(e):
            P.emit("sp", e, sems, dsems, final_wait=True)
    es.close()
    return nc, P


def _consts():
    s = np.arange(128)[:, None]
    t = np.arange(128)[None, :]
    cmat = np.zeros((128, NCM, 128), np.float32)
    cmat[:, CM_TRIF] = np.where(s <= t, -1.0 / 16, 0.0)
    cmat[:, CM_TRIB] = np.where(s >= t, -1.0 / 16, 0.0)
    cmat[:, CM_TRISF] = np.where(s > t, -1.0 / 16, 0.0)
    cmat[:, CM_TRISB] = np.where(s < t, -1.0 / 16, 0.0)
    cmat[:, CM_MF] = np.where(s <= t, 1.0, 0.0)
    cmat[:, CM_MB] = np.where(s >= t, 1.0, 0.0)
    prot = np.zeros((128, 128), np.float32)
    for i in range(128):
        if (i % 64) < 32:
            prot[i + 32, i] = -1.0
        else:
            prot[i - 32, i] = 1.0
    cmat[:, CM_PROT] = prot
    cmat[:, CM_ONES] = 1.0
    cmat[:, CM_ID] = np.eye(128, dtype=np.float32)
    return cmat.reshape(128, NCM * 128)


def _rope_tables():
    i = np.arange(128)
    tt = np.arange(1024)
    freqs = (np.float32(10000.0) ** (-np.arange(32, dtype=np.float32) / np.float32(32))).astype(np.float32)
    f = freqs[i % 32][:, None]
    pos = np.where((i < 64)[:, None], (tt // 64)[None, :], (tt % 64)[None, :]).astype(np.float32)
    ang = (pos * f).astype(np.float32)
    return np.cos(ang).astype(np.float32), np.sin(ang).astype(np.float32)


def _prep_inputs(inp):
    f = lambda a: np.ascontiguousarray(np.asarray(a, dtype=np.float32))
    x_prompt, x_sample = f(inp["x_prompt"]), f(inp["x_sample"])
    state_gla, cache_k, cache_v = f(inp["state_gla"]), f(inp["cache_k"]), f(inp["cache_v"])
    c, c_ctx = f(inp["c"]), f(inp["c_ctx"])
    shared = {
        "w_ada": f(inp["w_ada"]), "gla_w_in": f(inp["gla_w_in"]), "gla_w_out": f(inp["gla_w_out"]),
        "att_w_in": f(inp["att_w_in"]), "att_w_out": f(inp["att_w_out"]),
        "cmat": _consts(),
        "ones1k": np.ones((1, 1024), np.float32),
        "wa1": f(f(inp["gla_wa1"]).reshape(2, 2, 8, 128, 16).transpose(3, 0, 1, 2, 4).reshape(128, 512)),
        "wa2": f(f(inp["gla_wa2"]).transpose(0, 2, 1, 3).reshape(2, 16, 1024)),
        "ba": f(f(inp["gla_ba"]).reshape(2, 1, 1024)),
    }
    pv_base = np.zeros((128, NPV), np.float32)
    pv_base[:, PV_BADA:PV_BADA + 96] = f(inp["b_ada"]).reshape(4, 24, 128).transpose(2, 0, 1).reshape(128, 96)
    pv_base[:, PV_NG:PV_NG + 32] = f(inp["norm_g"]).reshape(4, 8, 128).transpose(2, 0, 1).reshape(128, 32)
    pv_base[:, PV_ON:PV_ON + 4] = f(inp["gla_onorm"]).reshape(2, 2, 128).transpose(2, 0, 1).reshape(128, 4)
    pv_base[:, PV_QN:PV_QN + 2] = f(inp["att_qnorm"]).T
    pv_base[:, PV_KN:PV_KN + 2] = f(inp["att_knorm"]).T
    rc, rs = _rope_tables()
    in_maps = []
    for core in range(8):
        m = dict(shared)
        pvv = pv_base.copy()
        if core < 4:
            b = core
            x = x_sample[b]
            cvec = c[b]
            m["st_in"] = f(state_gla[b])
            m["ckT_in"] = f(cache_k[b].transpose(0, 2, 3, 1))
            m["cv_in"] = f(cache_v[b])
            m["ropeC"], m["ropeS"] = rc, rs
            pvv[:, PV_AM:PV_AM + 40] = 0.0
            pvv[:, PV_RS] = 1.0
        else:
            p = core - 4
            x = x_prompt[4 * p:4 * p + 4].reshape(1024, 1024)
            cvec = c_ctx
            m["st_in"] = np.zeros((2, 2, 4, 128, 256), np.float32)
            m["ckT_in"] = np.zeros((2, 2, 128, 256), np.float32)
            m["cv_in"] = np.zeros((2, 256, 2, 128), np.float32)
            m["ropeC"] = np.ones((128, 1024), np.float32)
            m["ropeS"] = np.zeros((128, 1024), np.float32)
            am = np.zeros((4, 10), np.float32)
            for cc in range(4):
                am[cc, 2 * cc] = 1.0
                am[cc, 2 * cc + 1] = 1.0
            pvv[:, PV_AM:PV_AM + 40] = ((1.0 - am) * -30000.0).reshape(1, 40)
            pvv[:, PV_RS] = 0.0
        m["xT"] = f(x.T)
        m["cv8"] = f(cvec.reshape(8, 128).T)
        m["pvec"] = pvv
        in_maps.append(m)
    return in_maps


_NC_CACHE = {}


def run(inputs, L=4, dbg_names=(), trace=False):
    key = (L, tuple(dbg_names))
    if key not in _NC_CACHE:
        _NC_CACHE[key] = build(L, dbg_names)
    nc, P = _NC_CACHE[key]
    in_maps = _prep_inputs(inputs)
    res = run_bass_kernel_spmd(nc, in_maps, core_ids=list(range(8)), trace=trace)
    return res


def kernel(**inputs):
    res = run(inputs)
    r = res.results
    y_sample = np.stack([np.asarray(r[b]["yT"]).T for b in range(4)], 0)
    y_prompt = np.concatenate([np.asarray(r[4 + p]["yT"]).T.reshape(4, 256, 1024) for p in range(4)], 0)
    st = np.concatenate([np.asarray(r[4 + p]["st_out"]) for p in range(4)], 0)
    ck = np.concatenate([np.asarray(r[4 + p]["ckT_out"]).reshape(2, 2, 128, 4, 256).transpose(3, 0, 4, 1, 2) for p in range(4)], 0)
    cvn = np.concatenate([np.asarray(r[4 + p]["cv_out"]) for p in range(4)], 0)
    return (np.ascontiguousarray(y_prompt, dtype=np.float32), np.ascontiguousarray(y_sample, dtype=np.float32),
            np.ascontiguousarray(st, dtype=np.float32), np.ascontiguousarray(ck, dtype=np.float32),
            np.ascontiguousarray(cvn, dtype=np.float32))
```

```python
import numpy as np
from contextlib import ExitStack
import concourse.bass as bass
import concourse.mybir as mybir
from concourse.bass_utils import run_bass_kernel_spmd

F32 = mybir.dt.float32
BF16 = mybir.dt.bfloat16
AF = mybir.ActivationFunctionType
ALU = mybir.AluOpType

D = 1024
NTOK = 1024
EPS = 1e-6
NSLOT = 3
PV_BADA = 0
PV_NG = 96
PV_ON = 128
PV_QN = 132
PV_KN = 134
PV_AM = 136
PV_RS = 176
PV_AM2 = 192
NPV = 272
CM_TRIF, CM_TRIB, CM_TRISF, CM_TRISB, CM_MF, CM_MB, CM_PROT, CM_ONES, CM_ID = range(9)
NCM = 9

SAME_ENGINE_SYNC = True
import os as _os
ATT_STAGE = int(_os.environ.get('ATT_STAGE', '9'))
REORDER = int(_os.environ.get('REORDER', '1')) != 0
PRIO_CP = int(_os.environ.get('PRIO_CP', '1')) != 0
SYNC_SAME_WAR = int(_os.environ.get('SYNC_SAME_WAR', '1')) != 0


class Ins:
    __slots__ = ("eng", "fn", "reads", "writes", "dma", "deps", "alldeps", "needs_inc", "inc_val", "idx", "phase",
                 "epoch", "cost", "nbytes", "rdep", "t0", "t1")

    def __init__(self, eng, fn, reads, writes, dma):
        self.eng = eng
        self.fn = fn
        self.reads = reads
        self.writes = writes
        self.dma = dma
        self.deps = []
        self.alldeps = []
        self.needs_inc = False
        self.inc_val = None
        self.rdep = None


class Prog:
    ENGS = ("pe", "act", "dve", "pool", "sp")

    def __init__(self):
        self.ins = []
        self.last_write = {}
        self.readers = {}
        self.epoch = 0
        self.dma_names = []
        self.phase = ""

    def barrier(self):
        self.epoch += 1

    def add(self, eng, fn, reads=(), writes=(), dma=None, cost=0.1, nbytes=0):
        I = Ins(eng, fn, tuple(reads), tuple(writes), dma)
        I.idx = len(self.ins)
        I.phase = self.phase
        I.epoch = self.epoch
        I.cost = cost
        I.nbytes = nbytes
        deps = {}
        for r in I.reads:
            w = self.last_write.get(r)
            if w is not None:
                deps[w.idx] = (w, "raw")
        for w_ in I.writes:
            lw = self.last_write.get(w_)
            if lw is not None and lw.idx not in deps:
                deps[lw.idx] = (lw, "waw")
            for rd in self.readers.get(w_, ()):
                if rd.idx not in deps:
                    deps[rd.idx] = (rd, "war")
        for r in I.reads:
            self.readers.setdefault(r, []).append(I)
        for w_ in I.writes:
            self.last_write[w_] = I
            self.readers[w_] = []
        I.alldeps = list(deps.values())
        if dma is not None and dma not in self.dma_names:
            self.dma_names.append(dma)
        self.ins.append(I)
        return I

    def schedule(self, reorder=True):
        LAT_X, LAT_S = 0.2, 0.15
        WIN = int(_os.environ.get("WIN", "400"))
        t_eng = {e: 0.0 for e in self.ENGS}
        fin = {}
        dma_free = [0.0]
        new_order = []
        nseg = self.epoch + 1
        segs = [[] for _ in range(nseg)]
        for I in self.ins:
            segs[I.epoch].append(I)
        bar = []

        def lat(J, I, kind):
            if J.dma is not None:
                return 0.0
            if J.eng != I.eng:
                return 0.5 if J.eng == "pe" else LAT_X
            if I.eng == "pe" or (kind != "raw" and not SYNC_SAME_WAR):
                return 0.0
            return LAT_S

        for seg in segs:
            bar_time = {}
            for e in self.ENGS:
                bt = 0.0
                for J in bar:
                    if J.dma is not None or J.eng != e:
                        bt = max(bt, fin[J.idx] + (0.0 if J.dma is not None else 0.5))
                bar_time[e] = bt
            for I in seg:
                I.deps = [J for J in bar if (J.dma is not None or J.eng != I.eng)]
            pend = {e: [I for I in seg if I.eng == e] for e in self.ENGS}
            bl = {}
            if PRIO_CP:
                succ = {}
                inseg = set(I.idx for I in seg)
                for I in seg:
                    for J, kind in I.alldeps:
                        if J.idx in inseg:
                            succ.setdefault(J.idx, []).append((I, kind))
                for I in reversed(seg):
                    c_ = I.cost if I.dma is None else 3.0
                    m_ = 0.0
                    for K_, kind in succ.get(I.idx, ()):
                        m_ = max(m_, bl[K_.idx] + lat(I, K_, kind))
                    bl[I.idx] = c_ + m_
            head = {e: 0 for e in self.ENGS}
            done = set()
            remaining = len(seg)
            last_sched = {}
            while remaining:
                best = None
                for e in self.ENGS:
                    lst = pend[e]
                    h = head[e]
                    while h < len(lst) and lst[h].idx in done:
                        h += 1
                    head[e] = h
                    W = WIN if (reorder and e in ("pe", "act", "dve")) else 1
                    cnt = 0
                    k = h
                    while k < len(lst) and cnt < W:
                        I = lst[k]
                        k += 1
                        if I.idx in done:
                            continue
                        cnt += 1
                        if I.rdep is None:
                            r = 0.0
                            ok = True
                            for J, kind in I.alldeps:
                                f = fin.get(J.idx)
                                if f is None:
                                    ok = False
                                    break
                                r = max(r, f + lat(J, I, kind))
                            if not ok:
                                continue
                            I.rdep = r
                        r = max(I.rdep, bar_time[e], t_eng[e])
                        key = (int(r / 0.17), -bl[I.idx], I.idx) if PRIO_CP else (int(r / 0.3), I.idx)
                        if best is None or key < best[0]:
                            best = (key, r, e, I)
                assert best is not None, "scheduler deadlock"
                _, r, e, I = best
                I.t0 = r
                if I.dma is not None:
                    occ = 1.2 if e == "pool" else 0.15
                    if e == "sp":
                        f = r + 10.0 + I.nbytes / 150e3
                    elif I.nbytes > 200000:
                        st = max(r, dma_free[0])
                        f = st + 2.0 + I.nbytes / 200e3
                        dma_free[0] = f - 2.0
                    else:
                        f = r + 10.0
                    t_eng[e] = r + occ
                else:
                    f = r + I.cost
                    t_eng[e] = f
                    last_sched[e] = I
                fin[I.idx] = f
                I.t1 = f
                done.add(I.idx)
                new_order.append(I)
                remaining -= 1
            bar = list(last_sched.values())
            ld = {}
            for I in seg:
                if I.dma is not None and not I.dma.startswith("w"):
                    ld[I.dma] = I
            bar += list(ld.values())
        self.ins = new_order
        self.sim_time = max(fin.values())
        pos = {}
        for n, I in enumerate(self.ins):
            pos[I.idx] = n
        for I in self.ins:
            cand = list(I.deps)
            for J, kind in I.alldeps:
                if J.dma is None and J.eng == I.eng and (I.eng == "pe" or (kind != "raw" and not SYNC_SAME_WAR)):
                    continue
                cand.append(J)
            keep = {}
            for J in cand:
                key = ("d", J.dma) if J.dma is not None else ("e", J.eng)
                if key not in keep or pos[J.idx] > pos[keep[key].idx]:
                    keep[key] = J
            I.deps = list(keep.values())
            for J in I.deps:
                J.needs_inc = True

    def finalize(self):
        cnt = {e: 0 for e in self.ENGS}
        dcnt = {s: 0 for s in self.dma_names}
        for I in self.ins:
            if I.dma is not None:
                dcnt[I.dma] += 16
                I.inc_val = dcnt[I.dma]
            elif I.needs_inc:
                cnt[I.eng] += 1
                I.inc_val = cnt[I.eng]
        self.cnt = cnt
        self.dcnt = dcnt

    def emit(self, eng_name, eng_obj, sems, dsems, final_wait=False):
        known = {}
        for I in self.ins:
            if I.eng != eng_name:
                continue
            for J in I.deps:
                if J.dma is not None:
                    s = dsems[J.dma]
                    key = ("d", J.dma)
                else:
                    s = sems[J.eng]
                    key = ("e", J.eng)
                if known.get(key, 0) < J.inc_val:
                    eng_obj.wait_ge(s, J.inc_val)
                    known[key] = J.inc_val
            r = I.fn(eng_obj)
            if I.dma is not None:
                r.then_inc(dsems[I.dma], 16)
            elif I.needs_inc:
                r.then_inc(sems[I.eng], 1)
        if final_wait:
            for name, s in dsems.items():
                if self.dcnt[name] > 0:
                    eng_obj.wait_ge(s, self.dcnt[name])


def build(L=4, dbg_names=()):
    nc = bass.Bass("TRN2", target_bir_lowering=False)
    P = Prog()

    def din(name, shape):
        return nc.dram_tensor(name, list(shape), F32, kind="ExternalInput").ap()

    def dout(name, shape, dt=F32):
        return nc.dram_tensor(name, list(shape), dt, kind="ExternalOutput").ap()

    xT = din("xT", [1024, 1024])
    cv8 = din("cv8", [128, 8])
    pvec = din("pvec", [128, NPV])
    wa1 = din("wa1", [128, 512])
    wa2 = din("wa2", [2, 16, 1024])
    bad = din("ba", [2, 1, 1024])
    cmat = din("cmat", [128, NCM * 128])
    ones1k = din("ones1k", [1, 1024])
    ropeC = din("ropeC", [128, 1024])
    ropeS = din("ropeS", [128, 1024])
    st_in = din("st_in", [2, 2, 4, 128, 256])
    ckT_in = din("ckT_in", [2, 2, 128, 256])
    cv_in = din("cv_in", [2, 256, 2, 128])
    w_ada = din("w_ada", [4, 1024, 3072])
    gla_w_in = din("gla_w_in", [2, 1024, 3072])
    gla_w_out = din("gla_w_out", [2, 1024, 1024])
    att_w_in = din("att_w_in", [2, 1024, 2560])
    att_w_out = din("att_w_out", [2, 1024, 1024])
    yT = dout("yT", [1024, 1024])
    st_out = dout("st_out", [4, 2, 2, 4, 128, 256])
    ckT_out = dout("ckT_out", [2, 2, 128, 1024])
    cv_out = dout("cv_out", [4, 2, 256, 2, 128])
    dbg_out = {}

    es = ExitStack()

    def sb(name, shape, dt):
        return es.enter_context(nc.sbuf_tensor(name, list(shape), dt))

    X = sb("X", [128, 8 * 1024], F32)
    hT = sb("hT", [128, 8 * 1024], BF16)
    sg = sb("sg", [128, 8 * 1024], BF16)
    Wt = [sb(f"W{s}", [128, 8, 512], BF16) for s in range(NSLOT)]
    pv = sb("pv", [128, NPV], F32)
    cv8t = sb("cv8t", [128, 8], F32)
    scb = sb("scb", [128, 8], BF16)
    modt = sb("modt", [128, 4 * 24], F32)
    Gp = sb("Gp", [128, 4 * 8], F32)
    cm = sb("cm", [128, NCM * 128], BF16)
    wa1b = sb("wa1b", [128, 512], BF16)
    NTMP = int(_os.environ.get("NTMP", "8"))
    tmpall = sb("tmpall", [128, NTMP * 512], F32)
    TMP = [tmpall[:, i * 512:(i + 1) * 512] for i in range(NTMP)]
    NSQ = int(_os.environ.get("NSQ", "4"))
    SQ = [sb(f"sq{i}", [128, 512], BF16) for i in range(NSQ)]
    ARENA_BYTES = 94 * 1024
    arena = sb("arena", [128, ARENA_BYTES // 4], F32)
    psall = es.enter_context(nc.psum_tensor("psall", [128, 8 * 512], F32))
    ps = [psall[:, i * 512:(i + 1) * 512] for i in range(8)]

    class Carver:
        def __init__(self, base, nbytes_total):
            self.off = 0
            self.base = base
            self.total = nbytes_total

        def get(self, nelem, dt, rows=128):
            nbytes = nelem * (4 if dt == F32 else 2)
            nbytes = (nbytes + 31) // 32 * 32
            assert self.off + nbytes <= self.total, (self.off, nbytes)
            if self.base is arena:
                v = arena[0:rows, self.off // 4:(self.off + nbytes) // 4]
                if dt != F32:
                    v = v.bitcast(dt)
            else:
                assert dt == BF16
                v = self.base[0:rows, self.off // 2:(self.off + nbytes) // 2]
            self.off += nbytes
            return v

    ctr = {"bank": 0, "tmp": 0, "sq": 0, "evac": 0}

    POOLS = {"d": [0, 1, 2, 3, 4, 5, 6], "ada": [7],
             "gA": [0, 1], "gB": [2, 3], "gCo": [4, 5], "gCx": [6],
             "aQK": [0, 1, 2, 3, 4], "aO": [5, 6],
             "nP": [0, 1, 2, 3], "nS": [4, 5, 6]}
    import json as _json
    if _os.environ.get("GPOOLS"):
        POOLS.update(_json.loads(_os.environ["GPOOLS"]))
    pctr = {k: 0 for k in POOLS}

    def nb(pool="d"):
        lst = POOLS[pool]
        b = lst[pctr[pool] % len(lst)]
        pctr[pool] += 1
        return b

    TPOOLS = {"d": list(range(NTMP)), "gA": [0, 1, 2], "gC": [3, 4, 5]}
    tctr = {k: 0 for k in TPOOLS}

    def nt(pool="d"):
        lst = TPOOLS[pool]
        t = lst[tctr[pool] % len(lst)]
        tctr[pool] += 1
        return t

    def nsq():
        t = ctr["sq"] % NSQ
        ctr["sq"] += 1
        return t

    def is_ps(ap):
        return str(ap.space).endswith("PSUM")

    def c_act(out, in_):
        return 0.15 + out.free_size() * 0.0008

    def c_dve(out, k=1.0):
        return 0.07 + out.free_size() * 0.0011 * k

    def ACT(out, in_, func, reads, writes, bias=None, scale=None):
        kw = {}
        if bias is not None:
            kw["bias"] = bias
        if scale is not None:
            kw["scale"] = scale
        P.add("act", lambda e: e.activation(out=out, in_=in_, func=func, **kw), reads, writes, cost=c_act(out, in_))

    def TT(out, in0, in1, op, reads, writes):
        P.add("dve", lambda e: e.tensor_tensor(out=out, in0=in0, in1=in1, op=op), reads, writes, cost=c_dve(out))

    def TS(out, in0, s1, op0, reads, writes, s2=None, op1=None):
        if op1 is None:
            P.add("dve", lambda e: e.tensor_scalar(out=out, in0=in0, scalar1=s1, scalar2=None, op0=op0), reads, writes, cost=c_dve(out))
        else:
            P.add("dve", lambda e: e.tensor_scalar(out=out, in0=in0, scalar1=s1, scalar2=s2, op0=op0, op1=op1), reads, writes, cost=c_dve(out))

    def STT(out, in0, scalar, in1, op0, op1, reads, writes):
        P.add("dve", lambda e: e.scalar_tensor_tensor(out=out, in0=in0, scalar=scalar, in1=in1, op0=op0, op1=op1), reads, writes,
              cost=c_dve(out, 1.15))

    def CP(eng, out, in_, reads, writes):
        if eng == "act":
            P.add("act", lambda e: e.activation(out=out, in_=in_, func=AF.Copy), reads, writes, cost=c_act(out, in_))
        else:
            P.add("dve", lambda e: e.tensor_copy(out=out, in_=in_), reads, writes, cost=c_dve(out))

    def RECIP(out, in_, reads, writes):
        P.add("dve", lambda e: e.reciprocal(out=out, in_=in_), reads, writes, cost=0.1 + out.free_size() * 0.0065)

    def MM(out, lhsT, rhs, start, stop, reads, writes):
        P.add("pe", lambda e: e.matmul(out, lhsT, rhs, start=start, stop=stop), reads, writes,
              cost=0.03 + max(out.free_size(), 100) / 2700.0)

    def DMA(queue, out, in_, reads, writes, sem):
        P.add(queue, lambda e: e.dma_start(out=out, in_=in_), reads, writes, dma=sem, nbytes=in_.nbytes())

    def EVAC(out, in_, reads, writes, scale=None):
        ctr["evac"] += 1
        if ctr["evac"] % 2 == 0:
            ACT(out, in_, AF.Copy, reads, writes, scale=scale)
        else:
            if scale is None:
                CP("dve", out, in_, reads, writes)
            else:
                TS(out, in_, scale, ALU.mult, reads, writes)

    def DBG(name, ap, shape, reads, dt=F32):
        if name not in dbg_names:
            return
        o = dout("dbg_" + name, shape, dt)
        dbg_out[name] = o
        DMA("sp", o, ap, reads, [], "dbg_" + name)

    def Xs(k, th):
        return X[:, k * 1024 + th * 512:k * 1024 + (th + 1) * 512]

    def hTs(k, th):
        return hT[:, k * 1024 + th * 512:k * 1024 + (th + 1) * 512]

    def cms(slot, n=1):
        return cm[:, slot * 128:(slot + n) * 128]

    ones = cms(CM_ONES)

    def blk(ap, l, c0):
        return ap[l, :, c0:c0 + 512].rearrange("(k p) n -> p k n", p=128)

    def ada_blocks(l):
        return [blk(w_ada, l, j * 512) for j in range(6)]

    srcs = []
    for l in range(L):
        i = l // 2
        if l == 0:
            srcs += ada_blocks(0)[0:4]
        if l % 2 == 0:
            srcs += [blk(gla_w_in, i, c) for c in (2048, 2560, 0, 512, 1024, 1536)]
        else:
            srcs += [blk(att_w_in, i, c) for c in (1536, 2048, 1024, 0, 512)]
        if l == 0:
            srcs += ada_blocks(0)[4:6]
        if l + 1 < L:
            srcs += ada_blocks(l + 1)
        wo = gla_w_out if l % 2 == 0 else att_w_out
        srcs += [blk(wo, i, 0), blk(wo, i, 512)]

    class WStream:
        def __init__(self):
            self.next_issue = 0
            self.next_use = 0
            self.slot_of = {}

        def issue(self, slot):
            if self.next_issue >= len(srcs):
                return
            src = srcs[self.next_issue]
            DMA("pool", Wt[slot][:], src, [], [("W", slot)], f"w{slot}")
            self.slot_of[self.next_issue] = slot
            self.next_issue += 1

        def acquire(self):
            s = self.slot_of[self.next_use]
            self.next_use += 1
            return s

        def release(self, slot):
            self.issue(slot)

    W = WStream()

    DMA("sp", pv[:], pvec[:, :], [], ["pv"], "pv")
    DMA("sp", cv8t[:], cv8[:, :], [], ["cv8t"], "cv8")
    DMA("pool", cm[:], cmat[:, :], [], ["cm"], "cm")
    DMA("pool", wa1b[:], wa1[:, :], [], ["wa1b"], "wa1")
    for s in range(NSLOT):
        W.issue(s)
    for k in range(8):
        DMA("sp", X[:, k * 1024:(k + 1) * 1024], xT[k * 128:(k + 1) * 128, :], [], [("X", k, 0), ("X", k, 1)], f"x{k}")
    ACT(scb[:], cv8t[:], AF.Silu, ["cv8t"], ["scb"])

    def ada(l, pool="ada", blocks=range(6)):
        P.phase = 'ada'
        for b6 in blocks:
            s = W.acquire()
            mb = nb(pool)
            for m in range(4):
                for k in range(8):
                    MM(ps[mb][:, m:m + 1], Wt[s][:, k, m * 128:(m + 1) * 128], scb[:, k:k + 1], k == 0, k == 7,
                       [("W", s), "scb"], [("ps", mb)])
            W.release(s)
            TT(modt[:, l * 24 + b6 * 4:l * 24 + b6 * 4 + 4], ps[mb][:, 0:4],
               pv[:, PV_BADA + l * 24 + b6 * 4:PV_BADA + l * 24 + b6 * 4 + 4], ALU.add, [("ps", mb), "pv"], [("mod", l, b6)])
        if 3 in blocks:
            STT(Gp[:, l * 8:(l + 1) * 8], modt[:, l * 24 + 8:l * 24 + 16], 1.0, pv[:, PV_NG + l * 8:PV_NG + (l + 1) * 8],
                ALU.add, ALU.mult, [("mod", l, 2), ("mod", l, 3), "pv"], [("Gp", l)])

    def normmod(l):
        P.phase = 'normmod'
        for th in range(2):
            b = nb()
            for k in range(8):
                s_ = nsq()
                if k % 2 == 0:
                    ACT(SQ[s_][:], Xs(k, th), AF.Square, [("X", k, th)], [("sq", s_)])
                else:
                    TT(SQ[s_][:], Xs(k, th), Xs(k, th), ALU.mult, [("X", k, th)], [("sq", s_)])
                MM(ps[b][:, :], ones, SQ[s_][:], k == 0, k == 7, [("sq", s_), "cm"], [("ps", b)])
            ta, tc = nt(), nt()
            ACT(TMP[ta][:], ps[b][:, :], AF.Ln, [("ps", b)], [("tmp", ta)], bias=EPS, scale=1.0 / D)
            ACT(TMP[tc][:], TMP[ta][:], AF.Exp, [("tmp", ta)], [("tmp", tc)], scale=-0.5)
            for k in range(8):
                tb = nt()
                STT(TMP[tb][:], Xs(k, th), Gp[:, l * 8 + k:l * 8 + k + 1], TMP[tc][:], ALU.mult, ALU.mult,
                    [("X", k, th), ("Gp", l), ("tmp", tc)], [("tmp", tb)])
                ACT(hTs(k, th), TMP[tb][:], AF.Identity, [("tmp", tb), ("mod", l, 0), ("mod", l, 1)], [("hT", k, th)],
                    bias=modt[:, l * 24 + k:l * 24 + k + 1], scale=1.0)

    def proj_fm(slot, ms, evac, pool="d", kouter=False):
        if kouter:
            for th in range(2):
                bs_ = {m: nb(pool) for m in ms}
                for k in range(8):
                    for m in ms:
                        MM(ps[bs_[m]][:, :], Wt[slot][:, k, m * 128:(m + 1) * 128], hTs(k, th), k == 0, k == 7,
                           [("W", slot), ("hT", k, th)], [("ps", bs_[m])])
                for m in ms:
                    evac(m, th, bs_[m])
            return
        for th in range(2):
            for m in ms:
                b = nb(pool)
                for k in range(8):
                    MM(ps[b][:, :], Wt[slot][:, k, m * 128:(m + 1) * 128], hTs(k, th), k == 0, k == 7,
                       [("W", slot), ("hT", k, th)], [("ps", b)])
                evac(m, th, b)

    def proj_tm(slot, c0, ncols, evac):
        for t in range(8):
            b = nb()
            for k in range(8):
                MM(ps[b][:, 0:ncols], hT[:, k * 1024 + t * 128:k * 1024 + (t + 1) * 128], Wt[slot][:, k, c0:c0 + ncols],
                   k == 0, k == 7, [("W", slot), ("hT", k, t // 4)], [("ps", b)])
            evac(t, b)

    def out_proj(l):
        P.phase = 'outproj'
        for ob in range(2):
            s = W.acquire()
            for th in range(2):
                for m in range(4):
                    fc = ob * 4 + m
                    b = nb()
                    for k in range(8):
                        MM(ps[b][:, :], Wt[s][:, k, m * 128:(m + 1) * 128],
                           sg[:, k * 1024 + th * 512:k * 1024 + (th + 1) * 512], k == 0, k == 7,
                           [("W", s), ("sg", k, th)], [("ps", b)])
                    STT(Xs(fc, th), ps[b][:, :], modt[:, l * 24 + 16 + fc:l * 24 + 17 + fc], Xs(fc, th), ALU.mult, ALU.add,
                        [("ps", b), ("mod", l, 4), ("mod", l, 5), ("X", fc, th)], [("X", fc, th)])
            W.release(s)

    def gate_phase():
        P.phase = 'gate'
        for gb in range(2):
            s = W.acquire()

            def ev(m, th, b, gb=gb):
                fc = gb * 4 + m
                ACT(sg[:, fc * 1024 + th * 512:fc * 1024 + (th + 1) * 512], ps[b][:, :], AF.Silu, [("ps", b)], [("sg", fc, th)])
            proj_fm(s, range(4), ev, kouter=(gb == 0))
            W.release(s)

    def gla_layer(l, mid):
        i = l // 2
        cv = Carver(arena, ARENA_BYTES)
        qT = cv.get(4 * 1024, BF16)
        kT = cv.get(4 * 1024, BF16)
        ktm = cv.get(8 * 512, BF16)
        vtm = cv.get(8 * 1024, BF16)
        rT = [cv.get(1024, BF16, rows=17), cv.get(1024, BF16, rows=17)]
        wa2b = cv.get(1024, BF16, rows=17)
        Srun = cv.get(2 * 2 * 256, F32)
        Sfin = cv.get(2 * 4 * 256, F32)
        AT = [cv.get(256, BF16), cv.get(256, BF16)]
        hcv = Carver(hT, 16 * 1024)
        sets = []
        for si in range(2):
            c_ = cv if si == 0 else hcv
            sets.append(dict(g2=c_.get(16 * 128, BF16), qe=c_.get(2 * 1024, BF16), ke=c_.get(2 * 1024, BF16),
                             kp=c_.get(2 * 8 * 128, BF16)))
        Sall2 = [cv.get(2 * 8 * 256, BF16), cv.get(2 * 8 * 256, BF16)]
        dcol2 = [cv.get(16, F32), cv.get(16, F32)]
        dcolr2 = [cv.get(16, F32), cv.get(16, F32)]
        hT_all = [("hT", k, th) for k in range(8) for th in range(2)]

        DMA("pool", wa2b[0:16, :], wa2[i], [], ["wa2b"], "wa2")
        DMA("pool", wa2b[16:17, :], bad[i], [], ["wa2b1"], "bab")
        for d in range(2):
            DMA("pool", rT[d][16:17, :], ones1k[:, :], [], [("rTone", d)], f"rone{d}")
        P.phase = 'g_q'
        s = W.acquire()

        def evq(m, th, b):
            EVAC(qT[:, m * 1024 + th * 512:m * 1024 + (th + 1) * 512], ps[b][:, :], [("ps", b)], [("qT", m, th)], scale=128 ** -0.5)
        proj_fm(s, range(4), evq)
        W.release(s)
        P.phase = 'g_k'
        s = W.acquire()

        def evk(m, th, b):
            EVAC(kT[:, m * 1024 + th * 512:m * 1024 + (th + 1) * 512], ps[b][:, :], [("ps", b)], [("kT", m, th)])
        proj_fm(s, range(4), evk)

        W.release(s)
        for m in range(4):
            for th in range(2):
                b = nb()
                for tt in range(4):
                    t = th * 4 + tt
                    ot = ps[b][:, tt * 64:(tt + 1) * 64].bitcast(BF16)
                    src = kT[:, m * 1024 + t * 128:m * 1024 + (t + 1) * 128]
                    P.add("pe", lambda e, ot=ot, src=src: e.transpose(ot, src, cms(CM_ID)), [("kT", m, th), "cm"], [("ps", b)], cost=0.08)
                EVAC(ktm[:, m * 1024 + th * 512:m * 1024 + (th + 1) * 512], ps[b][:, 0:256].bitcast(BF16), [("ps", b)], [("ktm", th)])
        P.phase = 'g_v'
        for vb in range(2):
            s = W.acquire()

            def evv(t, b, vb=vb):
                EVAC(vtm[:, t * 1024 + vb * 512:t * 1024 + (vb + 1) * 512], ps[b][:, :], [("ps", b)], [("vtm", t, vb)])
            proj_tm(s, 0, 512, evv)
            W.release(s)
        P.phase = 'g_r'
        for d in range(2):
            for th in range(2):
                b = nb()
                for k in range(8):
                    o0 = ((i * 2 + d) * 8 + k) * 16
                    MM(ps[b][0:16, :], wa1b[:, o0:o0 + 16], hTs(k, th), k == 0, k == 7, ["wa1b", ("hT", k, th)], [("ps", b)])
                EVAC(rT[d][0:16, th * 512:(th + 1) * 512], ps[b][0:16, :], [("ps", b)], [("rT", d)])
        mid()

        def load_s0(h_):
            tb_ = 6 + h_ % 2
            for d_ in range(2):
                DMA("sp", TMP[tb_][:, d_ * 256:(d_ + 1) * 256], st_in[i, d_, h_, :, :], [], [("tmp", tb_)], f"sin{h_ % 2}{d_}")
        load_s0(0)
        for h in range(4):
            S = sets[h % 2]
            g2, qe, ke, kp = S["g2"], S["qe"], S["ke"], S["kp"]
            Sall = Sall2[h % 2]
            dcol = dcol2[h % 2]
            dcolr = dcolr2[h % 2]
            hs = h % 2
            hx = hT_all if hs == 1 else []
            hr = hx
            P.phase = f'gh_g{h}'
            for tq in range(2):
                for tp in (2 * tq, 2 * tq + 1):
                    b = tp % 2
                    for t in (2 * tp, 2 * tp + 1):
                        for d in range(2):
                            idx = (t % 2) * 2 + d
                            MM(ps[b][:, idx * 128:(idx + 1) * 128], rT[d][:, t * 128:(t + 1) * 128],
                               wa2b[:, d * 512 + h * 128:d * 512 + (h + 1) * 128], True, True,
                               [("rT", d), ("rTone", d), "wa2b", "wa2b1"], [("ps", b)])
                ACT(tmpall[:, 0:1024], psall[:, 0:1024], AF.Exp, [("ps", 0), ("ps", 1)], [("tmp", 0), ("tmp", 1)], scale=-1.0)
                ACT(g2[:, 2 * tq * 512:(2 * tq + 2) * 512], tmpall[:, 0:1024], AF.Ln, [("tmp", 0), ("tmp", 1)],
                    [("g2", hs, 2 * tq), ("g2", hs, 2 * tq + 1)] + hx, bias=1.0, scale=1.0)
            P.phase = f'gh_decfm{h}'
            for d in range(2):
                for th in range(2):
                    b = th
                    for tt in range(4):
                        t = th * 4 + tt
                        MM(ps[b][:, tt * 128:(tt + 1) * 128], g2[:, (t * 2 + d) * 128:(t * 2 + d + 1) * 128],
                           cms(CM_TRIF + d), True, True, [("g2", hs, t // 2), "cm"] + hr, [("ps", b)])
                te, ti = nt("gA"), nt("gA")
                Eb = TMP[te][:, :].bitcast(BF16)
                Eib = TMP[ti][:, :].bitcast(BF16)
                ACT(Eb, psall[:, 0:1024], AF.Exp, [("ps", 0), ("ps", 1)], [("tmp", te)])
                ACT(Eib, psall[:, 0:1024], AF.Exp, [("ps", 0), ("ps", 1)], [("tmp", ti)], scale=-1.0)
                src = psall[:, 127:1024:128] if d == 0 else psall[:, 0:1024:128]
                ACT(dcol[:, d * 8:d * 8 + 8], src, AF.Exp, [("ps", 0), ("ps", 1)], [("dcol", hs, d, 0), ("dcol", hs, d, 1)])
                TS(dcolr[:, d * 8:d * 8 + 8], dcol[:, d * 8:d * 8 + 8], pv[:, PV_RS:PV_RS + 1], ALU.mult,
                   [("dcol", hs, d, 0), ("dcol", hs, d, 1), "pv"], [("dcolr", hs, d, 0), ("dcolr", hs, d, 1)])
                TT(qe[:, d * 1024:(d + 1) * 1024], qT[:, h * 1024:(h + 1) * 1024], Eb, ALU.mult,
                   [("qT", h, 0), ("qT", h, 1), ("tmp", te)], [("qe", hs, d, 0), ("qe", hs, d, 1)] + hx)
                TT(ke[:, d * 1024:(d + 1) * 1024], kT[:, h * 1024:(h + 1) * 1024], Eib, ALU.mult,
                   [("kT", h, 0), ("kT", h, 1), ("tmp", ti)], [("ke", hs, d, 0), ("ke", hs, d, 1)] + hx)
            P.phase = f'gh_dectm{h}'
            for d in range(2):
                for th in range(2):
                    b = th
                    for tt in range(4):
                        t = th * 4 + tt
                        MM(ps[b][:, tt * 128:(tt + 1) * 128], cms(CM_TRISF + d),
                           g2[:, (t * 2 + d) * 128:(t * 2 + d + 1) * 128], True, True, [("g2", hs, t // 2), "cm"] + hr, [("ps", b)])
                tp_ = nt("gA")
                Epb = TMP[tp_][:, :].bitcast(BF16)
                ACT(Epb, psall[:, 0:1024], AF.Exp, [("ps", 0), ("ps", 1)], [("tmp", tp_)])
                TT(kp[:, d * 1024:(d + 1) * 1024], ktm[:, h * 1024:(h + 1) * 1024], Epb, ALU.mult,
                   [("ktm", 0), ("ktm", 1), ("tmp", tp_)], [("kp", hs, d, 0), ("kp", hs, d, 1)] + hx)
            P.phase = f'gh_chain{h}'
            if h + 1 < 4:
                load_s0(h + 1)
            cur_ap = {d: TMP[6 + h % 2][:, d * 256:(d + 1) * 256] for d in range(2)}
            cur_res = {d: ("tmp", 6 + h % 2) for d in range(2)}
            cur_slot = {0: 0, 1: 0}
            after_b = {0: False, 1: False}
            for n in range(8):
                for d in range(2):
                    t = n if d == 0 else 7 - n
                    Sc, Sres = cur_ap[d], cur_res[d]
                    so = Sall[:, (d * 8 + t) * 256:(d * 8 + t + 1) * 256]
                    if after_b[d]:
                        ACT(so, Sc, AF.Identity, [Sres, "pv"], [("Sall", hs, d, t)], scale=pv[:, PV_RS:PV_RS + 1])
                        col = dcolr[:, d * 8 + t:d * 8 + t + 1]
                        cres = ("dcolr", hs, d, t // 4)
                    else:
                        CP("act", so, Sc, [Sres], [("Sall", hs, d, t)])
                        col = dcol[:, d * 8 + t:d * 8 + t + 1]
                        cres = ("dcol", hs, d, t // 4)
                    b = nb("gB")
                    MM(ps[b][:, 0:256], kp[:, d * 1024 + t * 128:d * 1024 + (t + 1) * 128],
                       vtm[:, t * 1024 + h * 256:t * 1024 + (h + 1) * 256], True, True,
                       [("kp", hs, d, t // 4), ("vtm", t, h // 2)] + hr, [("ps", b)])
                    boundary = (t % 2 == 1) if d == 0 else (t % 2 == 0)
                    if boundary:
                        sq_ = t // 2
                        o_ap = Sfin[:, (d * 4 + sq_) * 256:(d * 4 + sq_ + 1) * 256]
                        o_res = ("Sfin", d, sq_)
                    else:
                        slot = 1 - cur_slot[d]
                        cur_slot[d] = slot
                        o_ap = Srun[:, (d * 2 + slot) * 256:(d * 2 + slot + 1) * 256]
                        o_res = ("Srun", d, slot)
                    STT(o_ap, Sc, col, ps[b][:, 0:256], ALU.mult, ALU.add, [Sres, cres, ("ps", b)], [o_res])
                    if boundary:
                        DMA("sp", st_out[sq_, i, d, h, :, :], o_ap, [o_res], [], f"sf{d}{sq_}")
                    cur_ap[d], cur_res[d] = o_ap, o_res
                    after_b[d] = boundary
            P.phase = f'gh_out{h}'
            pp = 0
            for th in range(2):
                bo = [nb("gCo"), nb("gCo")]
                for tt in range(4):
                    t = th * 4 + tt
                    if tt % 2 == 0:
                        ba_ = nb("gCx")
                        for t2 in (t, t + 1):
                            for d in range(2):
                                c0_ = (t2 - t) * 256 + d * 128
                                MM(ps[ba_][:, c0_:c0_ + 128], ke[:, d * 1024 + t2 * 128:d * 1024 + (t2 + 1) * 128],
                                   qe[:, d * 1024 + t2 * 128:d * 1024 + (t2 + 1) * 128], True, True,
                                   [("ke", hs, d, th), ("qe", hs, d, th)] + hr, [("ps", ba_)])
                    at = AT[pp]
                    TT(at, ps[ba_][:, (tt % 2) * 256:(tt % 2) * 256 + 256], cms(CM_MF, 2), ALU.mult, [("ps", ba_), "cm"], [("AT", pp)])
                    for j in range(2):
                        o = ps[bo[j]][:, tt * 128:(tt + 1) * 128]
                        vs = vtm[:, t * 1024 + h * 256 + j * 128:t * 1024 + h * 256 + (j + 1) * 128]
                        MM(o, vs, at[:, 0:128], True, False, [("vtm", t, h // 2), ("AT", pp)], [("ps", bo[j])])
                        MM(o, vs, at[:, 128:256], False, False, [("vtm", t, h // 2), ("AT", pp)], [("ps", bo[j])])
                        for d in range(2):
                            MM(o, Sall[:, (d * 8 + t) * 256 + j * 128:(d * 8 + t) * 256 + (j + 1) * 128],
                               qe[:, d * 1024 + t * 128:d * 1024 + (t + 1) * 128], False, d == 1,
                               [("Sall", hs, d, t), ("qe", hs, d, th)] + hr, [("ps", bo[j])])
                    pp ^= 1
                sqs = [nsq(), nsq()]
                for j in range(2):
                    ACT(SQ[sqs[j]][:], ps[bo[j]][:, :], AF.Square, [("ps", bo[j])], [("sq", sqs[j])])
                bs = nb("gCx")
                for j in range(2):
                    MM(ps[bs][:, :], ones, SQ[sqs[j]][:], j == 0, j == 1, [("sq", sqs[j]), "cm"], [("ps", bs)])
                ta, tc = nt("gC"), nt("gC")
                ACT(TMP[ta][:], ps[bs][:, :], AF.Ln, [("ps", bs)], [("tmp", ta)], bias=EPS, scale=1.0 / 256)
                ACT(TMP[tc][:], TMP[ta][:], AF.Exp, [("tmp", ta)], [("tmp", tc)], scale=-0.5)
                for j in range(2):
                    fc = h * 2 + j
                    tb = nt("gC")
                    STT(TMP[tb][:], ps[bo[j]][:, :], pv[:, PV_ON + i * 2 + j:PV_ON + i * 2 + j + 1], TMP[tc][:], ALU.mult, ALU.mult,
                        [("ps", bo[j]), "pv", ("tmp", tc)], [("tmp", tb)])
                    sgs = sg[:, fc * 1024 + th * 512:fc * 1024 + (th + 1) * 512]
                    TT(sgs, TMP[tb][:], sgs, ALU.mult, [("tmp", tb), ("sg", fc, th)], [("sg", fc, th)])

    def att_layer(l, mid):
        i = l // 2
        SC = 128 ** -0.5
        cv = Carver(arena, ARENA_BYTES)
        qT = cv.get(8 * 1024, BF16)
        kTf = cv.get(2 * 1280, BF16)
        vst = cv.get(8 * 256, F32)
        VA = cv.get(20 * 130, BF16)[:, 0:20 * 130]
        VA3 = VA.rearrange("p (n f) -> p n f", f=130)
        PT = [cv.get(5 * 512, BF16), cv.get(5 * 512, BF16)]
        rC = cv.get(1024, F32)
        rS = cv.get(1024, F32)
        kst = [cv.get(1024, F32), cv.get(1024, F32)]

        DMA("sp", rC, ropeC[:, :], [], ["rC"], "rC")
        DMA("sp", rS, ropeS[:, :], [], ["rS"], "rS")
        for g in range(2):
            DMA("pool", kTf[:, g * 1280 + 1024:(g + 1) * 1280], ckT_in[i, g, :, :], [], [("kTf", g, 2)], f"ck{g}")
        for u in range(2):
            DMA("pool", VA3[:, 16 + 2 * u:18 + 2 * u, 0:128], cv_in[i, u * 128:(u + 1) * 128, :, :], [], [("VA", 8 + u)], f"cvin{u}")
        CP("dve", VA3[:, :, 128], cms(CM_ONES)[:, 0:20], ["cm"], [("VAone",)])

        TMPX = [cv.get(512, F32) for _ in range(8)]
        SQX = [cv.get(512, BF16) for _ in range(8)]
        nt_all = [TMP[i_][:] for i_ in range(NTMP)] + TMPX
        ns_all = [SQ[i_][:] for i_ in range(NSQ)] + SQX
        xc = {"t": 0, "s": 0}

        def ntx():
            k_ = xc["t"] % len(nt_all)
            xc["t"] += 1
            return nt_all[k_], ("tmp", k_)

        def nsx():
            k_ = xc["s"] % len(ns_all)
            xc["s"] += 1
            return ns_all[k_], ("sq", k_)

        def normrope(b, wcol, th, out_bf, out_f32, kst_res, out_res):
            s0, r0 = nsx()
            ACT(s0, ps[b][:, :], AF.Square, [("ps", b)], [r0])
            b2 = nb("nS")
            MM(ps[b2][:, :], ones, s0, True, True, [r0, "cm"], [("ps", b2)])
            ta, ra = ntx()
            ACT(ta, ps[b2][:, :], AF.Ln, [("ps", b2)], [ra], bias=EPS, scale=1.0 / 128)
            ACT(ta, ta, AF.Exp, [ra], [ra], scale=-0.5)
            td, rd = ntx()
            STT(td, ps[b][:, :], wcol, ta, ALU.mult, ALU.mult, [("ps", b), "pv", ra], [rd])
            s1, r1 = nsx()
            CP("act", s1, td, [rd], [r1])
            b3 = nb("nS")
            MM(ps[b3][:, :], cms(CM_PROT), s1, True, True, [r1, "cm"], [("ps", b3)])
            TT(td, td, rC[:, th * 512:(th + 1) * 512], ALU.mult, [rd, "rC"], [rd])
            t1, rt1 = ntx()
            TT(t1, ps[b3][:, :], rS[:, th * 512:(th + 1) * 512], ALU.mult, [("ps", b3), "rS"], [rt1])
            if out_f32 is None:
                TT(out_bf, td, t1, ALU.add, [rd, rt1], [out_res])
            else:
                TT(out_f32, td, t1, ALU.add, [rd, rt1], [kst_res])
                CP("act", out_bf, out_f32, [kst_res], [out_res])

        P.phase = 'a_kv'
        s = W.acquire()

        def evk(m, th, b):
            normrope(b, pv[:, PV_KN + i:PV_KN + i + 1], th, kTf[:, m * 1280 + th * 512:m * 1280 + (th + 1) * 512],
                     kst[m][:, th * 512:(th + 1) * 512], ("kst", m, th), ("kTf", m, th))
        proj_fm(s, range(2), evk, pool="nP")
        for g in range(2):
            DMA("sp", ckT_out[i, g, :, :], kst[g], [("kst", g, 0), ("kst", g, 1)], [], f"kst{g}")

        def evv(t, b):
            ACT(vst[:, t * 256:(t + 1) * 256], ps[b][:, 0:256], AF.Copy, [("ps", b)], [("vst", t)])
            CP("dve", VA3[:, t * 2:(t + 1) * 2, 0:128], vst[:, t * 256:(t + 1) * 256].rearrange("p (g f) -> p g f", g=2),
               [("vst", t)], [("VA", t)])
        proj_tm(s, 256, 256, evv)
        W.release(s)
        for s_ in range(4):
            DMA("sp", cv_out[s_, i, :, :, :].rearrange("(u p) g d -> p u (g d)", p=128),
                vst[:, s_ * 512:(s_ + 1) * 512].rearrange("p (u f) -> p u f", u=2), [("vst", 2 * s_), ("vst", 2 * s_ + 1)], [], "vst")
        P.phase = 'a_q'
        for qb in range(2):
            s = W.acquire()

            def evq(m, th, b, qb=qb):
                hq = qb * 4 + m
                normrope(b, pv[:, PV_QN + i:PV_QN + i + 1], th, qT[:, hq * 1024 + th * 512:hq * 1024 + (th + 1) * 512],
                         None, None, ("qT", hq, th))
            proj_fm(s, range(4), evq, pool="nP")
            W.release(s)
        mid()

        P.phase = 'a_core'
        pp = 0
        for g in range(2):
            for c in range(4):
                for j in range(4):
                    hq = 4 * g + j
                    pt = PT[pp]
                    qs = qT[:, hq * 1024 + c * 256:hq * 1024 + (c + 1) * 256]
                    order_k = [k_ for k_ in range(5) if k_ != c] + [c]
                    slot_of = {k_: s_ for s_, k_ in enumerate(order_k)}
                    for s_ in range(5):
                        kpr = order_k[s_]
                        for e_ in range(2):
                            kt = 2 * kpr + e_
                            MM(ps[s_][:, e_ * 256:(e_ + 1) * 256], kTf[:, g * 1280 + kt * 128:g * 1280 + (kt + 1) * 128], qs,
                               True, True, [("kTf", g, min(kt // 4, 2)), ("qT", hq, c // 2)], [("ps", s_)])
                    bias_o = pv[:, PV_AM + c * 10 + 2 * order_k[0]:PV_AM + c * 10 + 2 * order_k[0] + 1]
                    for s_ in (0, 2):
                        ACT(pt[:, s_ * 512:(s_ + 2) * 512], psall[:, s_ * 512:(s_ + 2) * 512], AF.Exp,
                            [("ps", s_), ("ps", s_ + 1), "pv"], [("PT", pp, s_), ("PT", pp, s_ + 1)], scale=SC, bias=bias_o)
                    ACT(pt[:, 4 * 512:5 * 512], ps[4][:, :], AF.Exp, [("ps", 4), "pv"], [("PT", pp, 4)], scale=SC,
                        bias=pv[:, PV_AM + c * 10 + 2 * c:PV_AM + c * 10 + 2 * c + 1])
                    bo = nb("aO")
                    for hf in range(2):
                        for kt in range(10):
                            n_ = kt * 2 + g
                            po_ = slot_of[kt // 2] * 512 + (kt % 2) * 256
                            MM(ps[bo][:, hf * 256:hf * 256 + 129], pt[:, po_ + hf * 128:po_ + (hf + 1) * 128],
                               VA[:, n_ * 130:n_ * 130 + 129], kt == 0, kt == 9,
                               [("VA", kt), ("VAone",), ("PT", pp, slot_of[kt // 2])], [("ps", bo)])
                    bt = nb("aO")
                    for hf in range(2):
                        tr_, so_ = nt(), nsq()
                        rc = TMP[tr_][:, 0:1]
                        P.add("dve", lambda e, rc=rc, src=ps[bo][:, hf * 256 + 128:hf * 256 + 129]: e.reciprocal(out=rc, in_=src),
                              [("ps", bo)], [("tmp", tr_)], cost=0.1)
                        on = SQ[so_][:, 0:128]
                        TS(on, ps[bo][:, hf * 256:hf * 256 + 128], rc, ALU.mult, [("ps", bo), ("tmp", tr_)], [("sq", so_)])
                        ot = ps[bt][:, hf * 64:(hf + 1) * 64].bitcast(BF16)
                        P.add("pe", lambda e, ot=ot, on=on: e.transpose(ot, on, cms(CM_ID)), [("sq", so_), "cm"], [("ps", bt)], cost=0.08)
                    sgs = sg[:, hq * 1024 + c * 256:hq * 1024 + (c + 1) * 256]
                    TT(sgs, ps[bt][:, 0:128].bitcast(BF16), sgs, ALU.mult, [("ps", bt), ("sg", hq, c // 2)], [("sg", hq, c // 2)])
                    pp ^= 1

    PRE_OLD = int(_os.environ.get("PRE_OLD", "1"))
    ada(0, "ada", range(4))
    if not PRE_OLD:
        normmod(0)
        gate_phase()
    for l in range(L):
        def mid(l=l):
            if l == 0:
                ada(0, "ada", range(4, 6))
            if l + 1 < L:
                ada(l + 1, "ada")
        if PRE_OLD:
            normmod(l)
            gate_phase()
        if l % 2 == 0:
            gla_layer(l, mid)
        else:
            att_layer(l, mid)
        out_proj(l)
        if l + 1 < L and not PRE_OLD:
            normmod(l + 1)
            gate_phase()
        P.barrier()
    P.phase = 'final'
    for ob in range(2):
        for th in range(2):
            for m in range(4):
                k = ob * 4 + m
                DMA("sp", yT[k * 128:(k + 1) * 128, th * 512:(th + 1) * 512], Xs(k, th), [("X", k, th)], [], "y")

    P.schedule(reorder=REORDER)
    P.finalize()
    assert len(P.dma_names) + 5 <= 100, len(P.dma_names)
    sems = {e: es.enter_context(nc.semaphore(f"s_{e}")) for e in Prog.ENGS}
    dsems = {n: es.enter_context(nc.semaphore(f"d_{n}")) for n in P.dma_names}
    with nc.Block() as block:
        @block.tensor
        def _(e):
            P.emit("pe", e, sems, dsems)

        @block.scalar
        def _(e):
            P.emit("act", e, sems, dsems)

        @block.vector
        def _(e):
            P.emit("dve", e, sems, dsems)

        @block.gpsimd
        def _(e):
            P.emit("pool", e, sems, dsems)

        @block.sync
        def _(e):
            P.emit("sp", e, sems, dsems, final_wait=True)
    es.close()
    return nc, P


def _consts():
    s = np.arange(128)[:, None]
    t = np.arange(128)[None, :]
    cmat = np.zeros((128, NCM, 128), np.float32)
    cmat[:, CM_TRIF] = np.where(s <= t, -1.0 / 16, 0.0)
    cmat[:, CM_TRIB] = np.where(s >= t, -1.0 / 16, 0.0)
    cmat[:, CM_TRISF] = np.where(s > t, -1.0 / 16, 0.0)
    cmat[:, CM_TRISB] = np.where(s < t, -1.0 / 16, 0.0)
    cmat[:, CM_MF] = np.where(s <= t, 1.0, 0.0)
    cmat[:, CM_MB] = np.where(s >= t, 1.0, 0.0)
    prot = np.zeros((128, 128), np.float32)
    for i in range(128):
        if (i % 64) < 32:
            prot[i + 32, i] = -1.0
        else:
            prot[i - 32, i] = 1.0
    cmat[:, CM_PROT] = prot
    cmat[:, CM_ONES] = 1.0
    cmat[:, CM_ID] = np.eye(128, dtype=np.float32)
    return cmat.reshape(128, NCM * 128)


def _rope_tables():
    i = np.arange(128)
    tt = np.arange(1024)
    freqs = (np.float32(10000.0) ** (-np.arange(32, dtype=np.float32) / np.float32(32))).astype(np.float32)
    f = freqs[i % 32][:, None]
    pos = np.where((i < 64)[:, None], (tt // 64)[None, :], (tt % 64)[None, :]).astype(np.float32)
    ang = (pos * f).astype(np.float32)
    return np.cos(ang).astype(np.float32), np.sin(ang).astype(np.float32)


def _prep_inputs(inp):
    f = lambda a: np.ascontiguousarray(np.asarray(a, dtype=np.float32))
    x_prompt, x_sample = f(inp["x_prompt"]), f(inp["x_sample"])
    state_gla, cache_k, cache_v = f(inp["state_gla"]), f(inp["cache_k"]), f(inp["cache_v"])
    c, c_ctx = f(inp["c"]), f(inp["c_ctx"])
    shared = {
        "w_ada": f(inp["w_ada"]), "gla_w_in": f(inp["gla_w_in"]), "gla_w_out": f(inp["gla_w_out"]),
        "att_w_in": f(inp["att_w_in"]), "att_w_out": f(inp["att_w_out"]),
        "cmat": _consts(),
        "ones1k": np.ones((1, 1024), np.float32),
        "wa1": f(f(inp["gla_wa1"]).reshape(2, 2, 8, 128, 16).transpose(3, 0, 1, 2, 4).reshape(128, 512)),
        "wa2": f(f(inp["gla_wa2"]).transpose(0, 2, 1, 3).reshape(2, 16, 1024)),
        "ba": f(f(inp["gla_ba"]).reshape(2, 1, 1024)),
    }
    pv_base = np.zeros((128, NPV), np.float32)
    pv_base[:, PV_BADA:PV_BADA + 96] = f(inp["b_ada"]).reshape(4, 24, 128).transpose(2, 0, 1).reshape(128, 96)
    pv_base[:, PV_NG:PV_NG + 32] = f(inp["norm_g"]).reshape(4, 8, 128).transpose(2, 0, 1).reshape(128, 32)
    pv_base[:, PV_ON:PV_ON + 4] = f(inp["gla_onorm"]).reshape(2, 2, 128).transpose(2, 0, 1).reshape(128, 4)
    pv_base[:, PV_QN:PV_QN + 2] = f(inp["att_qnorm"]).T
    pv_base[:, PV_KN:PV_KN + 2] = f(inp["att_knorm"]).T
    rc, rs = _rope_tables()
    in_maps = []
    for core in range(8):
        m = dict(shared)
        pvv = pv_base.copy()
        if core < 4:
            b = core
            x = x_sample[b]
            cvec = c[b]
            m["st_in"] = f(state_gla[b])
            m["ckT_in"] = f(cache_k[b].transpose(0, 2, 3, 1))
            m["cv_in"] = f(cache_v[b])
            m["ropeC"], m["ropeS"] = rc, rs
            pvv[:, PV_AM:PV_AM + 40] = 0.0
            pvv[:, PV_RS] = 1.0
        else:
            p = core - 4
            x = x_prompt[4 * p:4 * p + 4].reshape(1024, 1024)
            cvec = c_ctx
            m["st_in"] = np.zeros((2, 2, 4, 128, 256), np.float32)
            m["ckT_in"] = np.zeros((2, 2, 128, 256), np.float32)
            m["cv_in"] = np.zeros((2, 256, 2, 128), np.float32)
            m["ropeC"] = np.ones((128, 1024), np.float32)
            m["ropeS"] = np.zeros((128, 1024), np.float32)
            am = np.zeros((4, 10), np.float32)
            for cc in range(4):
                am[cc, 2 * cc] = 1.0
                am[cc, 2 * cc + 1] = 1.0
            pvv[:, PV_AM:PV_AM + 40] = ((1.0 - am) * -30000.0).reshape(1, 40)
            pvv[:, PV_RS] = 0.0
        m["xT"] = f(x.T)
        m["cv8"] = f(cvec.reshape(8, 128).T)
        m["pvec"] = pvv
        in_maps.append(m)
    return in_maps


_NC_CACHE = {}


def run(inputs, L=4, dbg_names=(), trace=False):
    key = (L, tuple(dbg_names))
    if key not in _NC_CACHE:
        _NC_CACHE[key] = build(L, dbg_names)
    nc, P = _NC_CACHE[key]
    in_maps = _prep_inputs(inputs)
    res = run_bass_kernel_spmd(nc, in_maps, core_ids=list(range(8)), trace=trace)
    return res


def kernel(**inputs):
    res = run(inputs)
    r = res.results
    y_sample = np.stack([np.asarray(r[b]["yT"]).T for b in range(4)], 0)
    y_prompt = np.concatenate([np.asarray(r[4 + p]["yT"]).T.reshape(4, 256, 1024) for p in range(4)], 0)
    st = np.concatenate([np.asarray(r[4 + p]["st_out"]) for p in range(4)], 0)
    ck = np.concatenate([np.asarray(r[4 + p]["ckT_out"]).reshape(2, 2, 128, 4, 256).transpose(3, 0, 4, 1, 2) for p in range(4)], 0)
    cvn = np.concatenate([np.asarray(r[4 + p]["cv_out"]) for p in range(4)], 0)
    return (np.ascontiguousarray(y_prompt, dtype=np.float32), np.ascontiguousarray(y_sample, dtype=np.float32),
            np.ascontiguousarray(st, dtype=np.float32), np.ascontiguousarray(ck, dtype=np.float32),
            np.ascontiguousarray(cvn, dtype=np.float32))
```

```python
import numpy as np
from contextlib import ExitStack
import concourse.bass as bass
import concourse.mybir as mybir
from concourse.bass_utils import run_bass_kernel_spmd

F32 = mybir.dt.float32
BF16 = mybir.dt.bfloat16
AF = mybir.ActivationFunctionType
ALU = mybir.AluOpType

D = 1024
NTOK = 1024
EPS = 1e-6
NSLOT = 3
PV_BADA = 0
PV_NG = 96
PV_ON = 128
PV_QN = 132
PV_KN = 134
PV_AM = 136
PV_RS = 176
PV_AM2 = 192
NPV = 272
CM_TRIF, CM_TRIB, CM_TRISF, CM_TRISB, CM_MF, CM_MB, CM_PROT, CM_ONES, CM_ID = range(9)
NCM = 9

SAME_ENGINE_SYNC = True
import os as _os
ATT_STAGE = int(_os.environ.get('ATT_STAGE', '9'))
REORDER = int(_os.environ.get('REORDER', '1')) != 0
PRIO_CP = int(_os.environ.get('PRIO_CP', '1')) != 0
SYNC_SAME_WAR = int(_os.environ.get('SYNC_SAME_WAR', '1')) != 0


class Ins:
    __slots__ = ("eng", "fn", "reads", "writes", "dma", "deps", "alldeps", "needs_inc", "inc_val", "idx", "phase",
                 "epoch", "cost", "nbytes", "rdep", "t0", "t1")

    def __init__(self, eng, fn, reads, writes, dma):
        self.eng = eng
        self.fn = fn
        self.reads = reads
        self.writes = writes
        self.dma = dma
        self.deps = []
        self.alldeps = []
        self.needs_inc = False
        self.inc_val = None
        self.rdep = None


class Prog:
    ENGS = ("pe", "act", "dve", "pool", "sp")

    def __init__(self):
        self.ins = []
        self.last_write = {}
        self.readers = {}
        self.epoch = 0
        self.dma_names = []
        self.phase = ""

    def barrier(self):
        self.epoch += 1

    def add(self, eng, fn, reads=(), writes=(), dma=None, cost=0.1, nbytes=0):
        I = Ins(eng, fn, tuple(reads), tuple(writes), dma)
        I.idx = len(self.ins)
        I.phase = self.phase
        I.epoch = self.epoch
        I.cost = cost
        I.nbytes = nbytes
        deps = {}
        for r in I.reads:
            w = self.last_write.get(r)
            if w is not None:
                deps[w.idx] = (w, "raw")
        for w_ in I.writes:
            lw = self.last_write.get(w_)
            if lw is not None and lw.idx not in deps:
                deps[lw.idx] = (lw, "waw")
            for rd in self.readers.get(w_, ()):
                if rd.idx not in deps:
                    deps[rd.idx] = (rd, "war")
        for r in I.reads:
            self.readers.setdefault(r, []).append(I)
        for w_ in I.writes:
            self.last_write[w_] = I
            self.readers[w_] = []
        I.alldeps = list(deps.values())
        if dma is not None and dma not in self.dma_names:
            self.dma_names.append(dma)
        self.ins.append(I)
        return I

    def schedule(self, reorder=True):
        LAT_X, LAT_S = 0.2, 0.15
        WIN = int(_os.environ.get("WIN", "400"))
        t_eng = {e: 0.0 for e in self.ENGS}
        fin = {}
        dma_free = [0.0]
        new_order = []
        nseg = self.epoch + 1
        segs = [[] for _ in range(nseg)]
        for I in self.ins:
            segs[I.epoch].append(I)
        bar = []

        def lat(J, I, kind):
            if J.dma is not None:
                return 0.0
            if J.eng != I.eng:
                return 0.5 if J.eng == "pe" else LAT_X
            if I.eng == "pe" or (kind != "raw" and not SYNC_SAME_WAR):
                return 0.0
            return LAT_S

        for seg in segs:
            bar_time = {}
            for e in self.ENGS:
                bt = 0.0
                for J in bar:
                    if J.dma is not None or J.eng != e:
                        bt = max(bt, fin[J.idx] + (0.0 if J.dma is not None else 0.5))
                bar_time[e] = bt
            for I in seg:
                I.deps = [J for J in bar if (J.dma is not None or J.eng != I.eng)]
            pend = {e: [I for I in seg if I.eng == e] for e in self.ENGS}
            bl = {}
            if PRIO_CP:
                succ = {}
                inseg = set(I.idx for I in seg)
                for I in seg:
                    for J, kind in I.alldeps:
                        if J.idx in inseg:
                            succ.setdefault(J.idx, []).append((I, kind))
                for I in reversed(seg):
                    c_ = I.cost if I.dma is None else 3.0
                    m_ = 0.0
                    for K_, kind in succ.get(I.idx, ()):
                        m_ = max(m_, bl[K_.idx] + lat(I, K_, kind))
                    bl[I.idx] = c_ + m_
            head = {e: 0 for e in self.ENGS}
            done = set()
            remaining = len(seg)
            last_sched = {}
            while remaining:
                best = None
                for e in self.ENGS:
                    lst = pend[e]
                    h = head[e]
                    while h < len(lst) and lst[h].idx in done:
                        h += 1
                    head[e] = h
                    W = WIN if (reorder and e in ("pe", "act", "dve")) else 1
                    cnt = 0
                    k = h
                    while k < len(lst) and cnt < W:
                        I = lst[k]
                        k += 1
                        if I.idx in done:
                            continue
                        cnt += 1
                        if I.rdep is None:
                            r = 0.0
                            ok = True
                            for J, kind in I.alldeps:
                                f = fin.get(J.idx)
                                if f is None:
                                    ok = False
                                    break
                                r = max(r, f + lat(J, I, kind))
                            if not ok:
                                continue
                            I.rdep = r
                        r = max(I.rdep, bar_time[e], t_eng[e])
                        key = (int(r / 0.13), -bl[I.idx], I.idx) if PRIO_CP else (int(r / 0.3), I.idx)
                        if best is None or key < best[0]:
                            best = (key, r, e, I)
                assert best is not None, "scheduler deadlock"
                _, r, e, I = best
                I.t0 = r
                if I.dma is not None:
                    occ = 1.2 if e == "pool" else 0.15
                    if e == "sp":
                        f = r + 10.0 + I.nbytes / 150e3
                    elif I.nbytes > 200000:
                        st = max(r, dma_free[0])
                        f = st + 2.0 + I.nbytes / 200e3
                        dma_free[0] = f - 2.0
                    else:
                        f = r + 10.0
                    t_eng[e] = r + occ
                else:
                    f = r + I.cost
                    t_eng[e] = f
                    last_sched[e] = I
                fin[I.idx] = f
                I.t1 = f
                done.add(I.idx)
                new_order.append(I)
                remaining -= 1
            bar = list(last_sched.values())
            ld = {}
            for I in seg:
                if I.dma is not None and not I.dma.startswith("w"):
                    ld[I.dma] = I
            bar += list(ld.values())
        self.ins = new_order
        self.sim_time = max(fin.values())
        pos = {}
        for n, I in enumerate(self.ins):
            pos[I.idx] = n
        for I in self.ins:
            cand = list(I.deps)
            for J, kind in I.alldeps:
                if J.dma is None and J.eng == I.eng and (I.eng == "pe" or (kind != "raw" and not SYNC_SAME_WAR)):
                    continue
                cand.append(J)
            keep = {}
            for J in cand:
                key = ("d", J.dma) if J.dma is not None else ("e", J.eng)
                if key not in keep or pos[J.idx] > pos[keep[key].idx]:
                    keep[key] = J
            I.deps = list(keep.values())
            for J in I.deps:
                J.needs_inc = True

    def finalize(self):
        cnt = {e: 0 for e in self.ENGS}
        dcnt = {s: 0 for s in self.dma_names}
        for I in self.ins:
            if I.dma is not None:
                dcnt[I.dma] += 16
                I.inc_val = dcnt[I.dma]
            elif I.needs_inc:
                cnt[I.eng] += 1
                I.inc_val = cnt[I.eng]
        self.cnt = cnt
        self.dcnt = dcnt

    def emit(self, eng_name, eng_obj, sems, dsems, final_wait=False):
        known = {}
        for I in self.ins:
            if I.eng != eng_name:
                continue
            for J in I.deps:
                if J.dma is not None:
                    s = dsems[J.dma]
                    key = ("d", J.dma)
                else:
                    s = sems[J.eng]
                    key = ("e", J.eng)
                if known.get(key, 0) < J.inc_val:
                    eng_obj.wait_ge(s, J.inc_val)
                    known[key] = J.inc_val
            r = I.fn(eng_obj)
            if I.dma is not None:
                r.then_inc(dsems[I.dma], 16)
            elif I.needs_inc:
                r.then_inc(sems[I.eng], 1)
        if final_wait:
            for name, s in dsems.items():
                if self.dcnt[name] > 0:
                    eng_obj.wait_ge(s, self.dcnt[name])


def build(L=4, dbg_names=()):
    nc = bass.Bass("TRN2", target_bir_lowering=False)
    P = Prog()

    def din(name, shape):
        return nc.dram_tensor(name, list(shape), F32, kind="ExternalInput").ap()

    def dout(name, shape, dt=F32):
        return nc.dram_tensor(name, list(shape), dt, kind="ExternalOutput").ap()

    xT = din("xT", [1024, 1024])
    cv8 = din("cv8", [128, 8])
    pvec = din("pvec", [128, NPV])
    wa1 = din("wa1", [128, 512])
    wa2 = din("wa2", [2, 16, 1024])
    bad = din("ba", [2, 1, 1024])
    cmat = din("cmat", [128, NCM * 128])
    ones1k = din("ones1k", [1, 1024])
    ropeC = din("ropeC", [128, 1024])
    ropeS = din("ropeS", [128, 1024])
    st_in = din("st_in", [2, 2, 4, 128, 256])
    ckT_in = din("ckT_in", [2, 2, 128, 256])
    cv_in = din("cv_in", [2, 256, 2, 128])
    w_ada = din("w_ada", [4, 1024, 3072])
    gla_w_in = din("gla_w_in", [2, 1024, 3072])
    gla_w_out = din("gla_w_out", [2, 1024, 1024])
    att_w_in = din("att_w_in", [2, 1024, 2560])
    att_w_out = din("att_w_out", [2, 1024, 1024])
    yT = dout("yT", [1024, 1024])
    st_out = dout("st_out", [4, 2, 2, 4, 128, 256])
    ckT_out = dout("ckT_out", [2, 2, 128, 1024])
    cv_out = dout("cv_out", [4, 2, 256, 2, 128])
    dbg_out = {}

    es = ExitStack()

    def sb(name, shape, dt):
        return es.enter_context(nc.sbuf_tensor(name, list(shape), dt))

    X = sb("X", [128, 8 * 1024], F32)
    hT = sb("hT", [128, 8 * 1024], BF16)
    sg = sb("sg", [128, 8 * 1024], BF16)
    Wt = [sb(f"W{s}", [128, 8, 512], BF16) for s in range(NSLOT)]
    pv = sb("pv", [128, NPV], F32)
    cv8t = sb("cv8t", [128, 8], F32)
    scb = sb("scb", [128, 8], BF16)
    modt = sb("modt", [128, 4 * 24], F32)
    Gp = sb("Gp", [128, 4 * 8], F32)
    cm = sb("cm", [128, NCM * 128], BF16)
    wa1b = sb("wa1b", [128, 512], BF16)
    NTMP = int(_os.environ.get("NTMP", "8"))
    tmpall = sb("tmpall", [128, NTMP * 512], F32)
    TMP = [tmpall[:, i * 512:(i + 1) * 512] for i in range(NTMP)]
    NSQ = int(_os.environ.get("NSQ", "4"))
    SQ = [sb(f"sq{i}", [128, 512], BF16) for i in range(NSQ)]
    ARENA_BYTES = 94 * 1024
    arena = sb("arena", [128, ARENA_BYTES // 4], F32)
    psall = es.enter_context(nc.psum_tensor("psall", [128, 8 * 512], F32))
    ps = [psall[:, i * 512:(i + 1) * 512] for i in range(8)]

    class Carver:
        def __init__(self, base, nbytes_total):
            self.off = 0
            self.base = base
            self.total = nbytes_total

        def get(self, nelem, dt, rows=128):
            nbytes = nelem * (4 if dt == F32 else 2)
            nbytes = (nbytes + 31) // 32 * 32
            assert self.off + nbytes <= self.total, (self.off, nbytes)
            if self.base is arena:
                v = arena[0:rows, self.off // 4:(self.off + nbytes) // 4]
                if dt != F32:
                    v = v.bitcast(dt)
            else:
                assert dt == BF16
                v = self.base[0:rows, self.off // 2:(self.off + nbytes) // 2]
            self.off += nbytes
            return v

    ctr = {"bank": 0, "tmp": 0, "sq": 0, "evac": 0}

    POOLS = {"d": [0, 1, 2, 3, 4, 5, 6], "ada": [7],
             "gA": [0, 1], "gB": [2, 3], "gCo": [4, 5], "gCx": [6],
             "aQK": [0, 1, 2, 3, 4], "aO": [5, 6],
             "nP": [0, 1, 2, 3], "nS": [4, 5, 6]}
    import json as _json
    if _os.environ.get("GPOOLS"):
        POOLS.update(_json.loads(_os.environ["GPOOLS"]))
    pctr = {k: 0 for k in POOLS}

    def nb(pool="d"):
        lst = POOLS[pool]
        b = lst[pctr[pool] % len(lst)]
        pctr[pool] += 1
        return b

    TPOOLS = {"d": list(range(NTMP)), "gA": [0, 1, 2], "gC": [3, 4, 5]}
    tctr = {k: 0 for k in TPOOLS}

    def nt(pool="d"):
        lst = TPOOLS[pool]
        t = lst[tctr[pool] % len(lst)]
        tctr[pool] += 1
        return t

    def nsq():
        t = ctr["sq"] % NSQ
        ctr["sq"] += 1
        return t

    def is_ps(ap):
        return str(ap.space).endswith("PSUM")

    def c_act(out, in_):
        return 0.15 + out.free_size() * 0.0008

    def c_dve(out, k=1.0):
        return 0.07 + out.free_size() * 0.0011 * k

    def ACT(out, in_, func, reads, writes, bias=None, scale=None):
        kw = {}
        if bias is not None:
            kw["bias"] = bias
        if scale is not None:
            kw["scale"] = scale
        P.add("act", lambda e: e.activation(out=out, in_=in_, func=func, **kw), reads, writes, cost=c_act(out, in_))

    def TT(out, in0, in1, op, reads, writes):
        P.add("dve", lambda e: e.tensor_tensor(out=out, in0=in0, in1=in1, op=op), reads, writes, cost=c_dve(out))

    def TS(out, in0, s1, op0, reads, writes, s2=None, op1=None):
        if op1 is None:
            P.add("dve", lambda e: e.tensor_scalar(out=out, in0=in0, scalar1=s1, scalar2=None, op0=op0), reads, writes, cost=c_dve(out))
        else:
            P.add("dve", lambda e: e.tensor_scalar(out=out, in0=in0, scalar1=s1, scalar2=s2, op0=op0, op1=op1), reads, writes, cost=c_dve(out))

    def STT(out, in0, scalar, in1, op0, op1, reads, writes):
        P.add("dve", lambda e: e.scalar_tensor_tensor(out=out, in0=in0, scalar=scalar, in1=in1, op0=op0, op1=op1), reads, writes,
              cost=c_dve(out, 1.15))

    def CP(eng, out, in_, reads, writes):
        if eng == "act":
            P.add("act", lambda e: e.activation(out=out, in_=in_, func=AF.Copy), reads, writes, cost=c_act(out, in_))
        else:
            P.add("dve", lambda e: e.tensor_copy(out=out, in_=in_), reads, writes, cost=c_dve(out))

    def RECIP(out, in_, reads, writes):
        P.add("dve", lambda e: e.reciprocal(out=out, in_=in_), reads, writes, cost=0.1 + out.free_size() * 0.0065)

    def MM(out, lhsT, rhs, start, stop, reads, writes):
        P.add("pe", lambda e: e.matmul(out, lhsT, rhs, start=start, stop=stop), reads, writes,
              cost=0.03 + max(out.free_size(), 100) / 2700.0)

    def DMA(queue, out, in_, reads, writes, sem):
        P.add(queue, lambda e: e.dma_start(out=out, in_=in_), reads, writes, dma=sem, nbytes=in_.nbytes())

    def EVAC(out, in_, reads, writes, scale=None):
        ctr["evac"] += 1
        if ctr["evac"] % 2 == 0:
            ACT(out, in_, AF.Copy, reads, writes, scale=scale)
        else:
            if scale is None:
                CP("dve", out, in_, reads, writes)
            else:
                TS(out, in_, scale, ALU.mult, reads, writes)

    def DBG(name, ap, shape, reads, dt=F32):
        if name not in dbg_names:
            return
        o = dout("dbg_" + name, shape, dt)
        dbg_out[name] = o
        DMA("sp", o, ap, reads, [], "dbg_" + name)

    def Xs(k, th):
        return X[:, k * 1024 + th * 512:k * 1024 + (th + 1) * 512]

    def hTs(k, th):
        return hT[:, k * 1024 + th * 512:k * 1024 + (th + 1) * 512]

    def cms(slot, n=1):
        return cm[:, slot * 128:(slot + n) * 128]

    ones = cms(CM_ONES)

    def blk(ap, l, c0):
        return ap[l, :, c0:c0 + 512].rearrange("(k p) n -> p k n", p=128)

    def ada_blocks(l):
        return [blk(w_ada, l, j * 512) for j in range(6)]

    srcs = []
    for l in range(L):
        i = l // 2
        if l == 0:
            srcs += ada_blocks(0)[0:4]
        if l % 2 == 0:
            srcs += [blk(gla_w_in, i, c) for c in (2048, 2560, 0, 512, 1024, 1536)]
        else:
            srcs += [blk(att_w_in, i, c) for c in (1536, 2048, 1024, 0, 512)]
        if l == 0:
            srcs += ada_blocks(0)[4:6]
        if l + 1 < L:
            srcs += ada_blocks(l + 1)
        wo = gla_w_out if l % 2 == 0 else att_w_out
        srcs += [blk(wo, i, 0), blk(wo, i, 512)]

    class WStream:
        def __init__(self):
            self.next_issue = 0
            self.next_use = 0
            self.slot_of = {}

        def issue(self, slot):
            if self.next_issue >= len(srcs):
                return
            src = srcs[self.next_issue]
            DMA("pool", Wt[slot][:], src, [], [("W", slot)], f"w{slot}")
            self.slot_of[self.next_issue] = slot
            self.next_issue += 1

        def acquire(self):
            s = self.slot_of[self.next_use]
            self.next_use += 1
            return s

        def release(self, slot):
            self.issue(slot)

    W = WStream()

    DMA("sp", pv[:], pvec[:, :], [], ["pv"], "pv")
    DMA("sp", cv8t[:], cv8[:, :], [], ["cv8t"], "cv8")
    DMA("pool", cm[:], cmat[:, :], [], ["cm"], "cm")
    DMA("pool", wa1b[:], wa1[:, :], [], ["wa1b"], "wa1")
    for s in range(NSLOT):
        W.issue(s)
    for k in range(8):
        DMA("sp", X[:, k * 1024:(k + 1) * 1024], xT[k * 128:(k + 1) * 128, :], [], [("X", k, 0), ("X", k, 1)], f"x{k}")
    ACT(scb[:], cv8t[:], AF.Silu, ["cv8t"], ["scb"])

    def ada(l, pool="ada", blocks=range(6)):
        P.phase = 'ada'
        for b6 in blocks:
            s = W.acquire()
            mb = nb(pool)
            for m in range(4):
                for k in range(8):
                    MM(ps[mb][:, m:m + 1], Wt[s][:, k, m * 128:(m + 1) * 128], scb[:, k:k + 1], k == 0, k == 7,
                       [("W", s), "scb"], [("ps", mb)])
            W.release(s)
            TT(modt[:, l * 24 + b6 * 4:l * 24 + b6 * 4 + 4], ps[mb][:, 0:4],
               pv[:, PV_BADA + l * 24 + b6 * 4:PV_BADA + l * 24 + b6 * 4 + 4], ALU.add, [("ps", mb), "pv"], [("mod", l, b6)])
        if 3 in blocks:
            STT(Gp[:, l * 8:(l + 1) * 8], modt[:, l * 24 + 8:l * 24 + 16], 1.0, pv[:, PV_NG + l * 8:PV_NG + (l + 1) * 8],
                ALU.add, ALU.mult, [("mod", l, 2), ("mod", l, 3), "pv"], [("Gp", l)])

    def normmod(l):
        P.phase = 'normmod'
        for th in range(2):
            b = nb()
            for k in range(8):
                s_ = nsq()
                if k % 2 == 0:
                    ACT(SQ[s_][:], Xs(k, th), AF.Square, [("X", k, th)], [("sq", s_)])
                else:
                    TT(SQ[s_][:], Xs(k, th), Xs(k, th), ALU.mult, [("X", k, th)], [("sq", s_)])
                MM(ps[b][:, :], ones, SQ[s_][:], k == 0, k == 7, [("sq", s_), "cm"], [("ps", b)])
            ta, tc = nt(), nt()
            ACT(TMP[ta][:], ps[b][:, :], AF.Ln, [("ps", b)], [("tmp", ta)], bias=EPS, scale=1.0 / D)
            ACT(TMP[tc][:], TMP[ta][:], AF.Exp, [("tmp", ta)], [("tmp", tc)], scale=-0.5)
            for k in range(8):
                tb = nt()
                STT(TMP[tb][:], Xs(k, th), Gp[:, l * 8 + k:l * 8 + k + 1], TMP[tc][:], ALU.mult, ALU.mult,
                    [("X", k, th), ("Gp", l), ("tmp", tc)], [("tmp", tb)])
                ACT(hTs(k, th), TMP[tb][:], AF.Identity, [("tmp", tb), ("mod", l, 0), ("mod", l, 1)], [("hT", k, th)],
                    bias=modt[:, l * 24 + k:l * 24 + k + 1], scale=1.0)

    def proj_fm(slot, ms, evac, pool="d", kouter=False):
        if kouter:
            for th in range(2):
                bs_ = {m: nb(pool) for m in ms}
                for k in range(8):
                    for m in ms:
                        MM(ps[bs_[m]][:, :], Wt[slot][:, k, m * 128:(m + 1) * 128], hTs(k, th), k == 0, k == 7,
                           [("W", slot), ("hT", k, th)], [("ps", bs_[m])])
                for m in ms:
                    evac(m, th, bs_[m])
            return
        for th in range(2):
            for m in ms:
                b = nb(pool)
                for k in range(8):
                    MM(ps[b][:, :], Wt[slot][:, k, m * 128:(m + 1) * 128], hTs(k, th), k == 0, k == 7,
                       [("W", slot), ("hT", k, th)], [("ps", b)])
                evac(m, th, b)

    def proj_tm(slot, c0, ncols, evac):
        for t in range(8):
            b = nb()
            for k in range(8):
                MM(ps[b][:, 0:ncols], hT[:, k * 1024 + t * 128:k * 1024 + (t + 1) * 128], Wt[slot][:, k, c0:c0 + ncols],
                   k == 0, k == 7, [("W", slot), ("hT", k, t // 4)], [("ps", b)])
            evac(t, b)

    def out_proj(l):
        P.phase = 'outproj'
        for ob in range(2):
            s = W.acquire()
            for th in range(2):
                for m in range(4):
                    fc = ob * 4 + m
                    b = nb()
                    for k in range(8):
                        MM(ps[b][:, :], Wt[s][:, k, m * 128:(m + 1) * 128],
                           sg[:, k * 1024 + th * 512:k * 1024 + (th + 1) * 512], k == 0, k == 7,
                           [("W", s), ("sg", k, th)], [("ps", b)])
                    STT(Xs(fc, th), ps[b][:, :], modt[:, l * 24 + 16 + fc:l * 24 + 17 + fc], Xs(fc, th), ALU.mult, ALU.add,
                        [("ps", b), ("mod", l, 4), ("mod", l, 5), ("X", fc, th)], [("X", fc, th)])
            W.release(s)

    def gate_phase():
        P.phase = 'gate'
        for gb in range(2):
            s = W.acquire()

            def ev(m, th, b, gb=gb):
                fc = gb * 4 + m
                ACT(sg[:, fc * 1024 + th * 512:fc * 1024 + (th + 1) * 512], ps[b][:, :], AF.Silu, [("ps", b)], [("sg", fc, th)])
            proj_fm(s, range(4), ev, kouter=(gb == 0))
            W.release(s)

    def gla_layer(l, mid):
        i = l // 2
        cv = Carver(arena, ARENA_BYTES)
        qT = cv.get(4 * 1024, BF16)
        kT = cv.get(4 * 1024, BF16)
        ktm = cv.get(8 * 512, BF16)
        vtm = cv.get(8 * 1024, BF16)
        rT = [cv.get(1024, BF16, rows=17), cv.get(1024, BF16, rows=17)]
        wa2b = cv.get(1024, BF16, rows=17)
        Srun = cv.get(2 * 2 * 256, F32)
        Sfin = cv.get(2 * 4 * 256, F32)
        AT = [cv.get(256, BF16), cv.get(256, BF16)]
        hcv = Carver(hT, 16 * 1024)
        sets = []
        for si in range(2):
            c_ = cv if si == 0 else hcv
            sets.append(dict(g2=c_.get(16 * 128, BF16), qe=c_.get(2 * 1024, BF16), ke=c_.get(2 * 1024, BF16),
                             kp=c_.get(2 * 8 * 128, BF16)))
        Sall2 = [cv.get(2 * 8 * 256, BF16), cv.get(2 * 8 * 256, BF16)]
        dcol2 = [cv.get(16, F32), cv.get(16, F32)]
        dcolr2 = [cv.get(16, F32), cv.get(16, F32)]
        hT_all = [("hT", k, th) for k in range(8) for th in range(2)]

        DMA("pool", wa2b[0:16, :], wa2[i], [], ["wa2b"], "wa2")
        DMA("pool", wa2b[16:17, :], bad[i], [], ["wa2b1"], "bab")
        for d in range(2):
            DMA("pool", rT[d][16:17, :], ones1k[:, :], [], [("rTone", d)], f"rone{d}")
        P.phase = 'g_q'
        s = W.acquire()

        def evq(m, th, b):
            EVAC(qT[:, m * 1024 + th * 512:m * 1024 + (th + 1) * 512], ps[b][:, :], [("ps", b)], [("qT", m, th)], scale=128 ** -0.5)
        proj_fm(s, range(4), evq)
        W.release(s)
        P.phase = 'g_k'
        s = W.acquire()

        def evk(m, th, b):
            EVAC(kT[:, m * 1024 + th * 512:m * 1024 + (th + 1) * 512], ps[b][:, :], [("ps", b)], [("kT", m, th)])
        proj_fm(s, range(4), evk)

        W.release(s)
        for m in range(4):
            for th in range(2):
                b = nb()
                for tt in range(4):
                    t = th * 4 + tt
                    ot = ps[b][:, tt * 64:(tt + 1) * 64].bitcast(BF16)
                    src = kT[:, m * 1024 + t * 128:m * 1024 + (t + 1) * 128]
                    P.add("pe", lambda e, ot=ot, src=src: e.transpose(ot, src, cms(CM_ID)), [("kT", m, th), "cm"], [("ps", b)], cost=0.08)
                EVAC(ktm[:, m * 1024 + th * 512:m * 1024 + (th + 1) * 512], ps[b][:, 0:256].bitcast(BF16), [("ps", b)], [("ktm", th)])
        P.phase = 'g_v'
        for vb in range(2):
            s = W.acquire()

            def evv(t, b, vb=vb):
                EVAC(vtm[:, t * 1024 + vb * 512:t * 1024 + (vb + 1) * 512], ps[b][:, :], [("ps", b)], [("vtm", t, vb)])
            proj_tm(s, 0, 512, evv)
            W.release(s)
        P.phase = 'g_r'
        for d in range(2):
            for th in range(2):
                b = nb()
                for k in range(8):
                    o0 = ((i * 2 + d) * 8 + k) * 16
                    MM(ps[b][0:16, :], wa1b[:, o0:o0 + 16], hTs(k, th), k == 0, k == 7, ["wa1b", ("hT", k, th)], [("ps", b)])
                EVAC(rT[d][0:16, th * 512:(th + 1) * 512], ps[b][0:16, :], [("ps", b)], [("rT", d)])
        mid()

        def load_s0(h_):
            tb_ = 6 + h_ % 2
            for d_ in range(2):
                DMA("sp", TMP[tb_][:, d_ * 256:(d_ + 1) * 256], st_in[i, d_, h_, :, :], [], [("tmp", tb_)], f"sin{h_ % 2}{d_}")
        load_s0(0)
        for h in range(4):
            S = sets[h % 2]
            g2, qe, ke, kp = S["g2"], S["qe"], S["ke"], S["kp"]
            Sall = Sall2[h % 2]
            dcol = dcol2[h % 2]
            dcolr = dcolr2[h % 2]
            hs = h % 2
            hx = hT_all if hs == 1 else []
            hr = hx
            P.phase = f'gh_g{h}'
            for tq in range(2):
                for tp in (2 * tq, 2 * tq + 1):
                    b = tp % 2
                    for t in (2 * tp, 2 * tp + 1):
                        for d in range(2):
                            idx = (t % 2) * 2 + d
                            MM(ps[b][:, idx * 128:(idx + 1) * 128], rT[d][:, t * 128:(t + 1) * 128],
                               wa2b[:, d * 512 + h * 128:d * 512 + (h + 1) * 128], True, True,
                               [("rT", d), ("rTone", d), "wa2b", "wa2b1"], [("ps", b)])
                ACT(tmpall[:, 0:1024], psall[:, 0:1024], AF.Exp, [("ps", 0), ("ps", 1)], [("tmp", 0), ("tmp", 1)], scale=-1.0)
                ACT(g2[:, 2 * tq * 512:(2 * tq + 2) * 512], tmpall[:, 0:1024], AF.Ln, [("tmp", 0), ("tmp", 1)],
                    [("g2", hs, 2 * tq), ("g2", hs, 2 * tq + 1)] + hx, bias=1.0, scale=1.0)
            P.phase = f'gh_decfm{h}'
            for d in range(2):
                for th in range(2):
                    b = th
                    for tt in range(4):
                        t = th * 4 + tt
                        MM(ps[b][:, tt * 128:(tt + 1) * 128], g2[:, (t * 2 + d) * 128:(t * 2 + d + 1) * 128],
                           cms(CM_TRIF + d), True, True, [("g2", hs, t // 2), "cm"] + hr, [("ps", b)])
                te, ti = nt("gA"), nt("gA")
                Eb = TMP[te][:, :].bitcast(BF16)
                Eib = TMP[ti][:, :].bitcast(BF16)
                ACT(Eb, psall[:, 0:1024], AF.Exp, [("ps", 0), ("ps", 1)], [("tmp", te)])
                ACT(Eib, psall[:, 0:1024], AF.Exp, [("ps", 0), ("ps", 1)], [("tmp", ti)], scale=-1.0)
                src = psall[:, 127:1024:128] if d == 0 else psall[:, 0:1024:128]
                ACT(dcol[:, d * 8:d * 8 + 8], src, AF.Exp, [("ps", 0), ("ps", 1)], [("dcol", hs, d, 0), ("dcol", hs, d, 1)])
                TS(dcolr[:, d * 8:d * 8 + 8], dcol[:, d * 8:d * 8 + 8], pv[:, PV_RS:PV_RS + 1], ALU.mult,
                   [("dcol", hs, d, 0), ("dcol", hs, d, 1), "pv"], [("dcolr", hs, d, 0), ("dcolr", hs, d, 1)])
                TT(qe[:, d * 1024:(d + 1) * 1024], qT[:, h * 1024:(h + 1) * 1024], Eb, ALU.mult,
                   [("qT", h, 0), ("qT", h, 1), ("tmp", te)], [("qe", hs, d, 0), ("qe", hs, d, 1)] + hx)
                TT(ke[:, d * 1024:(d + 1) * 1024], kT[:, h * 1024:(h + 1) * 1024], Eib, ALU.mult,
                   [("kT", h, 0), ("kT", h, 1), ("tmp", ti)], [("ke", hs, d, 0), ("ke", hs, d, 1)] + hx)
            P.phase = f'gh_dectm{h}'
            for d in range(2):
                for th in range(2):
                    b = th
                    for tt in range(4):
                        t = th * 4 + tt
                        MM(ps[b][:, tt * 128:(tt + 1) * 128], cms(CM_TRISF + d),
                           g2[:, (t * 2 + d) * 128:(t * 2 + d + 1) * 128], True, True, [("g2", hs, t // 2), "cm"] + hr, [("ps", b)])
                tp_ = nt("gA")
                Epb = TMP[tp_][:, :].bitcast(BF16)
                ACT(Epb, psall[:, 0:1024], AF.Exp, [("ps", 0), ("ps", 1)], [("tmp", tp_)])
                TT(kp[:, d * 1024:(d + 1) * 1024], ktm[:, h * 1024:(h + 1) * 1024], Epb, ALU.mult,
                   [("ktm", 0), ("ktm", 1), ("tmp", tp_)], [("kp", hs, d, 0), ("kp", hs, d, 1)] + hx)
            P.phase = f'gh_chain{h}'
            if h + 1 < 4:
                load_s0(h + 1)
            cur_ap = {d: TMP[6 + h % 2][:, d * 256:(d + 1) * 256] for d in range(2)}
            cur_res = {d: ("tmp", 6 + h % 2) for d in range(2)}
            cur_slot = {0: 0, 1: 0}
            after_b = {0: False, 1: False}
            for n in range(8):
                for d in range(2):
                    t = n if d == 0 else 7 - n
                    Sc, Sres = cur_ap[d], cur_res[d]
                    so = Sall[:, (d * 8 + t) * 256:(d * 8 + t + 1) * 256]
                    if after_b[d]:
                        ACT(so, Sc, AF.Identity, [Sres, "pv"], [("Sall", hs, d, t)], scale=pv[:, PV_RS:PV_RS + 1])
                        col = dcolr[:, d * 8 + t:d * 8 + t + 1]
                        cres = ("dcolr", hs, d, t // 4)
                    else:
                        CP("act", so, Sc, [Sres], [("Sall", hs, d, t)])
                        col = dcol[:, d * 8 + t:d * 8 + t + 1]
                        cres = ("dcol", hs, d, t // 4)
                    b = nb("gB")
                    MM(ps[b][:, 0:256], kp[:, d * 1024 + t * 128:d * 1024 + (t + 1) * 128],
                       vtm[:, t * 1024 + h * 256:t * 1024 + (h + 1) * 256], True, True,
                       [("kp", hs, d, t // 4), ("vtm", t, h // 2)] + hr, [("ps", b)])
                    boundary = (t % 2 == 1) if d == 0 else (t % 2 == 0)
                    if boundary:
                        sq_ = t // 2
                        o_ap = Sfin[:, (d * 4 + sq_) * 256:(d * 4 + sq_ + 1) * 256]
                        o_res = ("Sfin", d, sq_)
                    else:
                        slot = 1 - cur_slot[d]
                        cur_slot[d] = slot
                        o_ap = Srun[:, (d * 2 + slot) * 256:(d * 2 + slot + 1) * 256]
                        o_res = ("Srun", d, slot)
                    STT(o_ap, Sc, col, ps[b][:, 0:256], ALU.mult, ALU.add, [Sres, cres, ("ps", b)], [o_res])
                    if boundary:
                        DMA("sp", st_out[sq_, i, d, h, :, :], o_ap, [o_res], [], f"sf{d}{sq_}")
                    cur_ap[d], cur_res[d] = o_ap, o_res
                    after_b[d] = boundary
            P.phase = f'gh_out{h}'
            pp = 0
            for th in range(2):
                bo = [nb("gCo"), nb("gCo")]
                for tt in range(4):
                    t = th * 4 + tt
                    if tt % 2 == 0:
                        ba_ = nb("gCx")
                        for t2 in (t, t + 1):
                            for d in range(2):
                                c0_ = (t2 - t) * 256 + d * 128
                                MM(ps[ba_][:, c0_:c0_ + 128], ke[:, d * 1024 + t2 * 128:d * 1024 + (t2 + 1) * 128],
                                   qe[:, d * 1024 + t2 * 128:d * 1024 + (t2 + 1) * 128], True, True,
                                   [("ke", hs, d, th), ("qe", hs, d, th)] + hr, [("ps", ba_)])
                    at = AT[pp]
                    TT(at, ps[ba_][:, (tt % 2) * 256:(tt % 2) * 256 + 256], cms(CM_MF, 2), ALU.mult, [("ps", ba_), "cm"], [("AT", pp)])
                    for j in range(2):
                        o = ps[bo[j]][:, tt * 128:(tt + 1) * 128]
                        vs = vtm[:, t * 1024 + h * 256 + j * 128:t * 1024 + h * 256 + (j + 1) * 128]
                        MM(o, vs, at[:, 0:128], True, False, [("vtm", t, h // 2), ("AT", pp)], [("ps", bo[j])])
                        MM(o, vs, at[:, 128:256], False, False, [("vtm", t, h // 2), ("AT", pp)], [("ps", bo[j])])
                        for d in range(2):
                            MM(o, Sall[:, (d * 8 + t) * 256 + j * 128:(d * 8 + t) * 256 + (j + 1) * 128],
                               qe[:, d * 1024 + t * 128:d * 1024 + (t + 1) * 128], False, d == 1,
                               [("Sall", hs, d, t), ("qe", hs, d, th)] + hr, [("ps", bo[j])])
                    pp ^= 1
                sqs = [nsq(), nsq()]
                for j in range(2):
                    ACT(SQ[sqs[j]][:], ps[bo[j]][:, :], AF.Square, [("ps", bo[j])], [("sq", sqs[j])])
                bs = nb("gCx")
                for j in range(2):
                    MM(ps[bs][:, :], ones, SQ[sqs[j]][:], j == 0, j == 1, [("sq", sqs[j]), "cm"], [("ps", bs)])
                ta, tc = nt("gC"), nt("gC")
                ACT(TMP[ta][:], ps[bs][:, :], AF.Ln, [("ps", bs)], [("tmp", ta)], bias=EPS, scale=1.0 / 256)
                ACT(TMP[tc][:], TMP[ta][:], AF.Exp, [("tmp", ta)], [("tmp", tc)], scale=-0.5)
                for j in range(2):
                    fc = h * 2 + j
                    tb = nt("gC")
                    STT(TMP[tb][:], ps[bo[j]][:, :], pv[:, PV_ON + i * 2 + j:PV_ON + i * 2 + j + 1], TMP[tc][:], ALU.mult, ALU.mult,
                        [("ps", bo[j]), "pv", ("tmp", tc)], [("tmp", tb)])
                    sgs = sg[:, fc * 1024 + th * 512:fc * 1024 + (th + 1) * 512]
                    TT(sgs, TMP[tb][:], sgs, ALU.mult, [("tmp", tb), ("sg", fc, th)], [("sg", fc, th)])

    def att_layer(l, mid):
        i = l // 2
        SC = 128 ** -0.5
        cv = Carver(arena, ARENA_BYTES)
        qT = cv.get(8 * 1024, BF16)
        kTf = cv.get(2 * 1280, BF16)
        vst = cv.get(8 * 256, F32)
        VA = cv.get(20 * 130, BF16)[:, 0:20 * 130]
        VA3 = VA.rearrange("p (n f) -> p n f", f=130)
        PT = [cv.get(5 * 512, BF16), cv.get(5 * 512, BF16)]
        rC = cv.get(1024, F32)
        rS = cv.get(1024, F32)
        kst = [cv.get(1024, F32), cv.get(1024, F32)]

        DMA("sp", rC, ropeC[:, :], [], ["rC"], "rC")
        DMA("sp", rS, ropeS[:, :], [], ["rS"], "rS")
        for g in range(2):
            DMA("pool", kTf[:, g * 1280 + 1024:(g + 1) * 1280], ckT_in[i, g, :, :], [], [("kTf", g, 2)], f"ck{g}")
        for u in range(2):
            DMA("pool", VA3[:, 16 + 2 * u:18 + 2 * u, 0:128], cv_in[i, u * 128:(u + 1) * 128, :, :], [], [("VA", 8 + u)], f"cvin{u}")
        CP("dve", VA3[:, :, 128], cms(CM_ONES)[:, 0:20], ["cm"], [("VAone",)])

        TMPX = [cv.get(512, F32) for _ in range(8)]
        SQX = [cv.get(512, BF16) for _ in range(8)]
        nt_all = [TMP[i_][:] for i_ in range(NTMP)] + TMPX
        ns_all = [SQ[i_][:] for i_ in range(NSQ)] + SQX
        xc = {"t": 0, "s": 0}

        def ntx():
            k_ = xc["t"] % len(nt_all)
            xc["t"] += 1
            return nt_all[k_], ("tmp", k_)

        def nsx():
            k_ = xc["s"] % len(ns_all)
            xc["s"] += 1
            return ns_all[k_], ("sq", k_)

        def normrope(b, wcol, th, out_bf, out_f32, kst_res, out_res):
            s0, r0 = nsx()
            ACT(s0, ps[b][:, :], AF.Square, [("ps", b)], [r0])
            b2 = nb("nS")
            MM(ps[b2][:, :], ones, s0, True, True, [r0, "cm"], [("ps", b2)])
            ta, ra = ntx()
            ACT(ta, ps[b2][:, :], AF.Ln, [("ps", b2)], [ra], bias=EPS, scale=1.0 / 128)
            ACT(ta, ta, AF.Exp, [ra], [ra], scale=-0.5)
            td, rd = ntx()
            STT(td, ps[b][:, :], wcol, ta, ALU.mult, ALU.mult, [("ps", b), "pv", ra], [rd])
            s1, r1 = nsx()
            CP("act", s1, td, [rd], [r1])
            b3 = nb("nS")
            MM(ps[b3][:, :], cms(CM_PROT), s1, True, True, [r1, "cm"], [("ps", b3)])
            TT(td, td, rC[:, th * 512:(th + 1) * 512], ALU.mult, [rd, "rC"], [rd])
            t1, rt1 = ntx()
            TT(t1, ps[b3][:, :], rS[:, th * 512:(th + 1) * 512], ALU.mult, [("ps", b3), "rS"], [rt1])
            if out_f32 is None:
                TT(out_bf, td, t1, ALU.add, [rd, rt1], [out_res])
            else:
                TT(out_f32, td, t1, ALU.add, [rd, rt1], [kst_res])
                CP("act", out_bf, out_f32, [kst_res], [out_res])

        P.phase = 'a_kv'
        s = W.acquire()

        def evk(m, th, b):
            normrope(b, pv[:, PV_KN + i:PV_KN + i + 1], th, kTf[:, m * 1280 + th * 512:m * 1280 + (th + 1) * 512],
                     kst[m][:, th * 512:(th + 1) * 512], ("kst", m, th), ("kTf", m, th))
        proj_fm(s, range(2), evk, pool="nP")
        for g in range(2):
            DMA("sp", ckT_out[i, g, :, :], kst[g], [("kst", g, 0), ("kst", g, 1)], [], f"kst{g}")

        def evv(t, b):
            ACT(vst[:, t * 256:(t + 1) * 256], ps[b][:, 0:256], AF.Copy, [("ps", b)], [("vst", t)])
            CP("dve", VA3[:, t * 2:(t + 1) * 2, 0:128], vst[:, t * 256:(t + 1) * 256].rearrange("p (g f) -> p g f", g=2),
               [("vst", t)], [("VA", t)])
        proj_tm(s, 256, 256, evv)
        W.release(s)
        for s_ in range(4):
            DMA("sp", cv_out[s_, i, :, :, :].rearrange("(u p) g d -> p u (g d)", p=128),
                vst[:, s_ * 512:(s_ + 1) * 512].rearrange("p (u f) -> p u f", u=2), [("vst", 2 * s_), ("vst", 2 * s_ + 1)], [], "vst")
        P.phase = 'a_q'
        for qb in range(2):
            s = W.acquire()

            def evq(m, th, b, qb=qb):
                hq = qb * 4 + m
                normrope(b, pv[:, PV_QN + i:PV_QN + i + 1], th, qT[:, hq * 1024 + th * 512:hq * 1024 + (th + 1) * 512],
                         None, None, ("qT", hq, th))
            proj_fm(s, range(4), evq, pool="nP")
            W.release(s)
        mid()

        P.phase = 'a_core'
        pp = 0
        for g in range(2):
            for c in range(4):
                for j in range(4):
                    hq = 4 * g + j
                    pt = PT[pp]
                    qs = qT[:, hq * 1024 + c * 256:hq * 1024 + (c + 1) * 256]
                    order_k = [k_ for k_ in range(5) if k_ != c] + [c]
                    slot_of = {k_: s_ for s_, k_ in enumerate(order_k)}
                    for s_ in range(5):
                        kpr = order_k[s_]
                        for e_ in range(2):
                            kt = 2 * kpr + e_
                            MM(ps[s_][:, e_ * 256:(e_ + 1) * 256], kTf[:, g * 1280 + kt * 128:g * 1280 + (kt + 1) * 128], qs,
                               True, True, [("kTf", g, min(kt // 4, 2)), ("qT", hq, c // 2)], [("ps", s_)])
                    bias_o = pv[:, PV_AM + c * 10 + 2 * order_k[0]:PV_AM + c * 10 + 2 * order_k[0] + 1]
                    for s_ in (0, 2):
                        ACT(pt[:, s_ * 512:(s_ + 2) * 512], psall[:, s_ * 512:(s_ + 2) * 512], AF.Exp,
                            [("ps", s_), ("ps", s_ + 1), "pv"], [("PT", pp, s_), ("PT", pp, s_ + 1)], scale=SC, bias=bias_o)
                    ACT(pt[:, 4 * 512:5 * 512], ps[4][:, :], AF.Exp, [("ps", 4), "pv"], [("PT", pp, 4)], scale=SC,
                        bias=pv[:, PV_AM + c * 10 + 2 * c:PV_AM + c * 10 + 2 * c + 1])
                    bo = nb("aO")
                    for hf in range(2):
                        for kt in range(10):
                            n_ = kt * 2 + g
                            po_ = slot_of[kt // 2] * 512 + (kt % 2) * 256
                            MM(ps[bo][:, hf * 256:hf * 256 + 129], pt[:, po_ + hf * 128:po_ + (hf + 1) * 128],
                               VA[:, n_ * 130:n_ * 130 + 129], kt == 0, kt == 9,
                               [("VA", kt), ("VAone",), ("PT", pp, slot_of[kt // 2])], [("ps", bo)])
                    bt = nb("aO")
                    for hf in range(2):
                        tr_, so_ = nt(), nsq()
                        rc = TMP[tr_][:, 0:1]
                        P.add("dve", lambda e, rc=rc, src=ps[bo][:, hf * 256 + 128:hf * 256 + 129]: e.reciprocal(out=rc, in_=src),
                              [("ps", bo)], [("tmp", tr_)], cost=0.1)
                        on = SQ[so_][:, 0:128]
                        TS(on, ps[bo][:, hf * 256:hf * 256 + 128], rc, ALU.mult, [("ps", bo), ("tmp", tr_)], [("sq", so_)])
                        ot = ps[bt][:, hf * 64:(hf + 1) * 64].bitcast(BF16)
                        P.add("pe", lambda e, ot=ot, on=on: e.transpose(ot, on, cms(CM_ID)), [("sq", so_), "cm"], [("ps", bt)], cost=0.08)
                    sgs = sg[:, hq * 1024 + c * 256:hq * 1024 + (c + 1) * 256]
                    TT(sgs, ps[bt][:, 0:128].bitcast(BF16), sgs, ALU.mult, [("ps", bt), ("sg", hq, c // 2)], [("sg", hq, c // 2)])
                    pp ^= 1

    PRE_OLD = int(_os.environ.get("PRE_OLD", "1"))
    ada(0, "ada", range(4))
    if not PRE_OLD:
        normmod(0)
        gate_phase()
    for l in range(L):
        def mid(l=l):
            if l == 0:
                ada(0, "ada", range(4, 6))
            if l + 1 < L:
                ada(l + 1, "ada")
        if PRE_OLD:
            normmod(l)
            gate_phase()
        if l % 2 == 0:
            gla_layer(l, mid)
        else:
            att_layer(l, mid)
        out_proj(l)
        if l + 1 < L and not PRE_OLD:
            normmod(l + 1)
            gate_phase()
        P.barrier()
    P.phase = 'final'
    for ob in range(2):
        for th in range(2):
            for m in range(4):
                k = ob * 4 + m
                DMA("sp", yT[k * 128:(k + 1) * 128, th * 512:(th + 1) * 512], Xs(k, th), [("X", k, th)], [], "y")

    P.schedule(reorder=REORDER)
    P.finalize()
    assert len(P.dma_names) + 5 <= 100, len(P.dma_names)
    sems = {e: es.enter_context(nc.semaphore(f"s_{e}")) for e in Prog.ENGS}
    dsems = {n: es.enter_context(nc.semaphore(f"d_{n}")) for n in P.dma_names}
    with nc.Block() as block:
        @block.tensor
        def _(e):
            P.emit("pe", e, sems, dsems)

        @block.scalar
        def _(e):
            P.emit("act", e, sems, dsems)

        @block.vector
        def _(e):
            P.emit("dve", e, sems, dsems)

        @block.gpsimd
        def _(e):
            P.emit("pool", e, sems, dsems)

        @block.sync
        def _(e):
            P.emit("sp", e, sems, dsems, final_wait=True)
    es.close()
    return nc, P


def _consts():
    s = np.arange(128)[:, None]
    t = np.arange(128)[None, :]
    cmat = np.zeros((128, NCM, 128), np.float32)
    cmat[:, CM_TRIF] = np.where(s <= t, -1.0 / 16, 0.0)
    cmat[:, CM_TRIB] = np.where(s >= t, -1.0 / 16, 0.0)
    cmat[:, CM_TRISF] = np.where(s > t, -1.0 / 16, 0.0)
    cmat[:, CM_TRISB] = np.where(s < t, -1.0 / 16, 0.0)
    cmat[:, CM_MF] = np.where(s <= t, 1.0, 0.0)
    cmat[:, CM_MB] = np.where(s >= t, 1.0, 0.0)
    prot = np.zeros((128, 128), np.float32)
    for i in range(128):
        if (i % 64) < 32:
            prot[i + 32, i] = -1.0
        else:
            prot[i - 32, i] = 1.0
    cmat[:, CM_PROT] = prot
    cmat[:, CM_ONES] = 1.0
    cmat[:, CM_ID] = np.eye(128, dtype=np.float32)
    return cmat.reshape(128, NCM * 128)


def _rope_tables():
    i = np.arange(128)
    tt = np.arange(1024)
    freqs = (np.float32(10000.0) ** (-np.arange(32, dtype=np.float32) / np.float32(32))).astype(np.float32)
    f = freqs[i % 32][:, None]
    pos = np.where((i < 64)[:, None], (tt // 64)[None, :], (tt % 64)[None, :]).astype(np.float32)
    ang = (pos * f).astype(np.float32)
    return np.cos(ang).astype(np.float32), np.sin(ang).astype(np.float32)


def _prep_inputs(inp):
    f = lambda a: np.ascontiguousarray(np.asarray(a, dtype=np.float32))
    x_prompt, x_sample = f(inp["x_prompt"]), f(inp["x_sample"])
    state_gla, cache_k, cache_v = f(inp["state_gla"]), f(inp["cache_k"]), f(inp["cache_v"])
    c, c_ctx = f(inp["c"]), f(inp["c_ctx"])
    shared = {
        "w_ada": f(inp["w_ada"]), "gla_w_in": f(inp["gla_w_in"]), "gla_w_out": f(inp["gla_w_out"]),
        "att_w_in": f(inp["att_w_in"]), "att_w_out": f(inp["att_w_out"]),
        "cmat": _consts(),
        "ones1k": np.ones((1, 1024), np.float32),
        "wa1": f(f(inp["gla_wa1"]).reshape(2, 2, 8, 128, 16).transpose(3, 0, 1, 2, 4).reshape(128, 512)),
        "wa2": f(f(inp["gla_wa2"]).transpose(0, 2, 1, 3).reshape(2, 16, 1024)),
        "ba": f(f(inp["gla_ba"]).reshape(2, 1, 1024)),
    }
    pv_base = np.zeros((128, NPV), np.float32)
    pv_base[:, PV_BADA:PV_BADA + 96] = f(inp["b_ada"]).reshape(4, 24, 128).transpose(2, 0, 1).reshape(128, 96)
    pv_base[:, PV_NG:PV_NG + 32] = f(inp["norm_g"]).reshape(4, 8, 128).transpose(2, 0, 1).reshape(128, 32)
    pv_base[:, PV_ON:PV_ON + 4] = f(inp["gla_onorm"]).reshape(2, 2, 128).transpose(2, 0, 1).reshape(128, 4)
    pv_base[:, PV_QN:PV_QN + 2] = f(inp["att_qnorm"]).T
    pv_base[:, PV_KN:PV_KN + 2] = f(inp["att_knorm"]).T
    rc, rs = _rope_tables()
    in_maps = []
    for core in range(8):
        m = dict(shared)
        pvv = pv_base.copy()
        if core < 4:
            b = core
            x = x_sample[b]
            cvec = c[b]
            m["st_in"] = f(state_gla[b])
            m["ckT_in"] = f(cache_k[b].transpose(0, 2, 3, 1))
            m["cv_in"] = f(cache_v[b])
            m["ropeC"], m["ropeS"] = rc, rs
            pvv[:, PV_AM:PV_AM + 40] = 0.0
            pvv[:, PV_RS] = 1.0
        else:
            p = core - 4
            x = x_prompt[4 * p:4 * p + 4].reshape(1024, 1024)
            cvec = c_ctx
            m["st_in"] = np.zeros((2, 2, 4, 128, 256), np.float32)
            m["ckT_in"] = np.zeros((2, 2, 128, 256), np.float32)
            m["cv_in"] = np.zeros((2, 256, 2, 128), np.float32)
            m["ropeC"] = np.ones((128, 1024), np.float32)
            m["ropeS"] = np.zeros((128, 1024), np.float32)
            am = np.zeros((4, 10), np.float32)
            for cc in range(4):
                am[cc, 2 * cc] = 1.0
                am[cc, 2 * cc + 1] = 1.0
            pvv[:, PV_AM:PV_AM + 40] = ((1.0 - am) * -30000.0).reshape(1, 40)
            pvv[:, PV_RS] = 0.0
        m["xT"] = f(x.T)
        m["cv8"] = f(cvec.reshape(8, 128).T)
        m["pvec"] = pvv
        in_maps.append(m)
    return in_maps


_NC_CACHE = {}


def run(inputs, L=4, dbg_names=(), trace=False):
    key = (L, tuple(dbg_names))
    if key not in _NC_CACHE:
        _NC_CACHE[key] = build(L, dbg_names)
    nc, P = _NC_CACHE[key]
    in_maps = _prep_inputs(inputs)
    res = run_bass_kernel_spmd(nc, in_maps, core_ids=list(range(8)), trace=trace)
    return res


def kernel(**inputs):
    res = run(inputs)
    r = res.results
    y_sample = np.stack([np.asarray(r[b]["yT"]).T for b in range(4)], 0)
    y_prompt = np.concatenate([np.asarray(r[4 + p]["yT"]).T.reshape(4, 256, 1024) for p in range(4)], 0)
    st = np.concatenate([np.asarray(r[4 + p]["st_out"]) for p in range(4)], 0)
    ck = np.concatenate([np.asarray(r[4 + p]["ckT_out"]).reshape(2, 2, 128, 4, 256).transpose(3, 0, 4, 1, 2) for p in range(4)], 0)
    cvn = np.concatenate([np.asarray(r[4 + p]["cv_out"]) for p in range(4)], 0)
    return (np.ascontiguousarray(y_prompt, dtype=np.float32), np.ascontiguousarray(y_sample, dtype=np.float32),
            np.ascontiguousarray(st, dtype=np.float32), np.ascontiguousarray(ck, dtype=np.float32),
            np.ascontiguousarray(cvn, dtype=np.float32))
```

```python
import numpy as np
from contextlib import ExitStack
import concourse.bass as bass
import concourse.mybir as mybir
from concourse.bass_utils import run_bass_kernel_spmd

F32 = mybir.dt.float32
BF16 = mybir.dt.bfloat16
AF = mybir.ActivationFunctionType
ALU = mybir.AluOpType

D = 1024
NTOK = 1024
EPS = 1e-6
NSLOT = 3
PV_BADA = 0
PV_NG = 96
PV_ON = 128
PV_QN = 132
PV_KN = 134
PV_AM = 136
PV_RS = 176
PV_AM2 = 192
NPV = 272
CM_TRIF, CM_TRIB, CM_TRISF, CM_TRISB, CM_MF, CM_MB, CM_PROT, CM_ONES, CM_ID = range(9)
NCM = 9

SAME_ENGINE_SYNC = True
import os as _os
ATT_STAGE = int(_os.environ.get('ATT_STAGE', '9'))
REORDER = int(_os.environ.get('REORDER', '1')) != 0
PRIO_CP = int(_os.environ.get('PRIO_CP', '1')) != 0
SYNC_SAME_WAR = int(_os.environ.get('SYNC_SAME_WAR', '1')) != 0


class Ins:
    __slots__ = ("eng", "fn", "reads", "writes", "dma", "deps", "alldeps", "needs_inc", "inc_val", "idx", "phase",
                 "epoch", "cost", "nbytes", "rdep", "t0", "t1")

    def __init__(self, eng, fn, reads, writes, dma):
        self.eng = eng
        self.fn = fn
        self.reads = reads
        self.writes = writes
        self.dma = dma
        self.deps = []
        self.alldeps = []
        self.needs_inc = False
        self.inc_val = None
        self.rdep = None


class Prog:
    ENGS = ("pe", "act", "dve", "pool", "sp")

    def __init__(self):
        self.ins = []
        self.last_write = {}
        self.readers = {}
        self.epoch = 0
        self.dma_names = []
        self.phase = ""

    def barrier(self):
        self.epoch += 1

    def add(self, eng, fn, reads=(), writes=(), dma=None, cost=0.1, nbytes=0):
        I = Ins(eng, fn, tuple(reads), tuple(writes), dma)
        I.idx = len(self.ins)
        I.phase = self.phase
        I.epoch = self.epoch
        I.cost = cost
        I.nbytes = nbytes
        deps = {}
        for r in I.reads:
            w = self.last_write.get(r)
            if w is not None:
                deps[w.idx] = (w, "raw")
        for w_ in I.writes:
            lw = self.last_write.get(w_)
            if lw is not None and lw.idx not in deps:
                deps[lw.idx] = (lw, "waw")
            for rd in self.readers.get(w_, ()):
                if rd.idx not in deps:
                    deps[rd.idx] = (rd, "war")
        for r in I.reads:
            self.readers.setdefault(r, []).append(I)
        for w_ in I.writes:
            self.last_write[w_] = I
            self.readers[w_] = []
        I.alldeps = list(deps.values())
        if dma is not None and dma not in self.dma_names:
            self.dma_names.append(dma)
        self.ins.append(I)
        return I

    def schedule(self, reorder=True):
        LAT_X, LAT_S = 0.2, 0.15
        WIN = int(_os.environ.get("WIN", "400"))
        t_eng = {e: 0.0 for e in self.ENGS}
        fin = {}
        dma_free = [0.0]
        new_order = []
        nseg = self.epoch + 1
        segs = [[] for _ in range(nseg)]
        for I in self.ins:
            segs[I.epoch].append(I)
        bar = []

        def lat(J, I, kind):
            if J.dma is not None:
                return 0.0
            if J.eng != I.eng:
                return 0.5 if J.eng == "pe" else LAT_X
            if I.eng == "pe" or (kind != "raw" and not SYNC_SAME_WAR):
                return 0.0
            return LAT_S

        for seg in segs:
            bar_time = {}
            for e in self.ENGS:
                bt = 0.0
                for J in bar:
                    if J.dma is not None or J.eng != e:
                        bt = max(bt, fin[J.idx] + (0.0 if J.dma is not None else 0.5))
                bar_time[e] = bt
            for I in seg:
                I.deps = [J for J in bar if (J.dma is not None or J.eng != I.eng)]
            pend = {e: [I for I in seg if I.eng == e] for e in self.ENGS}
            bl = {}
            if PRIO_CP:
                succ = {}
                inseg = set(I.idx for I in seg)
                for I in seg:
                    for J, kind in I.alldeps:
                        if J.idx in inseg:
                            succ.setdefault(J.idx, []).append((I, kind))
                for I in reversed(seg):
                    c_ = I.cost if I.dma is None else 3.0
                    m_ = 0.0
                    for K_, kind in succ.get(I.idx, ()):
                        m_ = max(m_, bl[K_.idx] + lat(I, K_, kind))
                    bl[I.idx] = c_ + m_
            head = {e: 0 for e in self.ENGS}
            done = set()
            remaining = len(seg)
            last_sched = {}
            while remaining:
                best = None
                for e in self.ENGS:
                    lst = pend[e]
                    h = head[e]
                    while h < len(lst) and lst[h].idx in done:
                        h += 1
                    head[e] = h
                    W = WIN if (reorder and e in ("pe", "act", "dve")) else 1
                    cnt = 0
                    k = h
                    while k < len(lst) and cnt < W:
                        I = lst[k]
                        k += 1
                        if I.idx in done:
                            continue
                        cnt += 1
                        if I.rdep is None:
                            r = 0.0
                            ok = True
                            for J, kind in I.alldeps:
                                f = fin.get(J.idx)
                                if f is None:
                                    ok = False
                                    break
                                r = max(r, f + lat(J, I, kind))
                            if not ok:
                                continue
                            I.rdep = r
                        r = max(I.rdep, bar_time[e], t_eng[e])
                        key = (int(r / 0.15), -bl[I.idx], I.idx) if PRIO_CP else (int(r / 0.3), I.idx)
                        if best is None or key < best[0]:
                            best = (key, r, e, I)
                assert best is not None, "scheduler deadlock"
                _, r, e, I = best
                I.t0 = r
                if I.dma is not None:
                    occ = 1.2 if e == "pool" else 0.15
                    if e == "sp":
                        f = r + 10.0 + I.nbytes / 150e3
                    elif I.nbytes > 200000:
                        st = max(r, dma_free[0])
                        f = st + 2.0 + I.nbytes / 200e3
                        dma_free[0] = f - 2.0
                    else:
                        f = r + 10.0
                    t_eng[e] = r + occ
                else:
                    f = r + I.cost
                    t_eng[e] = f
                    last_sched[e] = I
                fin[I.idx] = f
                I.t1 = f
                done.add(I.idx)
                new_order.append(I)
                remaining -= 1
            bar = list(last_sched.values())
            ld = {}
            for I in seg:
                if I.dma is not None and not I.dma.startswith("w"):
                    ld[I.dma] = I
            bar += list(ld.values())
        self.ins = new_order
        self.sim_time = max(fin.values())
        pos = {}
        for n, I in enumerate(self.ins):
            pos[I.idx] = n
        for I in self.ins:
            cand = list(I.deps)
            for J, kind in I.alldeps:
                if J.dma is None and J.eng == I.eng and (I.eng == "pe" or (kind != "raw" and not SYNC_SAME_WAR)):
                    continue
                cand.append(J)
            keep = {}
            for J in cand:
                key = ("d", J.dma) if J.dma is not None else ("e", J.eng)
                if key not in keep or pos[J.idx] > pos[keep[key].idx]:
                    keep[key] = J
            I.deps = list(keep.values())
            for J in I.deps:
                J.needs_inc = True

    def finalize(self):
        cnt = {e: 0 for e in self.ENGS}
        dcnt = {s: 0 for s in self.dma_names}
        for I in self.ins:
            if I.dma is not None:
                dcnt[I.dma] += 16
                I.inc_val = dcnt[I.dma]
            elif I.needs_inc:
                cnt[I.eng] += 1
                I.inc_val = cnt[I.eng]
        self.cnt = cnt
        self.dcnt = dcnt

    def emit(self, eng_name, eng_obj, sems, dsems, final_wait=False):
        known = {}
        for I in self.ins:
            if I.eng != eng_name:
                continue
            for J in I.deps:
                if J.dma is not None:
                    s = dsems[J.dma]
                    key = ("d", J.dma)
                else:
                    s = sems[J.eng]
                    key = ("e", J.eng)
                if known.get(key, 0) < J.inc_val:
                    eng_obj.wait_ge(s, J.inc_val)
                    known[key] = J.inc_val
            r = I.fn(eng_obj)
            if I.dma is not None:
                r.then_inc(dsems[I.dma], 16)
            elif I.needs_inc:
                r.then_inc(sems[I.eng], 1)
        if final_wait:
            for name, s in dsems.items():
                if self.dcnt[name] > 0:
                    eng_obj.wait_ge(s, self.dcnt[name])


def build(L=4, dbg_names=()):
    nc = bass.Bass("TRN2", target_bir_lowering=False)
    P = Prog()

    def din(name, shape):
        return nc.dram_tensor(name, list(shape), F32, kind="ExternalInput").ap()

    def dout(name, shape, dt=F32):
        return nc.dram_tensor(name, list(shape), dt, kind="ExternalOutput").ap()

    xT = din("xT", [1024, 1024])
    cv8 = din("cv8", [128, 8])
    pvec = din("pvec", [128, NPV])
    wa1 = din("wa1", [128, 512])
    wa2 = din("wa2", [2, 16, 1024])
    bad = din("ba", [2, 1, 1024])
    cmat = din("cmat", [128, NCM * 128])
    ones1k = din("ones1k", [1, 1024])
    ropeC = din("ropeC", [128, 1024])
    ropeS = din("ropeS", [128, 1024])
    st_in = din("st_in", [2, 2, 4, 128, 256])
    ckT_in = din("ckT_in", [2, 2, 128, 256])
    cv_in = din("cv_in", [2, 256, 2, 128])
    w_ada = din("w_ada", [4, 1024, 3072])
    gla_w_in = din("gla_w_in", [2, 1024, 3072])
    gla_w_out = din("gla_w_out", [2, 1024, 1024])
    att_w_in = din("att_w_in", [2, 1024, 2560])
    att_w_out = din("att_w_out", [2, 1024, 1024])
    yT = dout("yT", [1024, 1024])
    st_out = dout("st_out", [4, 2, 2, 4, 128, 256])
    ckT_out = dout("ckT_out", [2, 2, 128, 1024])
    cv_out = dout("cv_out", [4, 2, 256, 2, 128])
    dbg_out = {}

    es = ExitStack()

    def sb(name, shape, dt):
        return es.enter_context(nc.sbuf_tensor(name, list(shape), dt))

    X = sb("X", [128, 8 * 1024], F32)
    hT = sb("hT", [128, 8 * 1024], BF16)
    sg = sb("sg", [128, 8 * 1024], BF16)
    Wt = [sb(f"W{s}", [128, 8, 512], BF16) for s in range(NSLOT)]
    pv = sb("pv", [128, NPV], F32)
    cv8t = sb("cv8t", [128, 8], F32)
    scb = sb("scb", [128, 8], BF16)
    modt = sb("modt", [128, 4 * 24], F32)
    Gp = sb("Gp", [128, 4 * 8], F32)
    cm = sb("cm", [128, NCM * 128], BF16)
    wa1b = sb("wa1b", [128, 512], BF16)
    NTMP = int(_os.environ.get("NTMP", "8"))
    tmpall = sb("tmpall", [128, NTMP * 512], F32)
    TMP = [tmpall[:, i * 512:(i + 1) * 512] for i in range(NTMP)]
    NSQ = int(_os.environ.get("NSQ", "4"))
    SQ = [sb(f"sq{i}", [128, 512], BF16) for i in range(NSQ)]
    ARENA_BYTES = 94 * 1024
    arena = sb("arena", [128, ARENA_BYTES // 4], F32)
    psall = es.enter_context(nc.psum_tensor("psall", [128, 8 * 512], F32))
    ps = [psall[:, i * 512:(i + 1) * 512] for i in range(8)]

    class Carver:
        def __init__(self, base, nbytes_total):
            self.off = 0
            self.base = base
            self.total = nbytes_total

        def get(self, nelem, dt, rows=128):
            nbytes = nelem * (4 if dt == F32 else 2)
            nbytes = (nbytes + 31) // 32 * 32
            assert self.off + nbytes <= self.total, (self.off, nbytes)
            if self.base is arena:
                v = arena[0:rows, self.off // 4:(self.off + nbytes) // 4]
                if dt != F32:
                    v = v.bitcast(dt)
            else:
                assert dt == BF16
                v = self.base[0:rows, self.off // 2:(self.off + nbytes) // 2]
            self.off += nbytes
            return v

    ctr = {"bank": 0, "tmp": 0, "sq": 0, "evac": 0}

    POOLS = {"d": [0, 1, 2, 3, 4, 5, 6], "ada": [7],
             "gA": [0, 1], "gB": [2, 3], "gCo": [4, 5], "gCx": [6],
             "aQK": [0, 1, 2, 3, 4], "aO": [5, 6],
             "nP": [0, 1, 2, 3], "nS": [4, 5, 6]}
    import json as _json
    if _os.environ.get("GPOOLS"):
        POOLS.update(_json.loads(_os.environ["GPOOLS"]))
    pctr = {k: 0 for k in POOLS}

    def nb(pool="d"):
        lst = POOLS[pool]
        b = lst[pctr[pool] % len(lst)]
        pctr[pool] += 1
        return b

    TPOOLS = {"d": list(range(NTMP)), "gA": [0, 1, 2], "gC": [3, 4, 5]}
    tctr = {k: 0 for k in TPOOLS}

    def nt(pool="d"):
        lst = TPOOLS[pool]
        t = lst[tctr[pool] % len(lst)]
        tctr[pool] += 1
        return t

    def nsq():
        t = ctr["sq"] % NSQ
        ctr["sq"] += 1
        return t

    def is_ps(ap):
        return str(ap.space).endswith("PSUM")

    def c_act(out, in_):
        return 0.15 + out.free_size() * 0.0008

    def c_dve(out, k=1.0):
        return 0.07 + out.free_size() * 0.0011 * k

    def ACT(out, in_, func, reads, writes, bias=None, scale=None):
        kw = {}
        if bias is not None:
            kw["bias"] = bias
        if scale is not None:
            kw["scale"] = scale
        P.add("act", lambda e: e.activation(out=out, in_=in_, func=func, **kw), reads, writes, cost=c_act(out, in_))

    def TT(out, in0, in1, op, reads, writes):
        P.add("dve", lambda e: e.tensor_tensor(out=out, in0=in0, in1=in1, op=op), reads, writes, cost=c_dve(out))

    def TS(out, in0, s1, op0, reads, writes, s2=None, op1=None):
        if op1 is None:
            P.add("dve", lambda e: e.tensor_scalar(out=out, in0=in0, scalar1=s1, scalar2=None, op0=op0), reads, writes, cost=c_dve(out))
        else:
            P.add("dve", lambda e: e.tensor_scalar(out=out, in0=in0, scalar1=s1, scalar2=s2, op0=op0, op1=op1), reads, writes, cost=c_dve(out))

    def STT(out, in0, scalar, in1, op0, op1, reads, writes):
        P.add("dve", lambda e: e.scalar_tensor_tensor(out=out, in0=in0, scalar=scalar, in1=in1, op0=op0, op1=op1), reads, writes,
              cost=c_dve(out, 1.15))

    def CP(eng, out, in_, reads, writes):
        if eng == "act":
            P.add("act", lambda e: e.activation(out=out, in_=in_, func=AF.Copy), reads, writes, cost=c_act(out, in_))
        else:
            P.add("dve", lambda e: e.tensor_copy(out=out, in_=in_), reads, writes, cost=c_dve(out))

    def RECIP(out, in_, reads, writes):
        P.add("dve", lambda e: e.reciprocal(out=out, in_=in_), reads, writes, cost=0.1 + out.free_size() * 0.0065)

    def MM(out, lhsT, rhs, start, stop, reads, writes):
        P.add("pe", lambda e: e.matmul(out, lhsT, rhs, start=start, stop=stop), reads, writes,
              cost=0.03 + max(out.free_size(), 100) / 2700.0)

    def DMA(queue, out, in_, reads, writes, sem):
        P.add(queue, lambda e: e.dma_start(out=out, in_=in_), reads, writes, dma=sem, nbytes=in_.nbytes())

    def EVAC(out, in_, reads, writes, scale=None):
        ctr["evac"] += 1
        if ctr["evac"] % 2 == 0:
            ACT(out, in_, AF.Copy, reads, writes, scale=scale)
        else:
            if scale is None:
                CP("dve", out, in_, reads, writes)
            else:
                TS(out, in_, scale, ALU.mult, reads, writes)

    def DBG(name, ap, shape, reads, dt=F32):
        if name not in dbg_names:
            return
        o = dout("dbg_" + name, shape, dt)
        dbg_out[name] = o
        DMA("sp", o, ap, reads, [], "dbg_" + name)

    def Xs(k, th):
        return X[:, k * 1024 + th * 512:k * 1024 + (th + 1) * 512]

    def hTs(k, th):
        return hT[:, k * 1024 + th * 512:k * 1024 + (th + 1) * 512]

    def cms(slot, n=1):
        return cm[:, slot * 128:(slot + n) * 128]

    ones = cms(CM_ONES)

    def blk(ap, l, c0):
        return ap[l, :, c0:c0 + 512].rearrange("(k p) n -> p k n", p=128)

    def ada_blocks(l):
        return [blk(w_ada, l, j * 512) for j in range(6)]

    srcs = []
    for l in range(L):
        i = l // 2
        if l == 0:
            srcs += ada_blocks(0)[0:4]
        if l % 2 == 0:
            srcs += [blk(gla_w_in, i, c) for c in (2048, 2560, 0, 512, 1024, 1536)]
        else:
            srcs += [blk(att_w_in, i, c) for c in (1536, 2048, 1024, 0, 512)]
        if l == 0:
            srcs += ada_blocks(0)[4:6]
        if l + 1 < L:
            srcs += ada_blocks(l + 1)
        wo = gla_w_out if l % 2 == 0 else att_w_out
        srcs += [blk(wo, i, 0), blk(wo, i, 512)]

    class WStream:
        def __init__(self):
            self.next_issue = 0
            self.next_use = 0
            self.slot_of = {}

        def issue(self, slot):
            if self.next_issue >= len(srcs):
                return
            src = srcs[self.next_issue]
            DMA("pool", Wt[slot][:], src, [], [("W", slot)], f"w{slot}")
            self.slot_of[self.next_issue] = slot
            self.next_issue += 1

        def acquire(self):
            s = self.slot_of[self.next_use]
            self.next_use += 1
            return s

        def release(self, slot):
            self.issue(slot)

    W = WStream()

    DMA("sp", pv[:], pvec[:, :], [], ["pv"], "pv")
    DMA("sp", cv8t[:], cv8[:, :], [], ["cv8t"], "cv8")
    DMA("pool", cm[:], cmat[:, :], [], ["cm"], "cm")
    DMA("pool", wa1b[:], wa1[:, :], [], ["wa1b"], "wa1")
    for s in range(NSLOT):
        W.issue(s)
    for k in range(8):
        DMA("sp", X[:, k * 1024:(k + 1) * 1024], xT[k * 128:(k + 1) * 128, :], [], [("X", k, 0), ("X", k, 1)], f"x{k}")
    ACT(scb[:], cv8t[:], AF.Silu, ["cv8t"], ["scb"])

    def ada(l, pool="ada", blocks=range(6)):
        P.phase = 'ada'
        for b6 in blocks:
            s = W.acquire()
            mb = nb(pool)
            for m in range(4):
                for k in range(8):
                    MM(ps[mb][:, m:m + 1], Wt[s][:, k, m * 128:(m + 1) * 128], scb[:, k:k + 1], k == 0, k == 7,
                       [("W", s), "scb"], [("ps", mb)])
            W.release(s)
            TT(modt[:, l * 24 + b6 * 4:l * 24 + b6 * 4 + 4], ps[mb][:, 0:4],
               pv[:, PV_BADA + l * 24 + b6 * 4:PV_BADA + l * 24 + b6 * 4 + 4], ALU.add, [("ps", mb), "pv"], [("mod", l, b6)])
        if 3 in blocks:
            STT(Gp[:, l * 8:(l + 1) * 8], modt[:, l * 24 + 8:l * 24 + 16], 1.0, pv[:, PV_NG + l * 8:PV_NG + (l + 1) * 8],
                ALU.add, ALU.mult, [("mod", l, 2), ("mod", l, 3), "pv"], [("Gp", l)])

    def normmod(l):
        P.phase = 'normmod'
        for th in range(2):
            b = nb()
            for k in range(8):
                s_ = nsq()
                if k % 2 == 0:
                    ACT(SQ[s_][:], Xs(k, th), AF.Square, [("X", k, th)], [("sq", s_)])
                else:
                    TT(SQ[s_][:], Xs(k, th), Xs(k, th), ALU.mult, [("X", k, th)], [("sq", s_)])
                MM(ps[b][:, :], ones, SQ[s_][:], k == 0, k == 7, [("sq", s_), "cm"], [("ps", b)])
            ta, tc = nt(), nt()
            ACT(TMP[ta][:], ps[b][:, :], AF.Ln, [("ps", b)], [("tmp", ta)], bias=EPS, scale=1.0 / D)
            ACT(TMP[tc][:], TMP[ta][:], AF.Exp, [("tmp", ta)], [("tmp", tc)], scale=-0.5)
            for k in range(8):
                tb = nt()
                STT(TMP[tb][:], Xs(k, th), Gp[:, l * 8 + k:l * 8 + k + 1], TMP[tc][:], ALU.mult, ALU.mult,
                    [("X", k, th), ("Gp", l), ("tmp", tc)], [("tmp", tb)])
                ACT(hTs(k, th), TMP[tb][:], AF.Identity, [("tmp", tb), ("mod", l, 0), ("mod", l, 1)], [("hT", k, th)],
                    bias=modt[:, l * 24 + k:l * 24 + k + 1], scale=1.0)

    def proj_fm(slot, ms, evac, pool="d", kouter=False):
        if kouter:
            for th in range(2):
                bs_ = {m: nb(pool) for m in ms}
                for k in range(8):
                    for m in ms:
                        MM(ps[bs_[m]][:, :], Wt[slot][:, k, m * 128:(m + 1) * 128], hTs(k, th), k == 0, k == 7,
                           [("W", slot), ("hT", k, th)], [("ps", bs_[m])])
                for m in ms:
                    evac(m, th, bs_[m])
            return
        for th in range(2):
            for m in ms:
                b = nb(pool)
                for k in range(8):
                    MM(ps[b][:, :], Wt[slot][:, k, m * 128:(m + 1) * 128], hTs(k, th), k == 0, k == 7,
                       [("W", slot), ("hT", k, th)], [("ps", b)])
                evac(m, th, b)

    def proj_tm(slot, c0, ncols, evac):
        for t in range(8):
            b = nb()
            for k in range(8):
                MM(ps[b][:, 0:ncols], hT[:, k * 1024 + t * 128:k * 1024 + (t + 1) * 128], Wt[slot][:, k, c0:c0 + ncols],
                   k == 0, k == 7, [("W", slot), ("hT", k, t // 4)], [("ps", b)])
            evac(t, b)

    def out_proj(l):
        P.phase = 'outproj'
        for ob in range(2):
            s = W.acquire()
            for th in range(2):
                for m in range(4):
                    fc = ob * 4 + m
                    b = nb()
                    for k in range(8):
                        MM(ps[b][:, :], Wt[s][:, k, m * 128:(m + 1) * 128],
                           sg[:, k * 1024 + th * 512:k * 1024 + (th + 1) * 512], k == 0, k == 7,
                           [("W", s), ("sg", k, th)], [("ps", b)])
                    STT(Xs(fc, th), ps[b][:, :], modt[:, l * 24 + 16 + fc:l * 24 + 17 + fc], Xs(fc, th), ALU.mult, ALU.add,
                        [("ps", b), ("mod", l, 4), ("mod", l, 5), ("X", fc, th)], [("X", fc, th)])
            W.release(s)

    def gate_phase():
        P.phase = 'gate'
        for gb in range(2):
            s = W.acquire()

            def ev(m, th, b, gb=gb):
                fc = gb * 4 + m
                ACT(sg[:, fc * 1024 + th * 512:fc * 1024 + (th + 1) * 512], ps[b][:, :], AF.Silu, [("ps", b)], [("sg", fc, th)])
            proj_fm(s, range(4), ev, kouter=(gb == 0))
            W.release(s)

    def gla_layer(l, mid):
        i = l // 2
        cv = Carver(arena, ARENA_BYTES)
        qT = cv.get(4 * 1024, BF16)
        kT = cv.get(4 * 1024, BF16)
        ktm = cv.get(8 * 512, BF16)
        vtm = cv.get(8 * 1024, BF16)
        rT = [cv.get(1024, BF16, rows=17), cv.get(1024, BF16, rows=17)]
        wa2b = cv.get(1024, BF16, rows=17)
        Srun = cv.get(2 * 2 * 256, F32)
        Sfin = cv.get(2 * 4 * 256, F32)
        AT = [cv.get(256, BF16), cv.get(256, BF16)]
        hcv = Carver(hT, 16 * 1024)
        sets = []
        for si in range(2):
            c_ = cv if si == 0 else hcv
            sets.append(dict(g2=c_.get(16 * 128, BF16), qe=c_.get(2 * 1024, BF16), ke=c_.get(2 * 1024, BF16),
                             kp=c_.get(2 * 8 * 128, BF16)))
        Sall2 = [cv.get(2 * 8 * 256, BF16), cv.get(2 * 8 * 256, BF16)]
        dcol2 = [cv.get(16, F32), cv.get(16, F32)]
        dcolr2 = [cv.get(16, F32), cv.get(16, F32)]
        hT_all = [("hT", k, th) for k in range(8) for th in range(2)]

        DMA("pool", wa2b[0:16, :], wa2[i], [], ["wa2b"], "wa2")
        DMA("pool", wa2b[16:17, :], bad[i], [], ["wa2b1"], "bab")
        for d in range(2):
            DMA("pool", rT[d][16:17, :], ones1k[:, :], [], [("rTone", d)], f"rone{d}")
        P.phase = 'g_q'
        s = W.acquire()

        def evq(m, th, b):
            EVAC(qT[:, m * 1024 + th * 512:m * 1024 + (th + 1) * 512], ps[b][:, :], [("ps", b)], [("qT", m, th)], scale=128 ** -0.5)
        proj_fm(s, range(4), evq)
        W.release(s)
        P.phase = 'g_k'
        s = W.acquire()

        def evk(m, th, b):
            EVAC(kT[:, m * 1024 + th * 512:m * 1024 + (th + 1) * 512], ps[b][:, :], [("ps", b)], [("kT", m, th)])
        proj_fm(s, range(4), evk)

        W.release(s)
        for m in range(4):
            for th in range(2):
                b = nb()
                for tt in range(4):
                    t = th * 4 + tt
                    ot = ps[b][:, tt * 64:(tt + 1) * 64].bitcast(BF16)
                    src = kT[:, m * 1024 + t * 128:m * 1024 + (t + 1) * 128]
                    P.add("pe", lambda e, ot=ot, src=src: e.transpose(ot, src, cms(CM_ID)), [("kT", m, th), "cm"], [("ps", b)], cost=0.08)
                EVAC(ktm[:, m * 1024 + th * 512:m * 1024 + (th + 1) * 512], ps[b][:, 0:256].bitcast(BF16), [("ps", b)], [("ktm", th)])
        P.phase = 'g_v'
        for vb in range(2):
            s = W.acquire()

            def evv(t, b, vb=vb):
                EVAC(vtm[:, t * 1024 + vb * 512:t * 1024 + (vb + 1) * 512], ps[b][:, :], [("ps", b)], [("vtm", t, vb)])
            proj_tm(s, 0, 512, evv)
            W.release(s)
        P.phase = 'g_r'
        for d in range(2):
            for th in range(2):
                b = nb()
                for k in range(8):
                    o0 = ((i * 2 + d) * 8 + k) * 16
                    MM(ps[b][0:16, :], wa1b[:, o0:o0 + 16], hTs(k, th), k == 0, k == 7, ["wa1b", ("hT", k, th)], [("ps", b)])
                EVAC(rT[d][0:16, th * 512:(th + 1) * 512], ps[b][0:16, :], [("ps", b)], [("rT", d)])
        mid()

        def load_s0(h_):
            tb_ = 6 + h_ % 2
            for d_ in range(2):
                DMA("sp", TMP[tb_][:, d_ * 256:(d_ + 1) * 256], st_in[i, d_, h_, :, :], [], [("tmp", tb_)], f"sin{h_ % 2}{d_}")
        load_s0(0)
        for h in range(4):
            S = sets[h % 2]
            g2, qe, ke, kp = S["g2"], S["qe"], S["ke"], S["kp"]
            Sall = Sall2[h % 2]
            dcol = dcol2[h % 2]
            dcolr = dcolr2[h % 2]
            hs = h % 2
            hx = hT_all if hs == 1 else []
            hr = hx
            P.phase = f'gh_g{h}'
            for tq in range(2):
                for tp in (2 * tq, 2 * tq + 1):
                    b = tp % 2
                    for t in (2 * tp, 2 * tp + 1):
                        for d in range(2):
                            idx = (t % 2) * 2 + d
                            MM(ps[b][:, idx * 128:(idx + 1) * 128], rT[d][:, t * 128:(t + 1) * 128],
                               wa2b[:, d * 512 + h * 128:d * 512 + (h + 1) * 128], True, True,
                               [("rT", d), ("rTone", d), "wa2b", "wa2b1"], [("ps", b)])
                ACT(tmpall[:, 0:1024], psall[:, 0:1024], AF.Exp, [("ps", 0), ("ps", 1)], [("tmp", 0), ("tmp", 1)], scale=-1.0)
                ACT(g2[:, 2 * tq * 512:(2 * tq + 2) * 512], tmpall[:, 0:1024], AF.Ln, [("tmp", 0), ("tmp", 1)],
                    [("g2", hs, 2 * tq), ("g2", hs, 2 * tq + 1)] + hx, bias=1.0, scale=1.0)
            P.phase = f'gh_decfm{h}'
            for d in range(2):
                for th in range(2):
                    b = th
                    for tt in range(4):
                        t = th * 4 + tt
                        MM(ps[b][:, tt * 128:(tt + 1) * 128], g2[:, (t * 2 + d) * 128:(t * 2 + d + 1) * 128],
                           cms(CM_TRIF + d), True, True, [("g2", hs, t // 2), "cm"] + hr, [("ps", b)])
                te, ti = nt("gA"), nt("gA")
                Eb = TMP[te][:, :].bitcast(BF16)
                Eib = TMP[ti][:, :].bitcast(BF16)
                ACT(Eb, psall[:, 0:1024], AF.Exp, [("ps", 0), ("ps", 1)], [("tmp", te)])
                ACT(Eib, psall[:, 0:1024], AF.Exp, [("ps", 0), ("ps", 1)], [("tmp", ti)], scale=-1.0)
                src = psall[:, 127:1024:128] if d == 0 else psall[:, 0:1024:128]
                ACT(dcol[:, d * 8:d * 8 + 8], src, AF.Exp, [("ps", 0), ("ps", 1)], [("dcol", hs, d, 0), ("dcol", hs, d, 1)])
                TS(dcolr[:, d * 8:d * 8 + 8], dcol[:, d * 8:d * 8 + 8], pv[:, PV_RS:PV_RS + 1], ALU.mult,
                   [("dcol", hs, d, 0), ("dcol", hs, d, 1), "pv"], [("dcolr", hs, d, 0), ("dcolr", hs, d, 1)])
                TT(qe[:, d * 1024:(d + 1) * 1024], qT[:, h * 1024:(h + 1) * 1024], Eb, ALU.mult,
                   [("qT", h, 0), ("qT", h, 1), ("tmp", te)], [("qe", hs, d, 0), ("qe", hs, d, 1)] + hx)
                TT(ke[:, d * 1024:(d + 1) * 1024], kT[:, h * 1024:(h + 1) * 1024], Eib, ALU.mult,
                   [("kT", h, 0), ("kT", h, 1), ("tmp", ti)], [("ke", hs, d, 0), ("ke", hs, d, 1)] + hx)
            P.phase = f'gh_dectm{h}'
            for d in range(2):
                for th in range(2):
                    b = th
                    for tt in range(4):
                        t = th * 4 + tt
                        MM(ps[b][:, tt * 128:(tt + 1) * 128], cms(CM_TRISF + d),
                           g2[:, (t * 2 + d) * 128:(t * 2 + d + 1) * 128], True, True, [("g2", hs, t // 2), "cm"] + hr, [("ps", b)])
                tp_ = nt("gA")
                Epb = TMP[tp_][:, :].bitcast(BF16)
                ACT(Epb, psall[:, 0:1024], AF.Exp, [("ps", 0), ("ps", 1)], [("tmp", tp_)])
                TT(kp[:, d * 1024:(d + 1) * 1024], ktm[:, h * 1024:(h + 1) * 1024], Epb, ALU.mult,
                   [("ktm", 0), ("ktm", 1), ("tmp", tp_)], [("kp", hs, d, 0), ("kp", hs, d, 1)] + hx)
            P.phase = f'gh_chain{h}'
            if h + 1 < 4:
                load_s0(h + 1)
            cur_ap = {d: TMP[6 + h % 2][:, d * 256:(d + 1) * 256] for d in range(2)}
            cur_res = {d: ("tmp", 6 + h % 2) for d in range(2)}
            cur_slot = {0: 0, 1: 0}
            after_b = {0: False, 1: False}
            for n in range(8):
                for d in range(2):
                    t = n if d == 0 else 7 - n
                    Sc, Sres = cur_ap[d], cur_res[d]
                    so = Sall[:, (d * 8 + t) * 256:(d * 8 + t + 1) * 256]
                    if after_b[d]:
                        ACT(so, Sc, AF.Identity, [Sres, "pv"], [("Sall", hs, d, t)], scale=pv[:, PV_RS:PV_RS + 1])
                        col = dcolr[:, d * 8 + t:d * 8 + t + 1]
                        cres = ("dcolr", hs, d, t // 4)
                    else:
                        CP("act", so, Sc, [Sres], [("Sall", hs, d, t)])
                        col = dcol[:, d * 8 + t:d * 8 + t + 1]
                        cres = ("dcol", hs, d, t // 4)
                    b = nb("gB")
                    MM(ps[b][:, 0:256], kp[:, d * 1024 + t * 128:d * 1024 + (t + 1) * 128],
                       vtm[:, t * 1024 + h * 256:t * 1024 + (h + 1) * 256], True, True,
                       [("kp", hs, d, t // 4), ("vtm", t, h // 2)] + hr, [("ps", b)])
                    boundary = (t % 2 == 1) if d == 0 else (t % 2 == 0)
                    if boundary:
                        sq_ = t // 2
                        o_ap = Sfin[:, (d * 4 + sq_) * 256:(d * 4 + sq_ + 1) * 256]
                        o_res = ("Sfin", d, sq_)
                    else:
                        slot = 1 - cur_slot[d]
                        cur_slot[d] = slot
                        o_ap = Srun[:, (d * 2 + slot) * 256:(d * 2 + slot + 1) * 256]
                        o_res = ("Srun", d, slot)
                    STT(o_ap, Sc, col, ps[b][:, 0:256], ALU.mult, ALU.add, [Sres, cres, ("ps", b)], [o_res])
                    if boundary:
                        DMA("sp", st_out[sq_, i, d, h, :, :], o_ap, [o_res], [], f"sf{d}{sq_}")
                    cur_ap[d], cur_res[d] = o_ap, o_res
                    after_b[d] = boundary
            P.phase = f'gh_out{h}'
            pp = 0
            for th in range(2):
                bo = [nb("gCo"), nb("gCo")]
                for tt in range(4):
                    t = th * 4 + tt
                    if tt % 2 == 0:
                        ba_ = nb("gCx")
                        for t2 in (t, t + 1):
                            for d in range(2):
                                c0_ = (t2 - t) * 256 + d * 128
                                MM(ps[ba_][:, c0_:c0_ + 128], ke[:, d * 1024 + t2 * 128:d * 1024 + (t2 + 1) * 128],
                                   qe[:, d * 1024 + t2 * 128:d * 1024 + (t2 + 1) * 128], True, True,
                                   [("ke", hs, d, th), ("qe", hs, d, th)] + hr, [("ps", ba_)])
                    at = AT[pp]
                    TT(at, ps[ba_][:, (tt % 2) * 256:(tt % 2) * 256 + 256], cms(CM_MF, 2), ALU.mult, [("ps", ba_), "cm"], [("AT", pp)])
                    for j in range(2):
                        o = ps[bo[j]][:, tt * 128:(tt + 1) * 128]
                        vs = vtm[:, t * 1024 + h * 256 + j * 128:t * 1024 + h * 256 + (j + 1) * 128]
                        MM(o, vs, at[:, 0:128], True, False, [("vtm", t, h // 2), ("AT", pp)], [("ps", bo[j])])
                        MM(o, vs, at[:, 128:256], False, False, [("vtm", t, h // 2), ("AT", pp)], [("ps", bo[j])])
                        for d in range(2):
                            MM(o, Sall[:, (d * 8 + t) * 256 + j * 128:(d * 8 + t) * 256 + (j + 1) * 128],
                               qe[:, d * 1024 + t * 128:d * 1024 + (t + 1) * 128], False, d == 1,
                               [("Sall", hs, d, t), ("qe", hs, d, th)] + hr, [("ps", bo[j])])
                    pp ^= 1
                sqs = [nsq(), nsq()]
                for j in range(2):
                    ACT(SQ[sqs[j]][:], ps[bo[j]][:, :], AF.Square, [("ps", bo[j])], [("sq", sqs[j])])
                bs = nb("gCx")
                for j in range(2):
                    MM(ps[bs][:, :], ones, SQ[sqs[j]][:], j == 0, j == 1, [("sq", sqs[j]), "cm"], [("ps", bs)])
                ta, tc = nt("gC"), nt("gC")
                ACT(TMP[ta][:], ps[bs][:, :], AF.Ln, [("ps", bs)], [("tmp", ta)], bias=EPS, scale=1.0 / 256)
                ACT(TMP[tc][:], TMP[ta][:], AF.Exp, [("tmp", ta)], [("tmp", tc)], scale=-0.5)
                for j in range(2):
                    fc = h * 2 + j
                    tb = nt("gC")
                    STT(TMP[tb][:], ps[bo[j]][:, :], pv[:, PV_ON + i * 2 + j:PV_ON + i * 2 + j + 1], TMP[tc][:], ALU.mult, ALU.mult,
                        [("ps", bo[j]), "pv", ("tmp", tc)], [("tmp", tb)])
                    sgs = sg[:, fc * 1024 + th * 512:fc * 1024 + (th + 1) * 512]
                    TT(sgs, TMP[tb][:], sgs, ALU.mult, [("tmp", tb), ("sg", fc, th)], [("sg", fc, th)])

    def att_layer(l, mid):
        i = l // 2
        SC = 128 ** -0.5
        cv = Carver(arena, ARENA_BYTES)
        qT = cv.get(8 * 1024, BF16)
        kTf = cv.get(2 * 1280, BF16)
        vst = cv.get(8 * 256, F32)
        VA = cv.get(20 * 130, BF16)[:, 0:20 * 130]
        VA3 = VA.rearrange("p (n f) -> p n f", f=130)
        PT = [cv.get(5 * 512, BF16), cv.get(5 * 512, BF16)]
        rC = cv.get(1024, F32)
        rS = cv.get(1024, F32)
        kst = [cv.get(1024, F32), cv.get(1024, F32)]

        DMA("sp", rC, ropeC[:, :], [], ["rC"], "rC")
        DMA("sp", rS, ropeS[:, :], [], ["rS"], "rS")
        for g in range(2):
            DMA("pool", kTf[:, g * 1280 + 1024:(g + 1) * 1280], ckT_in[i, g, :, :], [], [("kTf", g, 2)], f"ck{g}")
        for u in range(2):
            DMA("pool", VA3[:, 16 + 2 * u:18 + 2 * u, 0:128], cv_in[i, u * 128:(u + 1) * 128, :, :], [], [("VA", 8 + u)], f"cvin{u}")
        CP("dve", VA3[:, :, 128], cms(CM_ONES)[:, 0:20], ["cm"], [("VAone",)])

        TMPX = [cv.get(512, F32) for _ in range(8)]
        SQX = [cv.get(512, BF16) for _ in range(8)]
        nt_all = [TMP[i_][:] for i_ in range(NTMP)] + TMPX
        ns_all = [SQ[i_][:] for i_ in range(NSQ)] + SQX
        xc = {"t": 0, "s": 0}

        def ntx():
            k_ = xc["t"] % len(nt_all)
            xc["t"] += 1
            return nt_all[k_], ("tmp", k_)

        def nsx():
            k_ = xc["s"] % len(ns_all)
            xc["s"] += 1
            return ns_all[k_], ("sq", k_)

        def normrope(b, wcol, th, out_bf, out_f32, kst_res, out_res):
            s0, r0 = nsx()
            ACT(s0, ps[b][:, :], AF.Square, [("ps", b)], [r0])
            b2 = nb("nS")
            MM(ps[b2][:, :], ones, s0, True, True, [r0, "cm"], [("ps", b2)])
            ta, ra = ntx()
            ACT(ta, ps[b2][:, :], AF.Ln, [("ps", b2)], [ra], bias=EPS, scale=1.0 / 128)
            ACT(ta, ta, AF.Exp, [ra], [ra], scale=-0.5)
            td, rd = ntx()
            STT(td, ps[b][:, :], wcol, ta, ALU.mult, ALU.mult, [("ps", b), "pv", ra], [rd])
            s1, r1 = nsx()
            CP("act", s1, td, [rd], [r1])
            b3 = nb("nS")
            MM(ps[b3][:, :], cms(CM_PROT), s1, True, True, [r1, "cm"], [("ps", b3)])
            TT(td, td, rC[:, th * 512:(th + 1) * 512], ALU.mult, [rd, "rC"], [rd])
            t1, rt1 = ntx()
            TT(t1, ps[b3][:, :], rS[:, th * 512:(th + 1) * 512], ALU.mult, [("ps", b3), "rS"], [rt1])
            if out_f32 is None:
                TT(out_bf, td, t1, ALU.add, [rd, rt1], [out_res])
            else:
                TT(out_f32, td, t1, ALU.add, [rd, rt1], [kst_res])
                CP("act", out_bf, out_f32, [kst_res], [out_res])

        P.phase = 'a_kv'
        s = W.acquire()

        def evk(m, th, b):
            normrope(b, pv[:, PV_KN + i:PV_KN + i + 1], th, kTf[:, m * 1280 + th * 512:m * 1280 + (th + 1) * 512],
                     kst[m][:, th * 512:(th + 1) * 512], ("kst", m, th), ("kTf", m, th))
        proj_fm(s, range(2), evk, pool="nP")
        for g in range(2):
            DMA("sp", ckT_out[i, g, :, :], kst[g], [("kst", g, 0), ("kst", g, 1)], [], f"kst{g}")

        def evv(t, b):
            ACT(vst[:, t * 256:(t + 1) * 256], ps[b][:, 0:256], AF.Copy, [("ps", b)], [("vst", t)])
            CP("dve", VA3[:, t * 2:(t + 1) * 2, 0:128], vst[:, t * 256:(t + 1) * 256].rearrange("p (g f) -> p g f", g=2),
               [("vst", t)], [("VA", t)])
        proj_tm(s, 256, 256, evv)
        W.release(s)
        for s_ in range(4):
            DMA("sp", cv_out[s_, i, :, :, :].rearrange("(u p) g d -> p u (g d)", p=128),
                vst[:, s_ * 512:(s_ + 1) * 512].rearrange("p (u f) -> p u f", u=2), [("vst", 2 * s_), ("vst", 2 * s_ + 1)], [], "vst")
        P.phase = 'a_q'
        for qb in range(2):
            s = W.acquire()

            def evq(m, th, b, qb=qb):
                hq = qb * 4 + m
                normrope(b, pv[:, PV_QN + i:PV_QN + i + 1], th, qT[:, hq * 1024 + th * 512:hq * 1024 + (th + 1) * 512],
                         None, None, ("qT", hq, th))
            proj_fm(s, range(4), evq, pool="nP")
            W.release(s)
        mid()

        P.phase = 'a_core'
        pp = 0
        for g in range(2):
            for c in range(4):
                for j in range(4):
                    hq = 4 * g + j
                    pt = PT[pp]
                    qs = qT[:, hq * 1024 + c * 256:hq * 1024 + (c + 1) * 256]
                    order_k = [k_ for k_ in range(5) if k_ != c] + [c]
                    slot_of = {k_: s_ for s_, k_ in enumerate(order_k)}
                    for s_ in range(5):
                        kpr = order_k[s_]
                        for e_ in range(2):
                            kt = 2 * kpr + e_
                            MM(ps[s_][:, e_ * 256:(e_ + 1) * 256], kTf[:, g * 1280 + kt * 128:g * 1280 + (kt + 1) * 128], qs,
                               True, True, [("kTf", g, min(kt // 4, 2)), ("qT", hq, c // 2)], [("ps", s_)])
                    bias_o = pv[:, PV_AM + c * 10 + 2 * order_k[0]:PV_AM + c * 10 + 2 * order_k[0] + 1]
                    for s_ in (0, 2):
                        ACT(pt[:, s_ * 512:(s_ + 2) * 512], psall[:, s_ * 512:(s_ + 2) * 512], AF.Exp,
                            [("ps", s_), ("ps", s_ + 1), "pv"], [("PT", pp, s_), ("PT", pp, s_ + 1)], scale=SC, bias=bias_o)
                    ACT(pt[:, 4 * 512:5 * 512], ps[4][:, :], AF.Exp, [("ps", 4), "pv"], [("PT", pp, 4)], scale=SC,
                        bias=pv[:, PV_AM + c * 10 + 2 * c:PV_AM + c * 10 + 2 * c + 1])
                    bo = nb("aO")
                    for hf in range(2):
                        for kt in range(10):
                            n_ = kt * 2 + g
                            po_ = slot_of[kt // 2] * 512 + (kt % 2) * 256
                            MM(ps[bo][:, hf * 256:hf * 256 + 129], pt[:, po_ + hf * 128:po_ + (hf + 1) * 128],
                               VA[:, n_ * 130:n_ * 130 + 129], kt == 0, kt == 9,
                               [("VA", kt), ("VAone",), ("PT", pp, slot_of[kt // 2])], [("ps", bo)])
                    bt = nb("aO")
                    tr_ = nt()
                    rc2 = TMP[tr_][:, 0:2]
                    P.add("dve", lambda e, rc2=rc2, src=ps[bo][:, 128:512:256]: e.reciprocal(out=rc2, in_=src),
                          [("ps", bo)], [("tmp", tr_)], cost=0.1)
                    for hf in range(2):
                        so_ = nsq()
                        on = SQ[so_][:, 0:128]
                        TS(on, ps[bo][:, hf * 256:hf * 256 + 128], rc2[:, hf:hf + 1], ALU.mult, [("ps", bo), ("tmp", tr_)], [("sq", so_)])
                        ot = ps[bt][:, hf * 64:(hf + 1) * 64].bitcast(BF16)
                        P.add("pe", lambda e, ot=ot, on=on: e.transpose(ot, on, cms(CM_ID)), [("sq", so_), "cm"], [("ps", bt)], cost=0.08)
                    sgs = sg[:, hq * 1024 + c * 256:hq * 1024 + (c + 1) * 256]
                    TT(sgs, ps[bt][:, 0:128].bitcast(BF16), sgs, ALU.mult, [("ps", bt), ("sg", hq, c // 2)], [("sg", hq, c // 2)])
                    pp ^= 1

    PRE_OLD = int(_os.environ.get("PRE_OLD", "1"))
    ada(0, "ada", range(4))
    if not PRE_OLD:
        normmod(0)
        gate_phase()
    for l in range(L):
        def mid(l=l):
            if l == 0:
                ada(0, "ada", range(4, 6))
            if l + 1 < L:
                ada(l + 1, "ada")
        if PRE_OLD:
            normmod(l)
            gate_phase()
        if l % 2 == 0:
            gla_layer(l, mid)
        else:
            att_layer(l, mid)
        out_proj(l)
        if l + 1 < L and not PRE_OLD:
            normmod(l + 1)
            gate_phase()
        P.barrier()
    P.phase = 'final'
    for ob in range(2):
        for th in range(2):
            for m in range(4):
                k = ob * 4 + m
                DMA("sp", yT[k * 128:(k + 1) * 128, th * 512:(th + 1) * 512], Xs(k, th), [("X", k, th)], [], "y")

    P.schedule(reorder=REORDER)
    P.finalize()
    assert len(P.dma_names) + 5 <= 100, len(P.dma_names)
    sems = {e: es.enter_context(nc.semaphore(f"s_{e}")) for e in Prog.ENGS}
    dsems = {n: es.enter_context(nc.semaphore(f"d_{n}")) for n in P.dma_names}
    with nc.Block() as block:
        @block.tensor
        def _(e):
            P.emit("pe", e, sems, dsems)

        @block.scalar
        def _(e):
            P.emit("act", e, sems, dsems)

        @block.vector
        def _(e):
            P.emit("dve", e, sems, dsems)

        @block.gpsimd
        def _(e):
            P.emit("pool", e, sems, dsems)

        @block.sync
        def _(e):
            P.emit("sp", e, sems, dsems, final_wait=True)
    es.close()
    return nc, P


def _consts():
    s = np.arange(128)[:, None]
    t = np.arange(128)[None, :]
    cmat = np.zeros((128, NCM, 128), np.float32)
    cmat[:, CM_TRIF] = np.where(s <= t, -1.0 / 16, 0.0)
    cmat[:, CM_TRIB] = np.where(s >= t, -1.0 / 16, 0.0)
    cmat[:, CM_TRISF] = np.where(s > t, -1.0 / 16, 0.0)
    cmat[:, CM_TRISB] = np.where(s < t, -1.0 / 16, 0.0)
    cmat[:, CM_MF] = np.where(s <= t, 1.0, 0.0)
    cmat[:, CM_MB] = np.where(s >= t, 1.0, 0.0)
    prot = np.zeros((128, 128), np.float32)
    for i in range(128):
        if (i % 64) < 32:
            prot[i + 32, i] = -1.0
        else:
            prot[i - 32, i] = 1.0
    cmat[:, CM_PROT] = prot
    cmat[:, CM_ONES] = 1.0
    cmat[:, CM_ID] = np.eye(128, dtype=np.float32)
    return cmat.reshape(128, NCM * 128)


def _rope_tables():
    i = np.arange(128)
    tt = np.arange(1024)
    freqs = (np.float32(10000.0) ** (-np.arange(32, dtype=np.float32) / np.float32(32))).astype(np.float32)
    f = freqs[i % 32][:, None]
    pos = np.where((i < 64)[:, None], (tt // 64)[None, :], (tt % 64)[None, :]).astype(np.float32)
    ang = (pos * f).astype(np.float32)
    return np.cos(ang).astype(np.float32), np.sin(ang).astype(np.float32)


def _prep_inputs(inp):
    f = lambda a: np.ascontiguousarray(np.asarray(a, dtype=np.float32))
    x_prompt, x_sample = f(inp["x_prompt"]), f(inp["x_sample"])
    state_gla, cache_k, cache_v = f(inp["state_gla"]), f(inp["cache_k"]), f(inp["cache_v"])
    c, c_ctx = f(inp["c"]), f(inp["c_ctx"])
    shared = {
        "w_ada": f(inp["w_ada"]), "gla_w_in": f(inp["gla_w_in"]), "gla_w_out": f(inp["gla_w_out"]),
        "att_w_in": f(inp["att_w_in"]), "att_w_out": f(inp["att_w_out"]),
        "cmat": _consts(),
        "ones1k": np.ones((1, 1024), np.float32),
        "wa1": f(f(inp["gla_wa1"]).reshape(2, 2, 8, 128, 16).transpose(3, 0, 1, 2, 4).reshape(128, 512)),
        "wa2": f(f(inp["gla_wa2"]).transpose(0, 2, 1, 3).reshape(2, 16, 1024)),
        "ba": f(f(inp["gla_ba"]).reshape(2, 1, 1024)),
    }
    pv_base = np.zeros((128, NPV), np.float32)
    pv_base[:, PV_BADA:PV_BADA + 96] = f(inp["b_ada"]).reshape(4, 24, 128).transpose(2, 0, 1).reshape(128, 96)
    pv_base[:, PV_NG:PV_NG + 32] = f(inp["norm_g"]).reshape(4, 8, 128).transpose(2, 0, 1).reshape(128, 32)
    pv_base[:, PV_ON:PV_ON + 4] = f(inp["gla_onorm"]).reshape(2, 2, 128).transpose(2, 0, 1).reshape(128, 4)
    pv_base[:, PV_QN:PV_QN + 2] = f(inp["att_qnorm"]).T
    pv_base[:, PV_KN:PV_KN + 2] = f(inp["att_knorm"]).T
    rc, rs = _rope_tables()
    in_maps = []
    for core in range(8):
        m = dict(shared)
        pvv = pv_base.copy()
        if core < 4:
            b = core
            x = x_sample[b]
            cvec = c[b]
            m["st_in"] = f(state_gla[b])
            m["ckT_in"] = f(cache_k[b].transpose(0, 2, 3, 1))
            m["cv_in"] = f(cache_v[b])
            m["ropeC"], m["ropeS"] = rc, rs
            pvv[:, PV_AM:PV_AM + 40] = 0.0
            pvv[:, PV_RS] = 1.0
        else:
            p = core - 4
            x = x_prompt[4 * p:4 * p + 4].reshape(1024, 1024)
            cvec = c_ctx
            m["st_in"] = np.zeros((2, 2, 4, 128, 256), np.float32)
            m["ckT_in"] = np.zeros((2, 2, 128, 256), np.float32)
            m["cv_in"] = np.zeros((2, 256, 2, 128), np.float32)
            m["ropeC"] = np.ones((128, 1024), np.float32)
            m["ropeS"] = np.zeros((128, 1024), np.float32)
            am = np.zeros((4, 10), np.float32)
            for cc in range(4):
                am[cc, 2 * cc] = 1.0
                am[cc, 2 * cc + 1] = 1.0
            pvv[:, PV_AM:PV_AM + 40] = ((1.0 - am) * -30000.0).reshape(1, 40)
            pvv[:, PV_RS] = 0.0
        m["xT"] = f(x.T)
        m["cv8"] = f(cvec.reshape(8, 128).T)
        m["pvec"] = pvv
        in_maps.append(m)
    return in_maps


_NC_CACHE = {}


def run(inputs, L=4, dbg_names=(), trace=False):
    key = (L, tuple(dbg_names))
    if key not in _NC_CACHE:
        _NC_CACHE[key] = build(L, dbg_names)
    nc, P = _NC_CACHE[key]
    in_maps = _prep_inputs(inputs)
    res = run_bass_kernel_spmd(nc, in_maps, core_ids=list(range(8)), trace=trace)
    return res


def kernel(**inputs):
    res = run(inputs)
    r = res.results
    y_sample = np.stack([np.asarray(r[b]["yT"]).T for b in range(4)], 0)
    y_prompt = np.concatenate([np.asarray(r[4 + p]["yT"]).T.reshape(4, 256, 1024) for p in range(4)], 0)
    st = np.concatenate([np.asarray(r[4 + p]["st_out"]) for p in range(4)], 0)
    ck = np.concatenate([np.asarray(r[4 + p]["ckT_out"]).reshape(2, 2, 128, 4, 256).transpose(3, 0, 4, 1, 2) for p in range(4)], 0)
    cvn = np.concatenate([np.asarray(r[4 + p]["cv_out"]) for p in range(4)], 0)
    return (np.ascontiguousarray(y_prompt, dtype=np.float32), np.ascontiguousarray(y_sample, dtype=np.float32),
            np.ascontiguousarray(st, dtype=np.float32), np.ascontiguousarray(ck, dtype=np.float32),
            np.ascontiguousarray(cvn, dtype=np.float32))
```
